# Optimizing a Trainium2 kernel written in Bass

```python
import math
import jax, jax.numpy as jnp
from jax import lax
import numpy as np

D_MODEL = 1024
BATCH = 4
SEQ = 8192
DEPTH = 2

GRID_W = 64
CTX_LEN = 256
N_EVEN = (DEPTH + 1) // 2
N_ODD = DEPTH // 2
EPS = 1e-6

M_HEADS = 4
M_DK = 128
M_DV = 128
M_CHUNK = 64
B_HEADS = 8
B_Q_RANK = 256
B_KV_RANK = 128
B_NOPE = 64
B_ROPE = 32
B_V = 64
ROPE_BASE = 10000.0
Q_BLOCK = 128
C_HEADS = 16
C_HEAD_DIM = 64
NA_KH_MAX = 8
NA_KW = 16
D_FF = 4 * D_MODEL

M_WIDTH = M_HEADS * M_DV
B_WIDTH = B_HEADS * B_V
AB_WIDTH = M_WIDTH + B_WIDTH
C_WIDTH = C_HEADS * C_HEAD_DIM
AB_SPLITS = (M_HEADS * M_DK, M_HEADS * M_DK, M_HEADS * M_DV, M_HEADS * M_DV, 4 * M_HEADS,
             B_Q_RANK, B_KV_RANK, B_ROPE)
AB_IN = sum(AB_SPLITS)

kernel_name = "hybrid_mlstm_mla_natten_dit_block"


def rmsnorm(x, g):
    xf = x.astype(jnp.float32)
    y = xf * lax.rsqrt(jnp.mean(xf * xf, axis=-1, keepdims=True) + EPS)
    return y.astype(x.dtype) * g


def modulate(x, g, shift, scale):
    return rmsnorm(x, g) * (1 + scale) + shift


def split_cols(a, sizes):
    idx = np.cumsum(np.array(sizes))[:-1].tolist()
    return jnp.split(a, idx, axis=-1)


def sqrelu_mlp(h, w1, w2):
    return jnp.square(jax.nn.relu(h @ w1)) @ w2


def rope_2d(T):
    pos = jnp.arange(T)
    row = (pos // GRID_W).astype(jnp.float32)
    col = (pos % GRID_W).astype(jnp.float32)
    n_f = B_ROPE // 4
    freqs = ROPE_BASE ** (-jnp.arange(n_f, dtype=jnp.float32) / n_f)
    ang = jnp.concatenate([row[:, None] * freqs, col[:, None] * freqs], axis=-1)
    return jnp.cos(ang)[:, None, :], jnp.sin(ang)[:, None, :]


def apply_rope(x, cos, sin):
    half = x.shape[-1] // 2
    xa = x[..., :half].astype(jnp.float32)
    xb = x[..., half:].astype(jnp.float32)
    out = jnp.concatenate([xa * cos - xb * sin, xa * sin + xb * cos], axis=-1)
    return out.astype(x.dtype)


def dense_attention(q, k, v, scale):
    s = jnp.einsum('bqhd,bkhd->bhqk', q, k).astype(jnp.float32) * scale
    p = jax.nn.softmax(s, axis=-1).astype(v.dtype)
    return jnp.einsum('bhqk,bkhd->bqhd', p, v)


def blocked_attention(q, k, v, scale):
    B, T, H, dq = q.shape
    nb = T // Q_BLOCK
    qb = jnp.moveaxis(q.reshape(B, nb, Q_BLOCK, H, dq), 1, 0)
    o = lax.map(lambda qi: dense_attention(qi, k, v, scale), qb)
    return jnp.moveaxis(o, 0, 1).reshape(B, T, H, v.shape[-1])


def mlstm_zero_state(B):
    return (jnp.zeros((B, M_HEADS, M_DK, M_DV), jnp.float32),
            jnp.zeros((B, M_HEADS, M_DK), jnp.float32),
            jnp.zeros((B, M_HEADS), jnp.float32))


def mlstm_chunked(q, k, v, log_i, log_f, state):
    B, H, T, _ = q.shape
    L = M_CHUNK
    nc = T // L

    def chunks(a):
        return jnp.moveaxis(a.reshape(a.shape[:2] + (nc, L) + a.shape[3:]), 2, 0)

    tril = jnp.tril(jnp.ones((L, L), dtype=bool))

    def step(carry, inp):
        C, n, m = carry
        qc, kc, vc, li, lf = inp
        b = jnp.cumsum(lf, axis=-1)
        d_mat = jnp.where(tril, b[..., :, None] - b[..., None, :] + li[..., None, :], -jnp.inf)
        inter = b + m[..., None]
        m_t = jnp.maximum(jnp.max(d_mat, axis=-1), inter)
        s = jnp.einsum('bhtd,bhsd->bhts', qc, kc) * jnp.exp(d_mat - m_t[..., None])
        w_inter = jnp.exp(inter - m_t)
        num = (w_inter[..., None] * jnp.einsum('bhtd,bhde->bhte', qc, C)
               + jnp.einsum('bhts,bhse->bhte', s, vc))
        den = w_inter * jnp.einsum('bhtd,bhd->bht', qc, n) + jnp.sum(s, axis=-1)
        h = num / jnp.maximum(jnp.abs(den), jnp.exp(-m_t))[..., None]
        b_last = b[..., -1]
        g = b_last[..., None] - b + li
        m_new = jnp.maximum(b_last + m, jnp.max(g, axis=-1))
        w_k = jnp.exp(g - m_new[..., None])
        decay = jnp.exp(b_last + m - m_new)
        C_new = decay[..., None, None] * C + jnp.einsum('bhs,bhsd,bhse->bhde', w_k, kc, vc)
        n_new = decay[..., None] * n + jnp.einsum('bhs,bhsd->bhd', w_k, kc)
        return (C_new, n_new, m_new), h

    state, hs = lax.scan(step, state, (chunks(q), chunks(k), chunks(v), chunks(log_i), chunks(log_f)))
    h = jnp.moveaxis(hs, 0, 2).reshape(B, H, T, v.shape[-1])
    return h, state


def mlstm_bidir(q, k, v, gates, states):
    i_f, f_f, i_b, f_b = gates
    h_f, s_f = mlstm_chunked(q, k, v, i_f, f_f, states[0])
    flip = lambda a: jnp.flip(a, axis=2)
    h_b, s_b = mlstm_chunked(flip(q), flip(k), flip(v), flip(i_b), flip(f_b), states[1])
    return h_f + flip(h_b), (s_f, s_b)


def mlstm_heads(a, d):
    B, T, _ = a.shape
    return a.reshape(B, T, M_HEADS, d).transpose(0, 2, 1, 3).astype(jnp.float32)


def mlstm_gates(pre, gate_b):
    B, T, _ = pre.shape
    g = (pre + gate_b).astype(jnp.float32).reshape(B, T, 4, M_HEADS).transpose(2, 0, 3, 1)
    return (g[0], jax.nn.log_sigmoid(g[1]), g[2], jax.nn.log_sigmoid(g[3]))


def mlstm_out(hm, o, norm_g):
    B, H, T, dv = hm.shape
    hn = rmsnorm(hm, norm_g[:, None, :]).transpose(0, 2, 1, 3).reshape(B, T, H * dv)
    return (hn * jax.nn.sigmoid(o.astype(jnp.float32))).astype(o.dtype)


def mla_qkv(cq, ckv, kr, q_norm_g, kv_norm_g, w_uq, w_ukv, rope):
    B, T, _ = cq.shape
    q = (rmsnorm(cq, q_norm_g) @ w_uq).reshape(B, T, B_HEADS, B_NOPE + B_ROPE)
    kv = (rmsnorm(ckv, kv_norm_g) @ w_ukv).reshape(B, T, B_HEADS, B_NOPE + B_V)
    q_nope, q_rope = q[..., :B_NOPE], q[..., B_NOPE:]
    k_nope, v = kv[..., :B_NOPE], kv[..., B_NOPE:]
    k_rope = kr[:, :, None, :]
    if rope is not None:
        q_rope = apply_rope(q_rope, *rope)
        k_rope = apply_rope(k_rope, *rope)
    q = jnp.concatenate([q_nope, q_rope], axis=-1)
    k = jnp.concatenate([k_nope, jnp.broadcast_to(k_rope, k_nope.shape[:-1] + (B_ROPE,))], axis=-1)
    return q, k, v


def mixer_ab(h, hc, w_in, gate_b, m_norm_g, q_norm_g, kv_norm_g, w_uq, w_ukv, rope, ctx_out):
    B, T, _ = h.shape
    mq, mk, mv, mo, mg, cq, ckv, kr = split_cols(h @ w_in, AB_SPLITS)
    mq_c, mk_c, mv_c, mo_c, mg_c, cq_c, ckv_c, kr_c = split_cols(hc @ w_in, AB_SPLITS)
    zero = mlstm_zero_state(B)
    q_scale = M_DK ** -0.5
    h_mc, ctx_states = mlstm_bidir(mlstm_heads(mq_c, M_DK) * q_scale, mlstm_heads(mk_c, M_DK),
                                   mlstm_heads(mv_c, M_DV), mlstm_gates(mg_c, gate_b), (zero, zero))
    h_ml, _ = mlstm_bidir(mlstm_heads(mq, M_DK) * q_scale, mlstm_heads(mk, M_DK),
                          mlstm_heads(mv, M_DV), mlstm_gates(mg, gate_b), ctx_states)
    m_lat = mlstm_out(h_ml, mo, m_norm_g)
    a_scale = (B_NOPE + B_ROPE) ** -0.5
    q_l, k_l, v_l = mla_qkv(cq, ckv, kr, q_norm_g, kv_norm_g, w_uq, w_ukv, rope)
    q_c, k_c, v_c = mla_qkv(cq_c, ckv_c, kr_c, q_norm_g, kv_norm_g, w_uq, w_ukv, None)
    b_lat = blocked_attention(q_l, jnp.concatenate([k_l, k_c], axis=1),
                              jnp.concatenate([v_l, v_c], axis=1), a_scale).reshape(B, T, B_WIDTH)
    y_lat = jnp.concatenate([m_lat, b_lat.astype(m_lat.dtype)], axis=-1)
    if not ctx_out:
        return y_lat, None
    m_ctx = mlstm_out(h_mc, mo_c, m_norm_g)
    b_ctx = dense_attention(q_c, k_c, v_c, a_scale).reshape(B, hc.shape[1], B_WIDTH)
    return y_lat, jnp.concatenate([m_ctx, b_ctx.astype(m_ctx.dtype)], axis=-1)


def na_mixer(h, hc, w_in, rel_bias, ctx_out):
    B, T, _ = h.shape
    rows = T // GRID_W
    H, d = C_HEADS, C_HEAD_DIM
    scale = d ** -0.5
    q, k, v = [a.reshape(B, T, H, d) for a in jnp.split(h @ w_in, 3, axis=-1)]
    qc, kc, vc = [a.reshape(B, hc.shape[1], H, d) for a in jnp.split(hc @ w_in, 3, axis=-1)]
    kh = min(NA_KH_MAX, rows)
    nwin = kh * NA_KW
    row_start = jnp.clip(jnp.arange(rows) - kh // 2, 0, rows - kh)
    col_idx = jnp.clip(jnp.arange(GRID_W) - NA_KW // 2, 0, GRID_W - NA_KW)[:, None] + jnp.arange(NA_KW)
    ci = col_idx - jnp.arange(GRID_W)[:, None] + NA_KW - 1
    k_grid = k.reshape(B, rows, GRID_W, H, d)
    v_grid = v.reshape(B, rows, GRID_W, H, d)
    q_rows = jnp.moveaxis(q.reshape(B, rows, GRID_W, H, d), 1, 0)

    def row_block(args):
        qr, r, rs = args
        kr = lax.dynamic_slice_in_dim(k_grid, rs, kh, axis=1)[:, :, col_idx]
        vr = lax.dynamic_slice_in_dim(v_grid, rs, kh, axis=1)[:, :, col_idx]
        ri = rs + jnp.arange(kh) - r + NA_KH_MAX - 1
        bias = rel_bias[:, ri[:, None, None], ci[None, :, :]]
        bias = bias.transpose(0, 2, 1, 3).reshape(H, GRID_W, nwin)
        s_win = jnp.einsum('bqhd,bxqyhd->bhqxy', qr, kr).reshape(B, H, GRID_W, nwin) * scale + bias
        s_ctx = jnp.einsum('bqhd,bkhd->bhqk', qr, kc) * scale
        p = jax.nn.softmax(jnp.concatenate([s_win, s_ctx], axis=-1).astype(jnp.float32), axis=-1)
        p = p.astype(v.dtype)
        p_win = p[..., :nwin].reshape(B, H, GRID_W, kh, NA_KW)
        return (jnp.einsum('bhqxy,bxqyhd->bqhd', p_win, vr)
                + jnp.einsum('bhqk,bkhd->bqhd', p[..., nwin:], vc))

    o = lax.map(row_block, (q_rows, jnp.arange(rows), row_start))
    y_lat = jnp.moveaxis(o, 0, 1).reshape(B, T, C_WIDTH)
    if not ctx_out:
        return y_lat, None
    y_ctx = dense_attention(qc, kc, vc, scale).reshape(B, hc.shape[1], C_WIDTH)
    return y_lat, y_ctx


def setup_inputs(seed: int = 0) -> dict:
    key = jax.random.key(seed)
    ks = jax.random.split(key, 24)
    f32 = jnp.float32
    nrm = lambda k, shape, s: jax.random.normal(k, shape, f32) * s
    gain = lambda k, shape: 1.0 + nrm(k, shape, 0.02)
    f_bias = jnp.linspace(3.0, 6.0, M_HEADS, dtype=f32)
    gate_base = jnp.concatenate([jnp.zeros((M_HEADS,), f32), f_bias, jnp.zeros((M_HEADS,), f32), f_bias])
    return {
        "x": nrm(ks[0], (BATCH, SEQ, D_MODEL), 1.0),
        "c": nrm(ks[1], (BATCH, D_MODEL), 1.0),
        "ctx": nrm(ks[2], (BATCH, CTX_LEN, D_MODEL), 1.0),
        "c_ctx": nrm(ks[3], (D_MODEL,), 1.0),
        "ada_w": nrm(ks[4], (DEPTH, D_MODEL, 6 * D_MODEL), 0.5 * D_MODEL ** -0.5),
        "ada_b": nrm(ks[5], (DEPTH, 6 * D_MODEL), 0.02),
        "norm1_g": gain(ks[6], (DEPTH, D_MODEL)),
        "norm2_g": gain(ks[7], (DEPTH, D_MODEL)),
        "mlp_w1": nrm(ks[8], (DEPTH, D_MODEL, D_FF), D_MODEL ** -0.5),
        "mlp_w2": nrm(ks[9], (DEPTH, D_FF, D_MODEL), D_FF ** -0.5),
        "ab_w_in": nrm(ks[10], (N_EVEN, D_MODEL, AB_IN), D_MODEL ** -0.5),
        "ab_gate_b": gate_base + nrm(ks[11], (N_EVEN, 4 * M_HEADS), 0.1),
        "ab_m_norm_g": gain(ks[12], (N_EVEN, M_HEADS, M_DV)),
        "ab_q_norm_g": gain(ks[13], (N_EVEN, B_Q_RANK)),
        "ab_kv_norm_g": gain(ks[14], (N_EVEN, B_KV_RANK)),
        "ab_w_uq": nrm(ks[15], (N_EVEN, B_Q_RANK, B_HEADS * (B_NOPE + B_ROPE)), B_Q_RANK ** -0.5),
        "ab_w_ukv": nrm(ks[16], (N_EVEN, B_KV_RANK, B_HEADS * (B_NOPE + B_V)), B_KV_RANK ** -0.5),
        "ab_w_out": nrm(ks[17], (N_EVEN, AB_WIDTH, D_MODEL), AB_WIDTH ** -0.5),
        "na_w_in": nrm(ks[18], (N_ODD, D_MODEL, 3 * C_WIDTH), D_MODEL ** -0.5),
        "na_rel_bias": nrm(ks[19], (N_ODD, C_HEADS, 2 * NA_KH_MAX - 1, 2 * NA_KW - 1), 0.5),
        "na_w_out": nrm(ks[20], (N_ODD, C_WIDTH, D_MODEL), C_WIDTH ** -0.5),
        "final_norm_g": gain(ks[21], (D_MODEL,)),
    }


def reference(x, c, ctx, c_ctx, ada_w, ada_b, norm1_g, norm2_g, mlp_w1, mlp_w2,
              ab_w_in, ab_gate_b, ab_m_norm_g, ab_q_norm_g, ab_kv_norm_g, ab_w_uq, ab_w_ukv, ab_w_out,
              na_w_in, na_rel_bias, na_w_out, final_norm_g):
    T = x.shape[1]
    rope = rope_2d(T)
    for i in range(DEPTH):
        ctx_out = i < DEPTH - 1
        mod = jax.nn.silu(c) @ ada_w[i] + ada_b[i]
        mod_c = jax.nn.silu(c_ctx) @ ada_w[i] + ada_b[i]
        sh1, sc1, g1, sh2, sc2, g2 = jnp.split(mod[:, None, :], 6, axis=-1)
        csh1, csc1, cg1, csh2, csc2, cg2 = jnp.split(mod_c, 6, axis=-1)
        h = modulate(x, norm1_g[i], sh1, sc1)
        hc = modulate(ctx, norm1_g[i], csh1, csc1)
        if i % 2 == 0:
            e = i // 2
            y, yc = mixer_ab(h, hc, ab_w_in[e], ab_gate_b[e], ab_m_norm_g[e], ab_q_norm_g[e],
                             ab_kv_norm_g[e], ab_w_uq[e], ab_w_ukv[e], rope, ctx_out)
            w_out = ab_w_out[e]
        else:
            o = i // 2
            y, yc = na_mixer(h, hc, na_w_in[o], na_rel_bias[o], ctx_out)
            w_out = na_w_out[o]
        x = x + g1 * (y @ w_out)
        x = x + g2 * sqrelu_mlp(modulate(x, norm2_g[i], sh2, sc2), mlp_w1[i], mlp_w2[i])
        if ctx_out:
            ctx = ctx + cg1 * (yc @ w_out)
            ctx = ctx + cg2 * sqrelu_mlp(modulate(ctx, norm2_g[i], csh2, csc2), mlp_w1[i], mlp_w2[i])
    return rmsnorm(x, final_norm_g)
```

```python
import contextlib
import numpy as np
import concourse.bass as bass
import concourse.mybir as mybir
from concourse.bass_utils import run_bass_kernel_spmd

F32 = mybir.dt.float32
BF16 = mybir.dt.bfloat16
AF = mybir.ActivationFunctionType
ALU = mybir.AluOpType

D = 1024
SEQ = 8192
CTX = 256
TL = 4352
TLC = TL + CTX
NK = SEQ + CTX
NOWN = 4096
EPS = 1e-6
GRID_W = 64
NEG = -30000.0


class Prog:
    ENG = ('pe', 'act', 'dve', 'pool', 'sp')
    NDSEM = 6

    def __init__(self, nc, stack):
        self.nc = nc
        self.stream = {e: [] for e in self.ENG}
        self.csem = {e: stack.enter_context(nc.semaphore("c_" + e)) for e in ('pe', 'act', 'dve', 'pool')}
        self.ccnt = {e: 0 for e in self.csem}
        self.dsem = {q: [stack.enter_context(nc.semaphore("d_%s%d" % (q, i))) for i in range(self.NDSEM)]
                     for q in ('sp', 'pool', 'act')}
        self.dcnt = {q: [0] * self.NDSEM for q in self.dsem}
        self.dnext = {q: 0 for q in self.dsem}
        self.waited = {e: {} for e in self.ENG}
        self.lastw = {}
        self.readers = {}
        self.nops = 0
        self.excl = set(["b%d" % i for i in range(8)])

    def _need(self, eng, dep, waits):
        key, sem, val = dep
        if self.waited[eng].get(key, 0) >= val:
            return
        self.waited[eng][key] = val
        waits.append((sem, val))

    def _deps(self, eng, reads, writes):
        waits = []
        for r in reads:
            d = self.lastw.get(r)
            if d is not None:
                self._need(eng, d, waits)
        for w in writes:
            d = self.lastw.get(w)
            if d is not None:
                self._need(eng, d, waits)
            for d in self.readers.get(w, ()):
                self._need(eng, d, waits)
        return waits

    def _commit(self, dep, reads, writes):
        for r in reads:
            lst = self.readers.setdefault(r, [])
            lst.append(dep)
            if len(lst) > 64:
                best = {}
                for d in lst:
                    if d[0] not in best or best[d[0]][2] < d[2]:
                        best[d[0]] = d
                self.readers[r] = list(best.values())
        for w in writes:
            self.lastw[w] = dep
            self.readers[w] = []

    def op(self, eng, fn, reads=(), writes=()):
        ex = [r for r in reads if r in self.excl]
        if ex:
            writes = list(writes) + ex
        waits = self._deps(eng, reads, writes)
        self.ccnt[eng] += 1
        n = self.ccnt[eng]
        sem = self.csem[eng]
        dep = (('c', eng), sem, n)
        if eng == 'pe':
            self.waited[eng][('c', eng)] = n
        self.stream[eng].append((waits, fn, (sem, 1)))
        self._commit(dep, reads, writes)
        self.nops += 1

    def dma(self, q, fn, reads=(), writes=()):
        waits = self._deps(q, reads, writes)
        s = self.dnext[q]
        self.dnext[q] = (s + 1) % self.NDSEM
        sem = self.dsem[q][s]
        prev = self.dcnt[q][s]
        if prev > 0:
            self._need(q, (('d', q, s), sem, 16 * prev), waits)
        self.dcnt[q][s] = prev + 1
        dep = (('d', q, s), sem, 16 * (prev + 1))
        self.stream[q].append((waits, fn, (sem, 16)))
        self._commit(dep, reads, writes)
        self.nops += 1

    def barrier(self):
        deps = []
        for e in self.csem:
            if self.ccnt[e] > 0:
                deps.append((('c', e), self.csem[e], self.ccnt[e]))
        for q in self.dsem:
            for s in range(self.NDSEM):
                if self.dcnt[q][s] > 0:
                    deps.append((('d', q, s), self.dsem[q][s], 16 * self.dcnt[q][s]))
        for eng in self.ENG:
            waits = []
            for d in deps:
                if d[0] == ('c', eng) and eng == 'pe':
                    continue
                self._need(eng, d, waits)
            if waits:
                self.stream[eng].append((waits, None, None))
        self.lastw = {}
        self.readers = {}

    def emit(self, block):
        def run(engobj, lst):
            for waits, fn, inc in lst:
                for sem, val in waits:
                    engobj.wait_ge(sem, val)
                if fn is not None:
                    ins = fn(engobj)
                    ins.then_inc(inc[0], inc[1])

        @block.tensor
        def _(e):
            run(e, self.stream['pe'])

        @block.scalar
        def _(e):
            run(e, self.stream['act'])

        @block.vector
        def _(e):
            run(e, self.stream['dve'])

        @block.gpsimd
        def _(e):
            run(e, self.stream['pool'])

        @block.sync
        def _(e):
            run(e, self.stream['sp'])


def build_program(stop_after=None, dbg=False):
    nc = bass.Bass("TRN2", target_bir_lowering=False)

    def din(name, shape, dt=F32):
        return nc.dram_tensor(name, list(shape), dt, kind="ExternalInput").ap()

    def dscr(name, shape, dt):
        return nc.dram_tensor(name, list(shape), dt, kind="Internal").ap()

    xT_in = din("xT", [8, 128, SEQ])
    ctxT_in = din("ctxT", [8, 128, CTX])
    scc_in = din("scc", [128, 16])
    ropeC_in = din("ropeC", [32, NK])
    ropeS_in = din("ropeS", [32, NK])
    ada_w_in = din("ada_w", [2, D, 6 * D])
    ada_b_in = din("ada_b", [2, 128, 48])
    n1g_in = din("n1g", [2, 128, 8])
    n2g_in = din("n2g", [2, 128, 8])
    fng_in = din("fng", [128, 8])
    w1_in = din("mlp_w1", [2, D, 4 * D])
    w2_in = din("mlp_w2", [2, 4 * D, D])
    wm_in = din("wm", [4, D, 512])
    wAtm_in = din("wAtm", [D, 1040])
    wAfm_in = din("wAfm", [D, 448])
    gateb_in = din("gateb", [128, 64])
    mng_in = din("mng", [128, 512])
    qng_in = din("qng", [128, 2])
    kvng_in = din("kvng", [128, 1])
    wuq_in = din("wuq", [256, 768])
    wuqB_in = din("wuqB", [256, 768])
    wukv_in = din("wukv", [128, 1024])
    wout0_in = din("wout0", [D, D])
    wn_in = din("wn", [8, D, 384])
    natab_in = din("natab", [16, 2, 128, 6 * 256])
    wout1_in = din("wout1", [D, D])

    out_ap = nc.dram_tensor("out", [8, 128, NOWN], F32, kind="ExternalOutput").ap()
    dbg_ap = nc.dram_tensor("dbg", [8, 128, TLC], F32, kind="ExternalOutput").ap() if dbg else None

    xs = dscr("xs", [8, 128, TLC], F32)
    ys = dscr("ys", [8, 128, TLC], BF16)
    cqn_s = dscr("cqn_s", [2, 128, TLC], BF16)
    ckvn_s = dscr("ckvn_s", [128, NK], BF16)
    kr_s = dscr("kr_s", [32, NK], BF16)

    with contextlib.ExitStack() as st:
        _tn = [0]

        def T(stack, name, shape, dt):
            _tn[0] += 1
            return stack.enter_context(nc.sbuf_tensor("s%d_%s" % (_tn[0], name), list(shape), dt))

        B = [st.enter_context(nc.psum_tensor("b%d" % i, [128, 512], F32)) for i in range(8)]
        BK = ["b%d" % i for i in range(8)]

        identb = T(st, "identb", [128, 128], BF16)
        onesb = T(st, "onesb", [128, 128], BF16)
        onesf = T(st, "onesf", [128, 128], F32)
        trif = T(st, "trif", [128, 128], F32)
        trib = T(st, "trib", [128, 128], F32)
        epsc = T(st, "epsc", [128, 1], F32)
        onec = T(st, "onec", [128, 1], F32)
        mod = [T(st, "mod%d" % l, [128, 96], F32) for l in range(2)]
        A1 = [T(st, "A1_%d" % l, [128, 16], F32) for l in range(2)]
        A2 = [T(st, "A2_%d" % l, [128, 16], F32) for l in range(2)]
        fng = T(st, "fng", [128, 8], F32)
        Crb = T(st, "Crb", [128, 4, 129], F32)

        pg = Prog(nc, st)
        block = st.enter_context(nc.Block())

        def MM(out, lhsT, rhs, start=True, stop=True, R=(), W=()):
            pg.op('pe', lambda e: e.matmul(out, lhsT=lhsT, rhs=rhs, start=start, stop=stop), R, W)

        def TR(out, in_, R=(), W=()):
            pg.op('pe', lambda e: e.transpose(out, in_, identb[:]), list(R) + ['identb'], W)

        def ACT(out, in_, func, bias=None, scale=None, accum=None, R=(), W=()):
            kw = {}
            if bias is not None:
                kw['bias'] = bias
            if scale is not None:
                kw['scale'] = scale
            if accum is not None:
                kw['accum_out'] = accum
            pg.op('act', lambda e: e.activation(out=out, in_=in_, func=func, **kw), R, W)

        def TT(eng, out, in0, in1, op, R=(), W=()):
            pg.op(eng, lambda e: e.tensor_tensor(out=out, in0=in0, in1=in1, op=op), R, W)

        def TS(eng, out, in0, s1, op0, s2=None, op1=None, R=(), W=()):
            if op1 is None:
                pg.op(eng, lambda e: e.tensor_scalar(out=out, in0=in0, scalar1=s1, scalar2=None, op0=op0), R, W)
            else:
                pg.op(eng, lambda e: e.tensor_scalar(out=out, in0=in0, scalar1=s1, scalar2=s2, op0=op0, op1=op1), R, W)

        def STT(eng, out, in0, scalar, in1, op0, op1, R=(), W=()):
            pg.op(eng, lambda e: e.scalar_tensor_tensor(out=out, in0=in0, scalar=scalar, in1=in1, op0=op0, op1=op1), R, W)

        def CP(eng, out, in_, R=(), W=()):
            if eng == 'act':
                pg.op('act', lambda e: e.copy(out=out, in_=in_), R, W)
            else:
                pg.op(eng, lambda e: e.tensor_copy(out=out, in_=in_), R, W)

        def RECIP(out, in_, R=(), W=()):
            pg.op('dve', lambda e: e.reciprocal(out=out, in_=in_), R, W)

        def MEMSET(eng, ap, val, W=()):
            pg.op(eng, lambda e: e.memset(ap, val), (), W)

        def DMA(q, out, in_, R=(), W=()):
            pg.dma(q, lambda e: e.dma_start(out=out, in_=in_), R, W)

        def wview(ap2d):
            return ap2d.rearrange("(kc p) n -> p kc n", p=128)

        MEMSET('pool', onesf[:], 1.0, W=['onesf'])
        MEMSET('pool', trif[:], 1.0, W=['trif'])
        MEMSET('pool', trib[:], 1.0, W=['trib'])
        MEMSET('pool', epsc[:], EPS, W=['epsc'])
        MEMSET('pool', onec[:], 1.0, W=['onec'])
        MEMSET('pool', Crb[:], 0.0, W=['Crb'])
        pg.op('pool', lambda e: e.affine_select(out=trif[:], in_=trif[:], pattern=[[1, 128]], compare_op=ALU.is_ge,
                                                fill=0.0, base=0, channel_multiplier=-1), ['trif'], ['trif'])
        pg.op('pool', lambda e: e.affine_select(out=trib[:], in_=trib[:], pattern=[[-1, 128]], compare_op=ALU.is_ge,
                                                fill=0.0, base=0, channel_multiplier=1), ['trib'], ['trib'])
        CP('dve', onesb[:], onesf[:], R=['onesf'], W=['onesb'])
        TT('dve', identb[:], trif[:], trib[:], ALU.mult, R=['trif', 'trib'], W=['identb'])
        DMA('sp', fng[:], fng_in, W=['fng'])

        def rmsnorm_fm(ph_bufs, srcs, src_keys, n, Dn, scales, shifts, outs, out_keys, scale_keys=()):
            sq, rstd, tmpf = ph_bufs
            nch = len(srcs)
            for c in range(nch):
                ACT(sq[:, c, :n], srcs[c], AF.Square, R=[src_keys[c]], W=['sq%d' % c])
            for c in range(nch):
                MM(B[0][:, :n], lhsT=onesb[:], rhs=sq[:, c, :n], start=(c == 0), stop=(c == nch - 1),
                   R=['sq%d' % c, 'onesb'], W=['b0'])
            ACT(rstd[:, :n], B[0][:, :n], AF.Sqrt, bias=epsc[:, 0:1], scale=1.0 / Dn, R=['b0', 'epsc'], W=['rstd'])
            RECIP(rstd[:, :n], rstd[:, :n], R=['rstd'], W=['rstd'])
            for c in range(nch):
                if shifts is None:
                    STT('dve', outs[c], srcs[c], scales[c], rstd[:, :n], ALU.mult, ALU.mult,
                        R=[src_keys[c], 'rstd'] + list(scale_keys), W=[out_keys[c]])
                else:
                    tk = 'tmpf%d' % (c % 2)
                    STT('dve', tmpf[c % 2][:, :n], srcs[c], scales[c], rstd[:, :n], ALU.mult, ALU.mult,
                        R=[src_keys[c], 'rstd'] + list(scale_keys), W=[tk])
                    ACT(outs[c], tmpf[c % 2][:, :n], AF.Identity, bias=shifts[c], R=[tk] + list(scale_keys),
                        W=[out_keys[c]])

        def phase_mod(l):
            with contextlib.ExitStack() as ph:
                wq = [T(ph, "adaq%d" % i, [128, 8, 1536], BF16) for i in range(2)]
                sccf = T(ph, "sccf", [128, 16], F32)
                sccb = T(ph, "sccb", [128, 16], BF16)
                adab = T(ph, "adab", [128, 48], F32)
                n1 = T(ph, "n1", [128, 8], F32)
                n2 = T(ph, "n2", [128, 8], F32)
                DMA('sp', sccf[:], scc_in, W=['sccf'])
                DMA('sp', adab[:], ada_b_in[l], W=['adab'])
                DMA('sp', n1[:], n1g_in[l], W=['n1'])
                DMA('sp', n2[:], n2g_in[l], W=['n2'])
                ACT(sccb[:], sccf[:], AF.Silu, R=['sccf'], W=['sccb'])
                for q in range(4):
                    buf = wq[q % 2]
                    key = "adaq%d" % (q % 2)
                    DMA('pool', buf[:], wview(ada_w_in[l][:, q * 1536:(q + 1) * 1536]), W=[key])
                    for f in range(12):
                        fc = q * 12 + f
                        for kc in range(8):
                            MM(B[1][:, fc:fc + 49:48], lhsT=buf[:, kc, f * 128:(f + 1) * 128],
                               rhs=sccb[:, 2 * kc:2 * kc + 2], start=(kc == 0), stop=(kc == 7),
                               R=[key, 'sccb'], W=['b1'])
                m = mod[l]
                mk = 'mod%d' % l
                for j in range(2):
                    TT('dve', m[:, j * 48:(j + 1) * 48], B[1][:, j * 48:(j + 1) * 48], adab[:], ALU.add,
                       R=['b1', 'adab'], W=[mk])
                for j in range(2):
                    STT('dve', A1[l][:, j * 8:(j + 1) * 8], m[:, j * 48 + 8:j * 48 + 16], 1.0, n1[:], ALU.add, ALU.mult,
                        R=[mk, 'n1'], W=['A1_%d' % l])
                    STT('dve', A2[l][:, j * 8:(j + 1) * 8], m[:, j * 48 + 32:j * 48 + 40], 1.0, n2[:], ALU.add, ALU.mult,
                        R=[mk, 'n2'], W=['A2_%d' % l])
                pg.barrier()

        def mod_ap(l, j, which, c):
            col = j * 48 + which * 8 + c
            return mod[l][:, col:col + 1]

        def derive_gates(G, gkey, nt, LF, EU, WW, DEC, tmpg, pfx):
            k = lambda s: pfx + s
            ACT(tmpg[:, :nt, :], G[:, :nt, 8:16], AF.Exp, scale=-1.0, R=[gkey], W=[k('tmpg')])
            ACT(tmpg[:, :nt, :], tmpg[:, :nt, :], AF.Ln, bias=onec[:, 0:1], R=[k('tmpg'), 'onec'], W=[k('tmpg')])
            TS('dve', LF[:, :nt, :], tmpg[:, :nt, :], -1.0, ALU.mult, R=[k('tmpg')], W=[k('LF')])
            for ti in range(nt):
                MM(B[1][:, ti * 8:ti * 8 + 4], lhsT=trif[:], rhs=LF[:, ti, 0:4], R=[k('LF'), 'trif'], W=['b1'])
                MM(B[1][:, ti * 8 + 4:ti * 8 + 8], lhsT=trib[:], rhs=LF[:, ti, 4:8], R=[k('LF'), 'trib'], W=['b1'])
                MM(B[0][:, ti * 8:ti * 8 + 8], lhsT=onesf[:], rhs=LF[:, ti, :], R=[k('LF'), 'onesf'], W=['b0'])
            bview = B[1][:, 0:nt * 8].rearrange("p (t e) -> p t e", e=8)
            tview = B[0][:, 0:nt * 8].rearrange("p (t e) -> p t e", e=8)
            TT('dve', tmpg[:, :nt, :], G[:, :nt, 0:8], bview, ALU.subtract, R=[gkey, 'b1'], W=[k('tmpg')])
            ACT(EU[:, :nt, :], tmpg[:, :nt, :], AF.Exp, R=[k('tmpg')], W=[k('EU')])
            TT('dve', tmpg[:, :nt, :], tmpg[:, :nt, :], tview, ALU.add, R=[k('tmpg'), 'b0'], W=[k('tmpg')])
            ACT(WW[:, :nt, :], tmpg[:, :nt, :], AF.Exp, R=[k('tmpg')], W=[k('WW')])
            ACT(DEC[:, :nt, :], tview, AF.Exp, R=['b0'], W=[k('DEC')])

        def phase_A(ph, hT, G):
            with contextlib.ExitStack() as pa:
                xg = [T(pa, "xg%d" % i, [128, 8, 512], F32) for i in range(2)]
                hgs = [T(pa, "hg%d" % i, [128, 8, 512], BF16) for i in range(2)]
                sq = T(pa, "sq", [128, 8, 512], BF16)
                rstd = T(pa, "rstd", [128, 512], F32)
                tmpf = [T(pa, "tmpf%d" % i, [128, 512], F32) for i in range(2)]
                wtm = T(pa, "wtm", [128, 8, 1040], BF16)
                wfm = T(pa, "wfm", [128, 8, 448], BF16)
                gateb = T(pa, "gateb", [128, 4, 16], F32)
                qng = T(pa, "qng", [128, 2], F32)
                kvng = T(pa, "kvng", [128, 1], F32)
                rc = T(pa, "rc", [128, 512], F32)
                rs = T(pa, "rs", [128, 512], F32)
                tA = T(pa, "tA", [128, 512], F32)
                tB = T(pa, "tB", [128, 512], F32)
                krst = T(pa, "krst", [128, 512], BF16)
                ckvst = T(pa, "ckvst", [128, 512], BF16)
                cqst = T(pa, "cqst", [128, 2, 512], BF16)
                Gr = T(pa, "Gr", [128, 4, 16], F32)
                LFr = T(pa, "LFr", [128, 4, 8], F32)
                EUr = T(pa, "EUr", [128, 4, 8], F32)
                WWr = T(pa, "WWr", [128, 4, 8], F32)
                DECr = T(pa, "DECr", [128, 4, 8], F32)
                tmpgr = T(pa, "tmpgr", [128, 4, 8], F32)
                mkt = T(pa, "mkt_a", [128, 512], BF16)
                Vw = T(pa, "Vw_a", [128, 4, 129], BF16)

                DMA('pool', wtm[:], wview(wAtm_in), W=['wtm'])
                DMA('pool', wfm[:], wview(wAfm_in), W=['wfm'])
                DMA('sp', gateb[:], gateb_in.rearrange("p (t e) -> p t e", e=16), W=['gateb'])
                DMA('sp', qng[:], qng_in, W=['qng'])
                DMA('sp', kvng[:], kvng_in, W=['kvng'])

                groups = []
                groups.append(('ctx', 0, CTX, TL, SEQ))
                j = SEQ
                first = True
                while j > TL:
                    n = 256 if first else 512
                    first = False
                    j -= n
                    groups.append(('rem', j, n, None, j))
                j = 0
                while j < TL:
                    n = min(512, TL - j)
                    groups.append(('loc', j, n, j, j))
                    j += n

                ginfo = {}

                def p1(gi):
                    kind, j0, n, loc0, key0 = groups[gi]
                    jm = 1 if kind == 'ctx' else 0
                    x = xg[gi % 2]
                    xk = 'xg%d' % (gi % 2)
                    src = ctxT_in if kind == 'ctx' else xT_in[:, :, j0:j0 + n]
                    DMA('sp', x[:, :, :n], src.rearrange("c p t -> p c t"), W=[xk])
                    if kind == 'rem':
                        hg_ = hgs[gi % 2]
                        hdst = [hg_[:, c, :n] for c in range(8)]
                        hk = ['hg%d' % (gi % 2)] * 8
                        hfull = lambda c, a, b_, hg_=hg_: hg_[:, c, a:b_]
                    else:
                        hdst = [hT[:, c, loc0:loc0 + n] for c in range(8)]
                        hk = ['hTg%d' % gi] * 8
                        hfull = lambda c, a, b_, _l=loc0: hT[:, c, _l + a:_l + b_]
                    rmsnorm_fm((sq, rstd, tmpf), [x[:, c, :n] for c in range(8)], [xk] * 8, n, D,
                               [A1[0][:, jm * 8 + c:jm * 8 + c + 1] for c in range(8)],
                               [mod_ap(0, jm, 0, c) for c in range(8)], hdst, hk, scale_keys=['A1_0', 'mod0'])
                    ginfo[gi] = (hdst, hk, hfull)

                p1(0)
                for gi, (kind, j0, n, loc0, key0) in enumerate(groups):
                    nt = n // 128
                    if gi + 1 < len(groups):
                        p1(gi + 1)
                    hdst, hk, hfull = ginfo.pop(gi)
                    hkey = hk[0]
                    for c in range(8):
                        MM(B[2][:, :n], lhsT=wfm[:, c, 256:384], rhs=hdst[c], start=(c == 0), stop=(c == 7),
                           R=['wfm', hkey], W=['b2'])
                    for c in range(8):
                        MM(B[3][0:96, :n], lhsT=wfm[:, c, 320:416], rhs=hdst[c], start=(c == 0), stop=(c == 7),
                           R=['wfm', hkey], W=['b3'])
                    for c in range(8):
                        MM(B[4][0:96, :n], lhsT=wfm[:, c, 352:448], rhs=hdst[c], start=(c == 0), stop=(c == 7),
                           R=['wfm', hkey], W=['b4'])
                    rmsnorm_fm((sq, rstd, tmpf), [B[2][:, :n]], ['b2'], n, 128, [kvng[:, 0:1]], None,
                               [ckvst[:, :n]], ['ckvst'], scale_keys=['kvng'])
                    DMA('sp', ckvn_s[:, key0:key0 + n], ckvst[:, :n], R=['ckvst'], W=['ckvn_s'])
                    DMA('sp', rc[64:96, :n], ropeC_in[:, key0:key0 + n], W=['rc'])
                    DMA('sp', rs[64:96, :n], ropeS_in[:, key0:key0 + n], W=['rs'])
                    TT('dve', tA[64:96, :n], B[3][64:96, :n], rc[64:96, :n], ALU.mult, R=['b3', 'rc'], W=['tA'])
                    TT('dve', tB[64:96, :n], B[4][64:96, :n], rs[64:96, :n], ALU.mult, R=['b4', 'rs'], W=['tB'])
                    TT('pool', krst[64:96, :n], tA[64:96, :n], tB[64:96, :n], ALU.add, R=['tA', 'tB'], W=['krst'])
                    DMA('sp', kr_s[:, key0:key0 + n], krst[64:96, :n], R=['krst'], W=['kr_s'])
                    if kind != 'rem':
                        for r in range(2):
                            for c in range(8):
                                MM(B[5 + r][:, :n], lhsT=wfm[:, c, r * 128:(r + 1) * 128], rhs=hdst[c],
                                   start=(c == 0), stop=(c == 7), R=['wfm', hkey], W=[BK[5 + r]])
                        rmsnorm_fm((sq, rstd, tmpf), [B[5][:, :n], B[6][:, :n]], ['b5', 'b6'], n, 256,
                                   [qng[:, 0:1], qng[:, 1:2]], None, [cqst[:, 0, :n], cqst[:, 1, :n]],
                                   ['cqst', 'cqst'], scale_keys=['qng'])
                        DMA('sp', cqn_s[:, :, loc0:loc0 + n].rearrange("c p t -> p c t"), cqst[:, :, :n],
                            R=['cqst'], W=['cqn_s'])
                    for ti in range(nt):
                        for c in range(8):
                            MM(B[1][:, ti * 16:(ti + 1) * 16], lhsT=hfull(c, ti * 128, (ti + 1) * 128),
                               rhs=wtm[:, c, 1024:1040], start=(c == 0), stop=(c == 7), R=['wtm', hkey], W=['b1'])
                    gsrc = B[1][:, 0:nt * 16].rearrange("p (t e) -> p t e", e=16)
                    if kind == 'rem':
                        Gd, gk = Gr, 'Gr'
                        TT('dve', Gr[:, :nt, :], gsrc, gateb[:, :nt, :], ALU.add, R=['b1', 'gateb'], W=['Gr'])
                    else:
                        t0 = loc0 // 128
                        Gd, gk = G[:, t0:t0 + nt, :], 'G'
                        TT('dve', Gd, gsrc, gateb[:, :nt, :], ALU.add, R=['b1', 'gateb'], W=['G'])
                    if kind in ('rem', 'ctx'):
                        derive_gates(Gd, gk, nt, LFr, EUr, WWr, DECr, tmpgr, 'r')
                        for ti in reversed(range(nt)):
                            for c in range(8):
                                MM(B[5][:, :512], lhsT=hfull(c, ti * 128, (ti + 1) * 128), rhs=wtm[:, c, 0:512],
                                   start=(c == 0), stop=(c == 7), R=['wtm', hkey], W=['b5'])
                            for c in range(8):
                                MM(B[6][:, :512], lhsT=hfull(c, ti * 128, (ti + 1) * 128), rhs=wtm[:, c, 512:1024],
                                   start=(c == 0), stop=(c == 7), R=['wtm', hkey], W=['b6'])
                            CP('act', mkt[:], B[5][:, :512], R=['b5'], W=['mkt_a'])
                            for hd in range(4):
                                TS('dve', Vw[:, hd, 0:128], B[6][:, hd * 128:(hd + 1) * 128],
                                   WWr[:, ti, 4 + hd:5 + hd], ALU.mult, R=['b6', 'rWW'], W=['Vw_a%d' % hd])
                                CP('dve', Vw[:, hd, 128:129], WWr[:, ti, 4 + hd:5 + hd], R=['rWW'], W=['Vw_a%d' % hd])
                            for hd in range(4):
                                ub = hd % 2
                                MM(B[ub][:, 0:129], lhsT=mkt[:, hd * 128:(hd + 1) * 128], rhs=Vw[:, hd, :],
                                   R=['mkt_a', 'Vw_a%d' % hd], W=[BK[ub]])
                                STT('dve', Crb[:, hd, :], Crb[:, hd, :], DECr[:, ti, 4 + hd:5 + hd], B[ub][:, 0:129],
                                    ALU.mult, ALU.add, R=['Crb', BK[ub], 'rDEC'], W=['Crb'])
                pg.barrier()

        def phase_B(ph, hT, G):
            NT = TLC // 128
            with contextlib.ExitStack() as pb:
                wm = [T(pb, "wm%d" % i, [128, 8, 512], BF16) for i in range(2)]
                mqT = T(pb, "mqT", [128, TLC], BF16)
                mkT = T(pb, "mkT", [128, TLC], BF16)
                mkt = T(pb, "mkt", [128, NT, 128], BF16)
                mvx = T(pb, "mvx", [128, NT, 129], BF16)
                gso = T(pb, "gso", [128, NT, 128], BF16)
                hacc = T(pb, "hacc", [128, NT, 128], F32)
                Cst = T(pb, "Cst", [128, 2, 129], F32)
                Cbf = T(pb, "Cbf", [128, 2, 2, 129], BF16)
                NS = 4
                trilf = [[T(pb, "trilf%d_%d" % (d, i), [128, 128], F32) for i in range(NS)] for d in range(2)]
                E1 = [[T(pb, "E1_%d_%d" % (d, i), [128, 128], F32) for i in range(NS)] for d in range(2)]
                DT = [[T(pb, "DT%d_%d" % (d, i), [128, 128], F32) for i in range(NS)] for d in range(2)]
                ST = [[T(pb, "ST%d_%d" % (d, i), [128, 128], BF16) for i in range(NS)] for d in range(2)]
                QsT = [[T(pb, "QsT%d_%d" % (d, i), [128, 128], BF16) for i in range(NS)] for d in range(2)]
                Vw = [[T(pb, "Vw%d_%d" % (d, i), [128, 129], BF16) for i in range(NS)] for d in range(2)]
                dtmp = [T(pb, "dtmp%d" % i, [128, 2], F32) for i in range(4)]
                sgt = [T(pb, "sgt%d" % i, [128, 128], F32) for i in range(2)]
                LF = T(pb, "LF", [128, NT, 8], F32)
                EU = T(pb, "EU", [128, NT, 8], F32)
                WW = T(pb, "WW", [128, NT, 8], F32)
                DEC = T(pb, "DEC", [128, NT, 8], F32)
                tmpg = T(pb, "tmpg", [128, NT, 8], F32)
                ssq = T(pb, "ssq", [128, NT], F32)
                rsq = T(pb, "rsq", [128, NT], F32)
                junk = T(pb, "junk", [128, 128], F32)
                mlat = [T(pb, "mlat%d" % i, [128, 128], BF16) for i in range(2)]
                ystage = T(pb, "ystage", [128, TLC], BF16)
                mngt = T(pb, "mngt", [128, 512], F32)

                DMA('sp', mngt[:], mng_in, W=['mngt'])
                MEMSET('pool', mvx[:, :, 128:129], 1.0, W=['mvx'])
                derive_gates(G, 'G', NT, LF, EU, WW, DEC, tmpg, 'l')

                def bufsel(d, k):
                    return "%d_%d" % (d, k % NS), k % NS

                def stage1(hd, d, tl, k):
                    gi = d * 4 + hd
                    tri = trif if d == 0 else trib
                    trik = 'trif' if d == 0 else 'trib'
                    S, s_ = bufsel(d, k)
                    pb_ = 0 if d == 0 else 7
                    c0 = (k % 2) * 128
                    ACT(trilf[d][s_][:], tri[:], AF.Copy, scale=LF[:, tl, gi:gi + 1], R=['lLF', trik], W=['trilf' + S])
                    MM(B[pb_][:, c0:c0 + 128], lhsT=onesf[:], rhs=trilf[d][s_][:], R=['onesf', 'trilf' + S], W=[BK[pb_]])

                def stage2(hd, d, tl, k):
                    gi = d * 4 + hd
                    sl = slice(tl * 128, (tl + 1) * 128)
                    S, s_ = bufsel(d, k)
                    pb_ = 0 if d == 0 else 7
                    c0 = (k % 2) * 128
                    ACT(E1[d][s_][:], B[pb_][:, c0:c0 + 128], AF.Exp, R=[BK[pb_]], W=['E1_' + S])
                    ACT(Vw[d][s_][:], mvx[:, tl, :], AF.Copy, scale=WW[:, tl, gi:gi + 1], R=['mvx', 'lWW'], W=['Vw' + S])
                    MM(B[pb_][:, 256 + c0:256 + c0 + 128], lhsT=mkT[:, sl], rhs=mqT[:, sl], R=['mkT', 'mqT'], W=[BK[pb_]])

                def stage3(hd, d, tl, k):
                    gi = d * 4 + hd
                    sl = slice(tl * 128, (tl + 1) * 128)
                    tri = trif if d == 0 else trib
                    trik = 'trif' if d == 0 else 'trib'
                    S, s_ = bufsel(d, k)
                    pb_ = 0 if d == 0 else 7
                    c0 = (k % 2) * 128
                    STT('dve', DT[d][s_][:], E1[d][s_][:], EU[:, tl, gi:gi + 1], tri[:], ALU.mult, ALU.mult,
                        R=['E1_' + S, 'lEU', trik], W=['DT' + S])
                    TT('dve', ST[d][s_][:], B[pb_][:, 256 + c0:256 + c0 + 128], DT[d][s_][:], ALU.mult,
                       R=[BK[pb_], 'DT' + S], W=['ST' + S])
                    TT('pool', QsT[d][s_][:], mqT[:, sl], E1[d][s_][:], ALU.mult, R=['mqT', 'E1_' + S], W=['QsT' + S])
                    ub = 5 + d
                    u0 = (k % 3) * 129
                    MM(B[ub][:, u0:u0 + 129], lhsT=mkt[:, tl, :], rhs=Vw[d][s_][:], R=['mkt', 'Vw' + S], W=[BK[ub]])

                def mchain(hd, d, tl, k):
                    gi = d * 4 + hd
                    S, s_ = bufsel(d, k)
                    D_ = str(d)
                    nb = 1 + 2 * d + (k % 2)
                    cin = 'Cbf%d_%d' % (d, k % 2)
                    cout = 'Cbf%d_%d' % (d, (k + 1) % 2)
                    MM(B[nb][:, 0:129], lhsT=ST[d][s_][:], rhs=mvx[:, tl, :], start=True, stop=False,
                       R=['ST' + S, 'mvx'], W=[BK[nb]])
                    MM(B[nb][:, 0:129], lhsT=QsT[d][s_][:], rhs=Cbf[:, d, k % 2, :], start=False, stop=True,
                       R=['QsT' + S, cin], W=[BK[nb]])
                    ub = 5 + d
                    u0 = (k % 3) * 129
                    STT('dve', Cst[:, d, :], Cst[:, d, :], DEC[:, tl, gi:gi + 1], B[ub][:, u0:u0 + 129],
                        ALU.mult, ALU.add, R=['Cst' + D_, BK[ub], 'lDEC'], W=['Cst' + D_])
                    CP('act', Cbf[:, d, (k + 1) % 2, :], Cst[:, d, :], R=['Cst' + D_], W=[cout])

                def mevac(hd, d, tl, k, first_write):
                    nb = 1 + 2 * d + (k % 2)
                    dk_ = 'dtmp%d_%d' % (d, k % 2)
                    dt_ = dtmp[d * 2 + (k % 2)]
                    ACT(dt_[:, 0:1], B[nb][:, 128:129], AF.Abs, R=[BK[nb]], W=[dk_])
                    TS('dve', dt_[:, 0:1], dt_[:, 0:1], 1.0, ALU.max, R=[dk_], W=[dk_])
                    RECIP(dt_[:, 1:2], dt_[:, 0:1], R=[dk_], W=[dk_])
                    hk = 'hacc%d' % tl
                    if first_write:
                        TS('dve', hacc[:, tl, :], B[nb][:, 0:128], dt_[:, 1:2], ALU.mult, R=[BK[nb], dk_], W=[hk])
                    else:
                        STT('dve', hacc[:, tl, :], B[nb][:, 0:128], dt_[:, 1:2], hacc[:, tl, :], ALU.mult, ALU.add,
                            R=[BK[nb], dk_, hk], W=[hk])

                for hd in range(4):
                    w = wm[hd % 2]
                    wk = 'wm%d' % (hd % 2)
                    DMA('pool', w[:], wview(wm_in[hd]), W=[wk])
                    for g in range(TLC // 512):
                        gs = slice(g * 512, (g + 1) * 512)
                        for which, dst, dk_, scl in ((0, mqT, 'mqT', 128.0 ** -0.5), (1, mkT, 'mkT', 1.0)):
                            fb = (6, 0)[which]
                            for c in range(8):
                                MM(B[fb][:, :512], lhsT=w[:, c, which * 128:(which + 1) * 128], rhs=hT[:, c, gs],
                                   start=(c == 0), stop=(c == 7), R=[wk, 'hT'], W=[BK[fb]])
                            if which == 0:
                                ACT(dst[:, gs], B[fb][:, :512], AF.Copy, scale=scl, R=[BK[fb]], W=[dk_])
                            else:
                                CP('dve', dst[:, gs], B[fb][:, :512], R=[BK[fb]], W=[dk_])
                        for ti in range(4):
                            tl = g * 4 + ti
                            tb_ = 1 + (tl % 4)
                            for c in range(8):
                                MM(B[tb_][:, 0:384], lhsT=hT[:, c, tl * 128:(tl + 1) * 128], rhs=w[:, c, 128:512],
                                   start=(c == 0), stop=(c == 7), R=[wk, 'hT'], W=[BK[tb_]])
                            CP('dve', mkt[:, tl, :], B[tb_][:, 0:128], R=[BK[tb_]], W=['mkt'])
                            CP('act', mvx[:, tl, 0:128], B[tb_][:, 128:256], R=[BK[tb_]], W=['mvx'])
                            ACT(sgt[tl % 2][:], B[tb_][:, 256:384], AF.Sigmoid, R=[BK[tb_]], W=['sgt%d' % (tl % 2)])
                            TT('pool', gso[:, tl, :], sgt[tl % 2][:], mngt[:, hd * 128:(hd + 1) * 128], ALU.mult,
                               R=['sgt%d' % (tl % 2), 'mngt'], W=['gso'])
                    MEMSET('pool', Cst[:], 0.0, W=['Cst0', 'Cst1'])
                    MEMSET('pool', Cbf[:], 0.0, W=['Cbf0_0', 'Cbf0_1', 'Cbf1_0', 'Cbf1_1'])
                    fseq = [34, 35] + list(range(34))
                    bseq = [35, 34] + list(reversed(range(34)))
                    seqs = (fseq, bseq)
                    written = set()
                    for t in range(-3, NT + 1):
                        if 0 <= t < NT:
                            j = t
                            for d in range(2):
                                mchain(hd, d, seqs[d][j], j)
                        if 0 <= t - 1 < NT:
                            for d in range(2):
                                tl = seqs[d][t - 1]
                                mevac(hd, d, tl, t - 1, tl not in written)
                                written.add(tl)
                        if 0 <= t < NT:
                            j = t
                            if j == 1:
                                CP('dve', Cst[:, 1, :], Crb[:, hd, :], R=['Crb'], W=['Cst1'])
                                CP('act', Cbf[:, 1, 0, :], Crb[:, hd, :], R=['Crb'], W=['Cbf1_0'])
                        for stg, off in ((stage3, 1), (stage2, 2), (stage1, 3)):
                            k = t + off
                            if 0 <= k < NT:
                                for d in range(2):
                                    stg(hd, d, seqs[d][k], k)
                    for tl in range(NT):
                        ACT(junk[:], hacc[:, tl, :], AF.Square, accum=ssq[:, tl:tl + 1], R=['hacc%d' % tl],
                            W=['junk', 'ssq'])
                    ACT(rsq[:], ssq[:], AF.Sqrt, bias=epsc[:, 0:1], scale=1.0 / 128.0, R=['ssq', 'epsc'], W=['rsq'])
                    RECIP(rsq[:], rsq[:], R=['rsq'], W=['rsq'])
                    for tl in range(NT):
                        s2 = tl % 2
                        STT('dve', mlat[s2][:], hacc[:, tl, :], rsq[:, tl:tl + 1], gso[:, tl, :], ALU.mult, ALU.mult,
                            R=['hacc%d' % tl, 'rsq', 'gso'], W=['mlat%d' % s2])
                        q4 = tl % 4
                        MM(B[7][:, q4 * 128:(q4 + 1) * 128], lhsT=mlat[s2][:], rhs=identb[:], R=['mlat%d' % s2, 'identb'],
                           W=['b7'])
                        if q4 == 3:
                            CP('act', ystage[:, (tl - 3) * 128:(tl + 1) * 128], B[7][:, 0:512], R=['b7'], W=['ystage'])
                    DMA('sp', ys[hd], ystage[:], R=['ystage'], W=['ys'])
                pg.barrier()

        def phase_C():
            NKT = NK // 128
            with contextlib.ExitStack() as pc:
                cqnT = T(pc, "cqnT", [128, 2, TLC], BF16)
                ckvnT = T(pc, "ckvnT", [128, NK], BF16)
                KT = T(pc, "KT", [128, NK], BF16)
                Vext = T(pc, "Vext", [128, NKT, 128], BF16)
                QT = [T(pc, "QT%d" % i, [128, 512], BF16) for i in range(2)]
                PT = [T(pc, "PT%d" % i, [128, 512], BF16) for i in range(3)]
                ystage = T(pc, "ystageC", [128, TLC], BF16)
                rc = T(pc, "rcC", [128, TLC], F32)
                rs = T(pc, "rsC", [128, TLC], F32)
                tA = T(pc, "tAC", [128, 512], F32)
                tB = T(pc, "tBC", [128, 512], F32)
                rl = T(pc, "rl", [128, 512], F32)
                wuq = T(pc, "wuq", [128, 2, 768], BF16)
                wuqB = T(pc, "wuqB", [128, 2, 768], BF16)
                wukv = T(pc, "wukv", [128, 1024], BF16)

                DMA('sp', cqnT[:], cqn_s.rearrange("c p t -> p c t"), W=['cqnT'])
                DMA('sp', ckvnT[:], ckvn_s, W=['ckvnT'])
                MEMSET('pool', KT[64:128, :], 0.0, W=['KTr', 'KTz'])
                DMA('sp', KT[64:96, :], kr_s, W=['KTr'])
                for i_ in range(2):
                    MEMSET('pool', QT[i_][64:128, :], 0.0, W=['QT%d' % i_])
                DMA('pool', wuq[:], wview(wuq_in), W=['wuq'])
                DMA('pool', wuqB[:], wview(wuqB_in), W=['wuqB'])
                DMA('pool', wukv[:], wukv_in, W=['wukv'])
                DMA('sp', rc[64:96, 0:TL], ropeC_in[:, 0:TL], W=['rcC'])
                DMA('sp', rc[64:96, TL:TLC], ropeC_in[:, SEQ:NK], W=['rcC'])
                DMA('sp', rs[64:96, 0:TL], ropeS_in[:, 0:TL], W=['rsC'])
                DMA('sp', rs[64:96, TL:TLC], ropeS_in[:, SEQ:NK], W=['rsC'])

                qgroups = [(g * 512, 512, list(range(NKT))) for g in range(8)]
                qgroups.append((4096, 256, list(range(NKT))))
                qgroups.append((TL, 256, [NKT - 2, NKT - 1]))
                a_scale = 96.0 ** -0.5
                cnt = 0
                for h in range(8):
                    voff = 0 if h % 2 == 0 else 64
                    ooff = 64 - voff
                    MEMSET('pool', Vext[:, :, ooff:ooff + 64], 1.0, W=['Vext'])
                    for kg in range((NK + 511) // 512):
                        k0 = kg * 512
                        n = min(512, NK - k0)
                        MM(B[0][0:64, :n], lhsT=wukv[:, h * 128:h * 128 + 64], rhs=ckvnT[:, k0:k0 + n],
                           R=['wukv', 'ckvnT'], W=['b0'])
                        CP('dve', KT[0:64, k0:k0 + n], B[0][0:64, :n], R=['b0'], W=['KTn'])
                        ntile = n // 128
                        for i in range(ntile):
                            kt = kg * 4 + i
                            MM(B[1][:, i * 64:(i + 1) * 64], lhsT=ckvnT[:, kt * 128:(kt + 1) * 128],
                               rhs=wukv[:, h * 128 + 64:h * 128 + 128], R=['wukv', 'ckvnT'], W=['b1'])
                        CP('dve', Vext[:, kg * 4:kg * 4 + ntile, voff:voff + 64],
                           B[1][:, 0:ntile * 64].rearrange("p (t e) -> p t e", e=64), R=['b1'], W=['Vext'])
                    items = []
                    for gidx, (q0, n, ktiles) in enumerate(qgroups):
                        for i, kt in enumerate(ktiles):
                            items.append((gidx, q0, n, i, kt, len(ktiles)))

                    def qprep(gidx, h=h):
                        q0, n, _ = qgroups[gidx]
                        qt = QT[gidx % 2]
                        qk = 'QT%d' % (gidx % 2)
                        for r in range(2):
                            MM(B[1][0:96, :n], lhsT=wuq[:, r, h * 96:(h + 1) * 96], rhs=cqnT[:, r, q0:q0 + n],
                               start=(r == 0), stop=(r == 1), R=['wuq', 'cqnT'], W=['b1'])
                        for r in range(2):
                            MM(B[2][0:96, :n], lhsT=wuqB[:, r, h * 96:(h + 1) * 96], rhs=cqnT[:, r, q0:q0 + n],
                               start=(r == 0), stop=(r == 1), R=['wuqB', 'cqnT'], W=['b2'])
                        CP('dve', qt[0:64, :n], B[1][0:64, :n], R=['b1'], W=[qk])
                        TT('dve', tA[64:96, :n], B[1][64:96, :n], rc[64:96, q0:q0 + n], ALU.mult, R=['b1', 'rcC'], W=['tAC'])
                        TT('dve', tB[64:96, :n], B[2][64:96, :n], rs[64:96, q0:q0 + n], ALU.mult, R=['b2', 'rsC'], W=['tBC'])
                        TT('pool', qt[64:96, :n], tA[64:96, :n], tB[64:96, :n], ALU.add, R=['tAC', 'tBC'], W=[qk])

                    def s_item(t):
                        gidx, q0, n, i, kt, nk_ = items[t]
                        qt = QT[gidx % 2]
                        qk = 'QT%d' % (gidx % 2)
                        sb = 3 + (t % 3)
                        pt = PT[t % 3]
                        pk = 'PT%d' % (t % 3)
                        MM(B[sb][:, :n], lhsT=KT[:, kt * 128:(kt + 1) * 128], rhs=qt[:, :n],
                           R=['KTn', 'KTr', 'KTz', qk], W=[BK[sb]])
                        ACT(pt[:, :n], B[sb][:, :n], AF.Exp, scale=a_scale, R=[BK[sb]], W=[pk])

                    def pv_item(t, voff=voff, ooff=ooff):
                        gidx, q0, n, i, kt, nk_ = items[t]
                        pt = PT[t % 3]
                        pk = 'PT%d' % (t % 3)
                        ob = 6 if gidx % 2 == 0 else 0
                        MM(B[ob][:, :n], lhsT=Vext[:, kt, :], rhs=pt[:, :n], start=(i == 0), stop=(i == nk_ - 1),
                           R=['Vext', pk], W=[BK[ob]])
                        if i == nk_ - 1:
                            CP('dve', rl[voff:voff + 64, :n], B[ob][ooff:ooff + 64, :n], R=[BK[ob]], W=['rl'])
                            RECIP(rl[voff:voff + 64, :n], rl[voff:voff + 64, :n], R=['rl'], W=['rl'])
                            TT('dve', ystage[voff:voff + 64, q0:q0 + n], B[ob][voff:voff + 64, :n],
                               rl[voff:voff + 64, :n], ALU.mult, R=[BK[ob], 'rl'], W=['ystageC'])

                    LA = 2
                    qprep(0)
                    for t in range(len(items) + LA):
                        if t < len(items):
                            s_item(t)
                            if items[t][3] == 0 and items[t][0] + 1 < len(qgroups):
                                qprep(items[t][0] + 1)
                        if t - LA >= 0:
                            pv_item(t - LA)
                    if h % 2 == 1:
                        DMA('sp', ys[4 + h // 2], ystage[:], R=['ystageC'], W=['ys'])
                pg.barrier()

        def phase_D(l):
            ngroups = (TLC // 256) if l == 0 else (NOWN // 256)
            with contextlib.ExitStack() as pd:
                wout = T(pd, "wout", [128, 8, 1024], BF16)
                W1 = T(pd, "W1", [128, 8, 4096], BF16)
                W2 = T(pd, "W2", [128, 32, 1024], BF16)
                xgs = [T(pd, "xgD%d" % i, [128, 8, 256], F32) for i in range(2)]
                ygs = [T(pd, "ygD%d" % i, [128, 8, 256], BF16) for i in range(2)]
                h2s = [T(pd, "h2_%d" % i, [128, 8, 256], BF16) for i in range(2)]
                hid = T(pd, "hid", [128, 32, 256], BF16)
                sq = T(pd, "sqD", [128, 8, 256], BF16)
                rstd = T(pd, "rstdD", [128, 256], F32)
                tmpf = [T(pd, "tmpfD%d" % i, [128, 256], F32) for i in range(2)]
                rt = [T(pd, "rt%d" % i, [128, 256], F32) for i in range(2)]
                DMA('pool', wout[:], wview(wout0_in if l == 0 else wout1_in), W=['wout'])
                for q in range(4):
                    DMA('pool', W1[:, :, q * 1024:(q + 1) * 1024], wview(w1_in[l][:, q * 1024:(q + 1) * 1024]),
                        W=['W1_%d' % q])
                for q in range(4):
                    DMA('pool', W2[:, q * 8:(q + 1) * 8, :], wview(w2_in[l][q * 1024:(q + 1) * 1024, :]), W=['W2_%d' % q])
                n = 256
                mk = 'mod%d' % l

                def load_group(gi):
                    t0 = gi * 256
                    jm = 1 if t0 >= TL else 0
                    if l == 0:
                        src = ctxT_in if jm else xT_in[:, :, t0:t0 + n]
                    else:
                        src = xs[:, :, t0:t0 + n]
                    DMA('sp', xgs[gi % 2][:], src.rearrange("c p t -> p c t"), R=['xs'] if l == 1 else [],
                        W=['xgD%d' % (gi % 2)])
                    DMA('sp', ygs[gi % 2][:], ys[:, :, t0:t0 + n].rearrange("c p t -> p c t"), R=['ys'],
                        W=['ygD%d' % (gi % 2)])

                def front(gi):
                    t0 = gi * 256
                    jm = 1 if t0 >= TL else 0
                    xg, yg = xgs[gi % 2], ygs[gi % 2]
                    XK, YK = 'xgD%d' % (gi % 2), 'ygD%d' % (gi % 2)
                    h2_ = h2s[gi % 2]
                    HK = 'h2_%d' % (gi % 2)
                    for dc in range(8):
                        ob = 3 + dc % 4
                        for c in range(8):
                            MM(B[ob][:, :n], lhsT=wout[:, c, dc * 128:(dc + 1) * 128], rhs=yg[:, c, :],
                               start=(c == 0), stop=(c == 7), R=['wout', YK], W=[BK[ob]])
                        STT('dve', xg[:, dc, :], B[ob][:, :n], mod_ap(l, jm, 2, dc), xg[:, dc, :], ALU.mult, ALU.add,
                            R=[BK[ob], mk, XK], W=[XK])
                    rmsnorm_fm((sq, rstd, tmpf), [xg[:, c, :] for c in range(8)], [XK] * 8, n, D,
                               [A2[l][:, jm * 8 + c:jm * 8 + c + 1] for c in range(8)],
                               [mod_ap(l, jm, 3, c) for c in range(8)], [h2_[:, c, :] for c in range(8)], [HK] * 8,
                               scale_keys=['A2_%d' % l, mk])

                def w1_stage(gi):
                    h2_ = h2s[gi % 2]
                    HK = 'h2_%d' % (gi % 2)
                    for fc in range(32):
                        hb = 3 + fc % 4
                        for c in range(8):
                            MM(B[hb][:, :n], lhsT=W1[:, c, fc * 128:(fc + 1) * 128], rhs=h2_[:, c, :],
                               start=(c == 0), stop=(c == 7), R=['W1_%d' % (fc // 8), HK], W=[BK[hb]])
                        r_ = rt[fc % 2]
                        rk = 'rt%d' % (fc % 2)
                        ACT(r_[:], B[hb][:, :n], AF.Relu, R=[BK[hb]], W=[rk])
                        TT('pool', hid[:, fc, :], r_[:], r_[:], ALU.mult, R=[rk], W=['hid'])

                def w2_stage(gi):
                    t0 = gi * 256
                    jm = 1 if t0 >= TL else 0
                    xg = xgs[gi % 2]
                    XK = 'xgD%d' % (gi % 2)
                    for dc in range(8):
                        ob = 1 + dc % 2
                        for fc in range(32):
                            MM(B[ob][:, :n], lhsT=W2[:, fc, dc * 128:(dc + 1) * 128], rhs=hid[:, fc, :],
                               start=(fc == 0), stop=(fc == 31), R=['W2_%d' % (fc // 8), 'hid'], W=[BK[ob]])
                        STT('dve', xg[:, dc, :], B[ob][:, :n], mod_ap(l, jm, 5, dc), xg[:, dc, :], ALU.mult, ALU.add,
                            R=[BK[ob], mk, XK], W=[XK])
                    if l == 0:
                        DMA('sp', xs[:, :, t0:t0 + n].rearrange("c p t -> p c t"), xg[:], R=[XK], W=['xs'])
                    else:
                        rmsnorm_fm((sq, rstd, tmpf), [xg[:, c, :] for c in range(8)], [XK] * 8, n, D,
                                   [fng[:, c:c + 1] for c in range(8)], None, [xg[:, c, :] for c in range(8)],
                                   [XK] * 8, scale_keys=['fng'])
                        DMA('sp', out_ap[:, :, t0:t0 + n].rearrange("c p t -> p c t"), xg[:], R=[XK], W=['out'])

                load_group(0)
                if ngroups > 1:
                    load_group(1)
                front(0)
                for gi in range(ngroups):
                    w1_stage(gi)
                    if gi + 1 < ngroups:
                        front(gi + 1)
                    w2_stage(gi)
                    if gi + 2 < ngroups:
                        load_group(gi + 2)
                pg.barrier()

        def phase_N():
            NT = TLC // 128
            with contextlib.ExitStack() as pn:
                hT = T(pn, "hT1", [128, 8, TLC], BF16)
                with contextlib.ExitStack() as pa:
                    xg = [T(pa, "xgN%d" % i, [128, 8, 512], F32) for i in range(2)]
                    sq = T(pa, "sqN", [128, 8, 512], BF16)
                    rstd = T(pa, "rstdN", [128, 512], F32)
                    tmpf = [T(pa, "tmpfN%d" % i, [128, 512], F32) for i in range(2)]
                    for gi in range(TLC // 512):
                        x = xg[gi % 2]
                        xk = 'xgN%d' % (gi % 2)
                        DMA('sp', x[:], xs[:, :, gi * 512:(gi + 1) * 512].rearrange("c p t -> p c t"), R=['xs'], W=[xk])
                        parts = [(0, 512, 0)] if gi < 8 else [(0, 256, 0), (256, 512, 1)]
                        for (a, b_, jm) in parts:
                            n = b_ - a
                            rmsnorm_fm((sq, rstd, tmpf), [x[:, c, a:b_] for c in range(8)], [xk] * 8, n, D,
                                       [A1[1][:, jm * 8 + c:jm * 8 + c + 1] for c in range(8)],
                                       [mod_ap(1, jm, 0, c) for c in range(8)],
                                       [hT[:, c, gi * 512 + a:gi * 512 + b_] for c in range(8)], ['hT1'] * 8,
                                       scale_keys=['A1_1', 'mod1'])
                    pg.barrier()
                with contextlib.ExitStack() as pb:
                    wn = [T(pb, "wn%d" % i, [128, 8, 384], BF16) for i in range(2)]
                    qz = [T(pb, "qz%d" % i, [128, TLC], BF16) for i in range(2)]
                    MEMSET('pool', qz[0][64:128, :], 0.0, W=['qz0'])
                    MEMSET('pool', qz[1][0:64, :], 0.0, W=['qz1'])
                    kT = T(pb, "kT", [128, TLC], BF16)
                    Vx = T(pb, "Vx", [128, NT, 2, 128], BF16)
                    tab = [T(pb, "tab%d" % i, [128, 2, 6 * 256], F32) for i in range(2)]
                    NSB = 6
                    SBANKS = [0, 1, 2, 3, 4, 7]
                    PT = [T(pb, "PTn%d" % i, [128, 256], BF16) for i in range(NSB)]
                    sb_ = [T(pb, "sbias%d" % i, [128, 256], F32) for i in range(NSB)]
                    rls = [T(pb, "rlN%d" % i, [128, 256], F32) for i in range(2)]
                    ystage = T(pb, "ystageN", [128, NOWN], BF16)
                    scale = 64.0 ** -0.5
                    MEMSET('pool', Vx[:, :, 0, 64:128], 1.0, W=['Vx'])
                    MEMSET('pool', Vx[:, :, 1, 0:64], 1.0, W=['Vx'])
                    tcnt = 0
                    for cp in range(8):
                        w = wn[cp % 2]
                        wk = 'wn%d' % (cp % 2)
                        DMA('pool', w[:], wview(wn_in[cp]), W=[wk])
                        for g in range(TLC // 512):
                            gs = slice(g * 512, (g + 1) * 512)
                            for which in (0, 1):
                                fb = (0, 1)[which]
                                for c in range(8):
                                    MM(B[fb][:, :512], lhsT=w[:, c, which * 128:(which + 1) * 128], rhs=hT[:, c, gs],
                                       start=(c == 0), stop=(c == 7), R=[wk, 'hT1'], W=[BK[fb]])
                                if which == 0:
                                    CP('dve', qz[0][0:64, gs], B[fb][0:64, :512], R=[BK[fb]], W=['qz0'])
                                    CP('dve', qz[1][64:128, gs], B[fb][64:128, :512], R=[BK[fb]], W=['qz1'])
                                else:
                                    CP('act', kT[:, gs], B[fb][:, :512], R=[BK[fb]], W=['kT'])
                            vb = (7, 2)[g % 2]
                            for ti in range(4):
                                tl = g * 4 + ti
                                for c in range(8):
                                    MM(B[vb][:, ti * 128:(ti + 1) * 128], lhsT=hT[:, c, tl * 128:(tl + 1) * 128],
                                       rhs=w[:, c, 256:384], start=(c == 0), stop=(c == 7), R=[wk, 'hT1'], W=[BK[vb]])
                            v4 = B[vb][:, :].rearrange("p (t e) -> p t e", e=128)
                            CP('dve', Vx[:, g * 4:(g + 1) * 4, 0, 0:64], v4[:, :, 0:64], R=[BK[vb]], W=['Vx'])
                            CP('act', Vx[:, g * 4:(g + 1) * 4, 1, 64:128], v4[:, :, 64:128], R=[BK[vb]], W=['Vx'])
                        for hh in range(2):
                            h = cp * 2 + hh
                            tb = tab[tcnt % 2]
                            tk = 'tab%d' % (tcnt % 2)
                            tcnt += 1
                            DMA('sp', tb[:], natab_in[h].rearrange("v p n -> p v n"), W=[tk])
                            ACT(tb[:], tb[:], AF.Exp, R=[tk], W=[tk])
                            voff = 0 if hh == 0 else 64
                            ooff = 64 - voff
                            hr = slice(hh * 64, (hh + 1) * 64)
                            items = []
                            for g in range(NOWN // 256):
                                kt0 = max(2 * g - 2, 0)
                                ktl = [(kt0 + i, i) for i in range(6)] + [(NT - 2, None), (NT - 1, None)]
                                for i, (kt, wi) in enumerate(ktl):
                                    items.append((g, i, kt, wi, len(ktl)))

                            def s_item(t, hh_=hh, tb=tb, tk=tk):
                                g, i, kt, wi, nk_ = items[t]
                                q0 = g * 256
                                var = 0 if g == 0 else 1
                                sbk = SBANKS[t % NSB]
                                pt = PT[t % NSB]
                                pk = 'PTn%d' % (t % NSB)
                                MM(B[sbk][:, :256], lhsT=kT[:, kt * 128:(kt + 1) * 128], rhs=qz[hh_][:, q0:q0 + 256],
                                   start=True, stop=True, R=['kT', 'qz%d' % hh_], W=[BK[sbk]])
                                if wi is not None:
                                    sbt = sb_[t % NSB]
                                    sk = 'sbias%d' % (t % NSB)
                                    ACT(sbt[:], B[sbk][:, :256], AF.Exp, scale=scale, R=[BK[sbk]], W=[sk])
                                    TT('pool' if (t % 3 == 0) else 'dve', pt[:], sbt[:],
                                       tb[:, var, wi * 256:(wi + 1) * 256], ALU.mult, R=[sk, tk], W=[pk])
                                else:
                                    ACT(pt[:], B[sbk][:, :256], AF.Exp, scale=scale, R=[BK[sbk]], W=[pk])

                            def pv_item(t, hh=hh, voff=voff, ooff=ooff):
                                g, i, kt, wi, nk_ = items[t]
                                q0 = g * 256
                                pt = PT[t % NSB]
                                pk = 'PTn%d' % (t % NSB)
                                ob = 5 + g % 2
                                MM(B[ob][:, :256], lhsT=Vx[:, kt, hh, :], rhs=pt[:], start=(i == 0),
                                   stop=(i == nk_ - 1), R=['Vx', pk], W=[BK[ob]])
                                if i == nk_ - 1:
                                    rl_ = rls[g % 2]
                                    rk = 'rlN%d' % (g % 2)

                                    def f1(rl_=rl_, rk=rk, ob=ob):
                                        CP('dve', rl_[voff:voff + 64, :], B[ob][ooff:ooff + 64, :256], R=[BK[ob]], W=[rk])

                                    def f2(rl_=rl_, rk=rk):
                                        ACT(rl_[voff:voff + 64, :], rl_[voff:voff + 64, :], AF.Ln, R=[rk], W=[rk])
                                        ACT(rl_[voff:voff + 64, :], rl_[voff:voff + 64, :], AF.Exp, scale=-1.0, R=[rk], W=[rk])

                                    def f3(rl_=rl_, rk=rk, ob=ob, q0=q0):
                                        TT('dve', ystage[voff:voff + 64, q0:q0 + 256], B[ob][voff:voff + 64, :256],
                                           rl_[voff:voff + 64, :], ALU.mult, R=[BK[ob], rk], W=['ystageN'])

                                    deferred.append((t + 2, f1))
                                    deferred.append((t + 4, f2))
                                    deferred.append((t + 7, f3))

                            deferred = []
                            LA = NSB - 1
                            for t in range(len(items) + LA + 8):
                                if t < len(items):
                                    s_item(t)
                                if 0 <= t - LA < len(items):
                                    pv_item(t - LA)
                                while deferred and deferred[0][0] <= t - LA:
                                    deferred.pop(0)[1]()
                            assert not deferred
                        DMA('sp', ys[cp][:, 0:NOWN], ystage[:], R=['ystageN'], W=['ys'])
                    pg.barrier()

        def run_all():
            phase_mod(0)
            if stop_after == 'mod0':
                return
            with contextlib.ExitStack() as l0:
                hT = T(l0, "hT", [128, 8, TLC], BF16)
                G = T(l0, "G", [128, TLC // 128, 16], F32)
                phase_A(l0, hT, G)
                if stop_after == 'A':
                    return
                phase_B(l0, hT, G)
            if stop_after == 'B':
                return
            phase_C()
            if stop_after == 'C':
                return
            phase_D(0)
            if stop_after == 'D0':
                return
            phase_mod(1)
            phase_N()
            if stop_after == 'N':
                return
            phase_D(1)

        run_all()
        if dbg:
            with contextlib.ExitStack() as pdg:
                buf = T(pdg, "dbgbuf", [128, 8, 512], F32)
                bufb = T(pdg, "dbgbufb", [128, 8, 512], BF16)
                src, isbf = dbg
                if src == 'mod':
                    DMA('sp', dbg_ap[0, :, 0:96], mod[0][:], W=['dbg'])
                    DMA('sp', dbg_ap[1, :, 0:16], A1[0][:], W=['dbg'])
                    DMA('sp', dbg_ap[2, :, 0:4 * 129], Crb[:].rearrange("p h e -> p (h e)"), W=['dbg'])
                srcap = {'xs': xs, 'ys': ys, 'mod': xs}[src]
                for gi in range(TLC // 512 if src != 'mod' else 0):
                    sl = slice(gi * 512, (gi + 1) * 512)
                    if isbf:
                        DMA('sp', bufb[:], srcap[:, :, sl].rearrange("c p t -> p c t"), W=['dbgbufb'])
                        CP('dve', buf[:], bufb[:], R=['dbgbufb'], W=['dbgbuf'])
                    else:
                        DMA('sp', buf[:], srcap[:, :, sl].rearrange("c p t -> p c t"), W=['dbgbuf'])
                    DMA('sp', dbg_ap[:, :, sl].rearrange("c p t -> p c t"), buf[:], R=['dbgbuf'], W=['dbg'])
        pg.barrier()
        pg.emit(block)
        nops = pg.nops
    return nc, nops


def _rope_tables(flip):
    j = np.arange(SEQ)
    p = (SEQ - 1 - j) if flip else j
    row = (p // GRID_W).astype(np.float32)
    col = (p % GRID_W).astype(np.float32)
    n_f = 8
    freqs = (np.float32(10000.0) ** (-np.arange(n_f, dtype=np.float32) / np.float32(n_f))).astype(np.float32)
    ang = np.concatenate([row[:, None] * freqs, col[:, None] * freqs], axis=-1).astype(np.float32)
    cos = np.cos(ang).astype(np.float32).T
    sin = np.sin(ang).astype(np.float32).T
    C = np.ones((32, NK), np.float32)
    S = np.zeros((32, NK), np.float32)
    C[0:16, :SEQ] = cos
    C[16:32, :SEQ] = cos
    S[0:16, :SEQ] = -sin
    S[16:32, :SEQ] = sin
    return C, S


def _na_tables(rel_bias, flip):
    H = rel_bias.shape[0]
    rows = SEQ // GRID_W
    out = np.full((H, 2, 128, 6, 256), NEG, np.float32)
    kk = np.arange(128)
    qq = np.arange(256)
    for var in range(2):
        g = 0 if var == 0 else 2
        kt0 = max(2 * g - 2, 0)
        for i in range(6):
            kj = (kt0 + i) * 128 + kk
            qj = g * 256 + qq
            kp = (SEQ - 1 - kj) if flip else kj
            qp = (SEQ - 1 - qj) if flip else qj
            kr, kc = kp // GRID_W, kp % GRID_W
            qr, qc = qp // GRID_W, qp % GRID_W
            rs = np.clip(qr - 4, 0, rows - 8)
            cs = np.clip(qc - 8, 0, GRID_W - 16)
            KR, QR = kr[:, None], qr[None, :]
            KC, QC = kc[:, None], qc[None, :]
            valid = (KR >= rs[None, :]) & (KR < rs[None, :] + 8) & (KC >= cs[None, :]) & (KC < cs[None, :] + 16)
            ri = np.clip(KR - QR + 7, 0, 14)
            ci = np.clip(KC - QC + 15, 0, 30)
            vals = rel_bias[:, ri, ci]
            out[:, var, :, i, :] = np.where(valid[None], vals, np.float32(NEG))
    return out.reshape(H, 2, 128, 6 * 256)


def _prep_inputs(inp):
    f = lambda a: np.ascontiguousarray(np.asarray(a, dtype=np.float32))
    x, c, ctx, c_ctx = f(inp["x"]), f(inp["c"]), f(inp["ctx"]), f(inp["c_ctx"])
    fm = lambda v: np.ascontiguousarray(v.reshape(-1, 128).T)
    shared = {}
    shared["ada_w"] = f(inp["ada_w"])
    shared["ada_b"] = np.stack([fm(f(inp["ada_b"])[l]) for l in range(2)])
    shared["n1g"] = np.stack([fm(f(inp["norm1_g"])[l]) for l in range(2)])
    shared["n2g"] = np.stack([fm(f(inp["norm2_g"])[l]) for l in range(2)])
    shared["fng"] = fm(f(inp["final_norm_g"]))
    shared["mlp_w1"] = f(inp["mlp_w1"])
    shared["mlp_w2"] = f(inp["mlp_w2"])
    w_in = f(inp["ab_w_in"])[0]
    mq, mk, mv, mo = w_in[:, 0:512], w_in[:, 512:1024], w_in[:, 1024:1536], w_in[:, 1536:2048]
    mg = w_in[:, 2048:2064]
    cq, ckv, kr = w_in[:, 2064:2320], w_in[:, 2320:2448], w_in[:, 2448:2480]
    shared["wm"] = np.ascontiguousarray(np.stack(
        [np.concatenate([a[:, h * 128:(h + 1) * 128] for a in (mq, mk, mv, mo)], axis=1) for h in range(4)]))
    perm = np.concatenate([np.arange(16, 32), np.arange(0, 16)])
    shared["wAfm"] = np.ascontiguousarray(np.concatenate([cq, ckv, kr, kr[:, perm]], axis=1))
    gate_b = f(inp["ab_gate_b"])[0]
    ordn = np.concatenate([np.arange(0, 4), np.arange(8, 12), np.arange(4, 8), np.arange(12, 16)])
    ordf = np.concatenate([np.arange(8, 12), np.arange(0, 4), np.arange(12, 16), np.arange(4, 8)])
    shared["mng"] = np.ascontiguousarray(np.tile(f(inp["ab_m_norm_g"])[0].reshape(1, 512), (128, 1)))
    shared["qng"] = fm(f(inp["ab_q_norm_g"])[0])
    shared["kvng"] = fm(f(inp["ab_kv_norm_g"])[0])
    wuq = f(inp["ab_w_uq"])[0]
    shared["wuq"] = wuq
    wuqB = wuq.copy().reshape(256, 8, 96)
    wuqB[:, :, 64:96] = wuqB[:, :, 64:96][:, :, perm]
    shared["wuqB"] = np.ascontiguousarray(wuqB.reshape(256, 768))
    shared["wukv"] = f(inp["ab_w_ukv"])[0]
    shared["wout0"] = f(inp["ab_w_out"])[0]
    nw = f(inp["na_w_in"])[0]
    shared["wn"] = np.ascontiguousarray(np.stack(
        [np.concatenate([nw[:, o + cp * 128:o + (cp + 1) * 128] for o in (0, 1024, 2048)], axis=1) for cp in range(8)]))
    shared["wout1"] = f(inp["na_w_out"])[0]
    rel_bias = f(inp["na_rel_bias"])[0]
    per_flip = {}
    for flip in (False, True):
        o = ordf if flip else ordn
        C, S = _rope_tables(flip)
        per_flip[flip] = {
            "wAtm": np.ascontiguousarray(np.concatenate([mk, mv, mg[:, o]], axis=1)),
            "gateb": np.ascontiguousarray(np.tile(gate_b[o].reshape(1, 16), (128, 4))),
            "ropeC": C, "ropeS": S,
            "natab": _na_tables(rel_bias, flip),
        }
    in_maps = []
    for core in range(8):
        b, s = core // 2, core % 2
        flip = (s == 1)
        xb = x[b][::-1] if flip else x[b]
        cb = ctx[b][::-1] if flip else ctx[b]
        m = dict(shared)
        m.update(per_flip[flip])
        m["xT"] = np.ascontiguousarray(xb.T).reshape(8, 128, SEQ)
        m["ctxT"] = np.ascontiguousarray(cb.T).reshape(8, 128, CTX)
        cc = np.stack([c[b], c_ctx], axis=0)
        m["scc"] = np.ascontiguousarray(cc.reshape(2, 8, 128).transpose(2, 1, 0).reshape(128, 16))
        in_maps.append(m)
    return in_maps


_CACHE = {}


def kernel(**inputs):
    in_maps = _prep_inputs(inputs)
    if "nc" not in _CACHE:
        _CACHE["nc"] = build_program()[0]
    nc = _CACHE["nc"]
    res = run_bass_kernel_spmd(nc, in_maps, core_ids=list(range(8)))
    out = np.empty((4, SEQ, D), np.float32)
    for core in range(8):
        b, s = core // 2, core % 2
        o = np.asarray(res.results[core]["out"]).reshape(D, NOWN).T
        if s == 0:
            out[b, 0:NOWN] = o
        else:
            out[b, SEQ - NOWN:SEQ] = o[::-1]
    return out
```

```python
import contextlib
import numpy as np
import concourse.bass as bass
import concourse.mybir as mybir
from concourse.bass_utils import run_bass_kernel_spmd

F32 = mybir.dt.float32
BF16 = mybir.dt.bfloat16
AF = mybir.ActivationFunctionType
ALU = mybir.AluOpType

D = 1024
SEQ = 8192
CTX = 256
TL = 4352
TLC = TL + CTX
NK = SEQ + CTX
NOWN = 4096
EPS = 1e-6
GRID_W = 64
NEG = -30000.0


class Prog:
    ENG = ('pe', 'act', 'dve', 'pool', 'sp')
    NDSEM = 6

    def __init__(self, nc, stack):
        self.nc = nc
        self.stream = {e: [] for e in self.ENG}
        self.csem = {e: stack.enter_context(nc.semaphore("c_" + e)) for e in ('pe', 'act', 'dve', 'pool')}
        self.ccnt = {e: 0 for e in self.csem}
        self.dsem = {q: [stack.enter_context(nc.semaphore("d_%s%d" % (q, i))) for i in range(self.NDSEM)]
                     for q in ('sp', 'pool', 'act')}
        self.dcnt = {q: [0] * self.NDSEM for q in self.dsem}
        self.dnext = {q: 0 for q in self.dsem}
        self.waited = {e: {} for e in self.ENG}
        self.lastw = {}
        self.readers = {}
        self.nops = 0
        self.pending = {e: False for e in self.ENG}
        self.excl = set(["b%d" % i for i in range(8)])

    def _need(self, eng, dep, waits):
        key, sem, val = dep
        if self.waited[eng].get(key, 0) >= val:
            return
        self.waited[eng][key] = val
        waits.append((sem, val))

    def _deps(self, eng, reads, writes):
        waits = []
        for r in reads:
            d = self.lastw.get(r)
            if d is not None:
                self._need(eng, d, waits)
        for w in writes:
            d = self.lastw.get(w)
            if d is not None:
                self._need(eng, d, waits)
            for d in self.readers.get(w, ()):
                self._need(eng, d, waits)
        return waits

    def _commit(self, dep, reads, writes):
        for r in reads:
            lst = self.readers.setdefault(r, [])
            lst.append(dep)
            if len(lst) > 64:
                best = {}
                for d in lst:
                    if d[0] not in best or best[d[0]][2] < d[2]:
                        best[d[0]] = d
                self.readers[r] = list(best.values())
        for w in writes:
            self.lastw[w] = dep
            self.readers[w] = []

    def op(self, eng, fn, reads=(), writes=(), inc=True):
        ex = [r for r in reads if r in self.excl]
        if ex:
            writes = list(writes) + ex
        waits = self._deps(eng, reads, writes)
        sem = self.csem[eng]
        if inc:
            self.ccnt[eng] += 1
            n = self.ccnt[eng]
            incv = (sem, 1)
            self.pending[eng] = False
        else:
            n = self.ccnt[eng] + 1
            incv = None
            self.pending[eng] = True
        dep = (('c', eng), sem, n)
        if eng == 'pe':
            self.waited[eng][('c', eng)] = n
        self.stream[eng].append((waits, fn, incv))
        self._commit(dep, reads, writes)
        self.nops += 1

    def dma(self, q, fn, reads=(), writes=()):
        waits = self._deps(q, reads, writes)
        s = self.dnext[q]
        self.dnext[q] = (s + 1) % self.NDSEM
        sem = self.dsem[q][s]
        prev = self.dcnt[q][s]
        if prev > 0:
            self._need(q, (('d', q, s), sem, 16 * prev), waits)
        self.dcnt[q][s] = prev + 1
        dep = (('d', q, s), sem, 16 * (prev + 1))
        self.stream[q].append((waits, fn, (sem, 16)))
        self._commit(dep, reads, writes)
        self.nops += 1

    def barrier(self):
        assert not any(self.pending.values()), self.pending
        deps = []
        for e in self.csem:
            if self.ccnt[e] > 0:
                deps.append((('c', e), self.csem[e], self.ccnt[e]))
        for q in self.dsem:
            for s in range(self.NDSEM):
                if self.dcnt[q][s] > 0:
                    deps.append((('d', q, s), self.dsem[q][s], 16 * self.dcnt[q][s]))
        for eng in self.ENG:
            waits = []
            for d in deps:
                if d[0] == ('c', eng) and eng == 'pe':
                    continue
                self._need(eng, d, waits)
            if waits:
                self.stream[eng].append((waits, None, None))
        self.lastw = {}
        self.readers = {}

    def emit(self, block):
        def run(engobj, lst):
            for waits, fn, inc in lst:
                for sem, val in waits:
                    engobj.wait_ge(sem, val)
                if fn is not None:
                    ins = fn(engobj)
                    if inc is not None:
                        ins.then_inc(inc[0], inc[1])

        @block.tensor
        def _(e):
            run(e, self.stream['pe'])

        @block.scalar
        def _(e):
            run(e, self.stream['act'])

        @block.vector
        def _(e):
            run(e, self.stream['dve'])

        @block.gpsimd
        def _(e):
            run(e, self.stream['pool'])

        @block.sync
        def _(e):
            run(e, self.stream['sp'])


def build_program(stop_after=None, dbg=False):
    nc = bass.Bass("TRN2", target_bir_lowering=False)

    def din(name, shape, dt=F32):
        return nc.dram_tensor(name, list(shape), dt, kind="ExternalInput").ap()

    def dscr(name, shape, dt):
        return nc.dram_tensor(name, list(shape), dt, kind="Internal").ap()

    xT_in = din("xT", [8, 128, SEQ])
    ctxT_in = din("ctxT", [8, 128, CTX])
    scc_in = din("scc", [128, 16])
    ropeC_in = din("ropeC", [32, NK])
    ropeS_in = din("ropeS", [32, NK])
    ada_w_in = din("ada_w", [2, D, 6 * D])
    ada_b_in = din("ada_b", [2, 128, 48])
    n1g_in = din("n1g", [2, 128, 8])
    n2g_in = din("n2g", [2, 128, 8])
    fng_in = din("fng", [128, 8])
    w1_in = din("mlp_w1", [2, D, 4 * D])
    w2_in = din("mlp_w2", [2, 4 * D, D])
    wm_in = din("wm", [4, D, 512])
    wAtm_in = din("wAtm", [D, 1040])
    wAfm_in = din("wAfm", [D, 448])
    gateb_in = din("gateb", [128, 64])
    mng_in = din("mng", [128, 512])
    qng_in = din("qng", [128, 2])
    kvng_in = din("kvng", [128, 1])
    wuq_in = din("wuq", [256, 768])
    wuqB_in = din("wuqB", [256, 768])
    wukv_in = din("wukv", [128, 1024])
    wout0_in = din("wout0", [D, D])
    wn_in = din("wn", [8, D, 384])
    natab_in = din("natab", [16, 2, 128, 6 * 256])
    wout1_in = din("wout1", [D, D])

    out_ap = nc.dram_tensor("out", [8, 128, NOWN], F32, kind="ExternalOutput").ap()
    dbg_ap = nc.dram_tensor("dbg", [8, 128, TLC], F32, kind="ExternalOutput").ap() if dbg else None

    xs = dscr("xs", [8, 128, TLC], F32)
    ys = dscr("ys", [8, 128, TLC], BF16)
    cqn_s = dscr("cqn_s", [2, 128, TLC], BF16)
    ckvn_s = dscr("ckvn_s", [128, NK], BF16)
    kr_s = dscr("kr_s", [32, NK], BF16)

    with contextlib.ExitStack() as st:
        _tn = [0]

        def T(stack, name, shape, dt):
            _tn[0] += 1
            return stack.enter_context(nc.sbuf_tensor("s%d_%s" % (_tn[0], name), list(shape), dt))

        B = [st.enter_context(nc.psum_tensor("b%d" % i, [128, 512], F32)) for i in range(8)]
        BK = ["b%d" % i for i in range(8)]

        identb = T(st, "identb", [128, 128], BF16)
        onesb = T(st, "onesb", [128, 128], BF16)
        onesf = T(st, "onesf", [128, 128], F32)
        trif = T(st, "trif", [128, 128], F32)
        trib = T(st, "trib", [128, 128], F32)
        epsc = T(st, "epsc", [128, 1], F32)
        onec = T(st, "onec", [128, 1], F32)
        mod = [T(st, "mod%d" % l, [128, 96], F32) for l in range(2)]
        A1 = [T(st, "A1_%d" % l, [128, 16], F32) for l in range(2)]
        A2 = [T(st, "A2_%d" % l, [128, 16], F32) for l in range(2)]
        fng = T(st, "fng", [128, 8], F32)
        Crb = T(st, "Crb", [128, 4, 129], F32)

        pg = Prog(nc, st)
        block = st.enter_context(nc.Block())

        def MM(out, lhsT, rhs, start=True, stop=True, R=(), W=()):
            pg.op('pe', lambda e: e.matmul(out, lhsT=lhsT, rhs=rhs, start=start, stop=stop), R, W, inc=bool(stop))

        def TR(out, in_, R=(), W=()):
            pg.op('pe', lambda e: e.transpose(out, in_, identb[:]), list(R) + ['identb'], W)

        def ACT(out, in_, func, bias=None, scale=None, accum=None, R=(), W=()):
            kw = {}
            if bias is not None:
                kw['bias'] = bias
            if scale is not None:
                kw['scale'] = scale
            if accum is not None:
                kw['accum_out'] = accum
            pg.op('act', lambda e: e.activation(out=out, in_=in_, func=func, **kw), R, W)

        def TT(eng, out, in0, in1, op, R=(), W=()):
            pg.op(eng, lambda e: e.tensor_tensor(out=out, in0=in0, in1=in1, op=op), R, W)

        def TS(eng, out, in0, s1, op0, s2=None, op1=None, R=(), W=()):
            if op1 is None:
                pg.op(eng, lambda e: e.tensor_scalar(out=out, in0=in0, scalar1=s1, scalar2=None, op0=op0), R, W)
            else:
                pg.op(eng, lambda e: e.tensor_scalar(out=out, in0=in0, scalar1=s1, scalar2=s2, op0=op0, op1=op1), R, W)

        def STT(eng, out, in0, scalar, in1, op0, op1, R=(), W=()):
            pg.op(eng, lambda e: e.scalar_tensor_tensor(out=out, in0=in0, scalar=scalar, in1=in1, op0=op0, op1=op1), R, W)

        def CP(eng, out, in_, R=(), W=()):
            if eng == 'act':
                pg.op('act', lambda e: e.copy(out=out, in_=in_), R, W)
            else:
                pg.op(eng, lambda e: e.tensor_copy(out=out, in_=in_), R, W)

        def RECIP(out, in_, R=(), W=()):
            pg.op('dve', lambda e: e.reciprocal(out=out, in_=in_), R, W)

        def MEMSET(eng, ap, val, W=()):
            pg.op(eng, lambda e: e.memset(ap, val), (), W)

        def DMA(q, out, in_, R=(), W=()):
            pg.dma(q, lambda e: e.dma_start(out=out, in_=in_), R, W)

        def wview(ap2d):
            return ap2d.rearrange("(kc p) n -> p kc n", p=128)

        MEMSET('pool', onesf[:], 1.0, W=['onesf'])
        MEMSET('pool', trif[:], 1.0, W=['trif'])
        MEMSET('pool', trib[:], 1.0, W=['trib'])
        MEMSET('pool', epsc[:], EPS, W=['epsc'])
        MEMSET('pool', onec[:], 1.0, W=['onec'])
        MEMSET('pool', Crb[:], 0.0, W=['Crb'])
        pg.op('pool', lambda e: e.affine_select(out=trif[:], in_=trif[:], pattern=[[1, 128]], compare_op=ALU.is_ge,
                                                fill=0.0, base=0, channel_multiplier=-1), ['trif'], ['trif'])
        pg.op('pool', lambda e: e.affine_select(out=trib[:], in_=trib[:], pattern=[[-1, 128]], compare_op=ALU.is_ge,
                                                fill=0.0, base=0, channel_multiplier=1), ['trib'], ['trib'])
        CP('dve', onesb[:], onesf[:], R=['onesf'], W=['onesb'])
        TT('dve', identb[:], trif[:], trib[:], ALU.mult, R=['trif', 'trib'], W=['identb'])
        DMA('sp', fng[:], fng_in, W=['fng'])

        def rmsnorm_fm(ph_bufs, srcs, src_keys, n, Dn, scales, shifts, outs, out_keys, scale_keys=()):
            sq, rstd, tmpf = ph_bufs
            nch = len(srcs)
            for c in range(nch):
                ACT(sq[:, c, :n], srcs[c], AF.Square, R=[src_keys[c]], W=['sq%d' % c])
            for c in range(nch):
                MM(B[0][:, :n], lhsT=onesb[:], rhs=sq[:, c, :n], start=(c == 0), stop=(c == nch - 1),
                   R=['sq%d' % c, 'onesb'], W=['b0'])
            ACT(rstd[:, :n], B[0][:, :n], AF.Sqrt, bias=epsc[:, 0:1], scale=1.0 / Dn, R=['b0', 'epsc'], W=['rstd'])
            RECIP(rstd[:, :n], rstd[:, :n], R=['rstd'], W=['rstd'])
            for c in range(nch):
                if shifts is None:
                    STT('dve', outs[c], srcs[c], scales[c], rstd[:, :n], ALU.mult, ALU.mult,
                        R=[src_keys[c], 'rstd'] + list(scale_keys), W=[out_keys[c]])
                else:
                    tk = 'tmpf%d' % (c % 2)
                    STT('dve', tmpf[c % 2][:, :n], srcs[c], scales[c], rstd[:, :n], ALU.mult, ALU.mult,
                        R=[src_keys[c], 'rstd'] + list(scale_keys), W=[tk])
                    ACT(outs[c], tmpf[c % 2][:, :n], AF.Identity, bias=shifts[c], R=[tk] + list(scale_keys),
                        W=[out_keys[c]])

        def phase_mod(l):
            with contextlib.ExitStack() as ph:
                wq = [T(ph, "adaq%d" % i, [128, 8, 1536], BF16) for i in range(2)]
                sccf = T(ph, "sccf", [128, 16], F32)
                sccb = T(ph, "sccb", [128, 16], BF16)
                adab = T(ph, "adab", [128, 48], F32)
                n1 = T(ph, "n1", [128, 8], F32)
                n2 = T(ph, "n2", [128, 8], F32)
                DMA('sp', sccf[:], scc_in, W=['sccf'])
                DMA('sp', adab[:], ada_b_in[l], W=['adab'])
                DMA('sp', n1[:], n1g_in[l], W=['n1'])
                DMA('sp', n2[:], n2g_in[l], W=['n2'])
                ACT(sccb[:], sccf[:], AF.Silu, R=['sccf'], W=['sccb'])
                for q in range(4):
                    buf = wq[q % 2]
                    key = "adaq%d" % (q % 2)
                    DMA('pool', buf[:], wview(ada_w_in[l][:, q * 1536:(q + 1) * 1536]), W=[key])
                    for f in range(12):
                        fc = q * 12 + f
                        for kc in range(8):
                            MM(B[1][:, fc:fc + 49:48], lhsT=buf[:, kc, f * 128:(f + 1) * 128],
                               rhs=sccb[:, 2 * kc:2 * kc + 2], start=(kc == 0), stop=(kc == 7),
                               R=[key, 'sccb'], W=['b1'])
                m = mod[l]
                mk = 'mod%d' % l
                for j in range(2):
                    TT('dve', m[:, j * 48:(j + 1) * 48], B[1][:, j * 48:(j + 1) * 48], adab[:], ALU.add,
                       R=['b1', 'adab'], W=[mk])
                for j in range(2):
                    STT('dve', A1[l][:, j * 8:(j + 1) * 8], m[:, j * 48 + 8:j * 48 + 16], 1.0, n1[:], ALU.add, ALU.mult,
                        R=[mk, 'n1'], W=['A1_%d' % l])
                    STT('dve', A2[l][:, j * 8:(j + 1) * 8], m[:, j * 48 + 32:j * 48 + 40], 1.0, n2[:], ALU.add, ALU.mult,
                        R=[mk, 'n2'], W=['A2_%d' % l])
                pg.barrier()

        def mod_ap(l, j, which, c):
            col = j * 48 + which * 8 + c
            return mod[l][:, col:col + 1]

        def derive_gates(G, gkey, nt, LF, EU, WW, DEC, tmpg, pfx):
            k = lambda s: pfx + s
            ACT(tmpg[:, :nt, :], G[:, :nt, 8:16], AF.Exp, scale=-1.0, R=[gkey], W=[k('tmpg')])
            ACT(tmpg[:, :nt, :], tmpg[:, :nt, :], AF.Ln, bias=onec[:, 0:1], R=[k('tmpg'), 'onec'], W=[k('tmpg')])
            TS('dve', LF[:, :nt, :], tmpg[:, :nt, :], -1.0, ALU.mult, R=[k('tmpg')], W=[k('LF')])
            for ti in range(nt):
                MM(B[1][:, ti * 8:ti * 8 + 4], lhsT=trif[:], rhs=LF[:, ti, 0:4], R=[k('LF'), 'trif'], W=['b1'])
                MM(B[1][:, ti * 8 + 4:ti * 8 + 8], lhsT=trib[:], rhs=LF[:, ti, 4:8], R=[k('LF'), 'trib'], W=['b1'])
                MM(B[0][:, ti * 8:ti * 8 + 8], lhsT=onesf[:], rhs=LF[:, ti, :], R=[k('LF'), 'onesf'], W=['b0'])
            bview = B[1][:, 0:nt * 8].rearrange("p (t e) -> p t e", e=8)
            tview = B[0][:, 0:nt * 8].rearrange("p (t e) -> p t e", e=8)
            TT('dve', tmpg[:, :nt, :], G[:, :nt, 0:8], bview, ALU.subtract, R=[gkey, 'b1'], W=[k('tmpg')])
            ACT(EU[:, :nt, :], tmpg[:, :nt, :], AF.Exp, R=[k('tmpg')], W=[k('EU')])
            TT('dve', tmpg[:, :nt, :], tmpg[:, :nt, :], tview, ALU.add, R=[k('tmpg'), 'b0'], W=[k('tmpg')])
            ACT(WW[:, :nt, :], tmpg[:, :nt, :], AF.Exp, R=[k('tmpg')], W=[k('WW')])
            ACT(DEC[:, :nt, :], tview, AF.Exp, R=['b0'], W=[k('DEC')])

        def phase_A(ph, hT, G):
            with contextlib.ExitStack() as pa:
                xg = [T(pa, "xg%d" % i, [128, 8, 512], F32) for i in range(2)]
                hgs = [T(pa, "hg%d" % i, [128, 8, 512], BF16) for i in range(2)]
                sq = T(pa, "sq", [128, 8, 512], BF16)
                rstd = T(pa, "rstd", [128, 512], F32)
                tmpf = [T(pa, "tmpf%d" % i, [128, 512], F32) for i in range(2)]
                wtm = T(pa, "wtm", [128, 8, 1040], BF16)
                wfm = T(pa, "wfm", [128, 8, 448], BF16)
                gateb = T(pa, "gateb", [128, 4, 16], F32)
                qng = T(pa, "qng", [128, 2], F32)
                kvng = T(pa, "kvng", [128, 1], F32)
                rc = T(pa, "rc", [128, 512], F32)
                rs = T(pa, "rs", [128, 512], F32)
                tA = T(pa, "tA", [128, 512], F32)
                tB = T(pa, "tB", [128, 512], F32)
                krst = T(pa, "krst", [128, 512], BF16)
                ckvst = T(pa, "ckvst", [128, 512], BF16)
                cqst = T(pa, "cqst", [128, 2, 512], BF16)
                Gr = T(pa, "Gr", [128, 4, 16], F32)
                LFr = T(pa, "LFr", [128, 4, 8], F32)
                EUr = T(pa, "EUr", [128, 4, 8], F32)
                WWr = T(pa, "WWr", [128, 4, 8], F32)
                DECr = T(pa, "DECr", [128, 4, 8], F32)
                tmpgr = T(pa, "tmpgr", [128, 4, 8], F32)
                mkt = T(pa, "mkt_a", [128, 512], BF16)
                Vw = T(pa, "Vw_a", [128, 4, 129], BF16)

                DMA('pool', wtm[:], wview(wAtm_in), W=['wtm'])
                DMA('pool', wfm[:], wview(wAfm_in), W=['wfm'])
                DMA('sp', gateb[:], gateb_in.rearrange("p (t e) -> p t e", e=16), W=['gateb'])
                DMA('sp', qng[:], qng_in, W=['qng'])
                DMA('sp', kvng[:], kvng_in, W=['kvng'])

                groups = []
                groups.append(('ctx', 0, CTX, TL, SEQ))
                j = SEQ
                first = True
                while j > TL:
                    n = 256 if first else 512
                    first = False
                    j -= n
                    groups.append(('rem', j, n, None, j))
                j = 0
                while j < TL:
                    n = min(512, TL - j)
                    groups.append(('loc', j, n, j, j))
                    j += n

                ginfo = {}

                def p1(gi):
                    kind, j0, n, loc0, key0 = groups[gi]
                    jm = 1 if kind == 'ctx' else 0
                    x = xg[gi % 2]
                    xk = 'xg%d' % (gi % 2)
                    src = ctxT_in if kind == 'ctx' else xT_in[:, :, j0:j0 + n]
                    DMA('sp', x[:, :, :n], src.rearrange("c p t -> p c t"), W=[xk])
                    if kind == 'rem':
                        hg_ = hgs[gi % 2]
                        hdst = [hg_[:, c, :n] for c in range(8)]
                        hk = ['hg%d' % (gi % 2)] * 8
                        hfull = lambda c, a, b_, hg_=hg_: hg_[:, c, a:b_]
                    else:
                        hdst = [hT[:, c, loc0:loc0 + n] for c in range(8)]
                        hk = ['hTg%d' % gi] * 8
                        hfull = lambda c, a, b_, _l=loc0: hT[:, c, _l + a:_l + b_]
                    rmsnorm_fm((sq, rstd, tmpf), [x[:, c, :n] for c in range(8)], [xk] * 8, n, D,
                               [A1[0][:, jm * 8 + c:jm * 8 + c + 1] for c in range(8)],
                               [mod_ap(0, jm, 0, c) for c in range(8)], hdst, hk, scale_keys=['A1_0', 'mod0'])
                    ginfo[gi] = (hdst, hk, hfull)

                p1(0)
                for gi, (kind, j0, n, loc0, key0) in enumerate(groups):
                    nt = n // 128
                    if gi + 1 < len(groups):
                        p1(gi + 1)
                    hdst, hk, hfull = ginfo.pop(gi)
                    hkey = hk[0]
                    for c in range(8):
                        MM(B[2][:, :n], lhsT=wfm[:, c, 256:384], rhs=hdst[c], start=(c == 0), stop=(c == 7),
                           R=['wfm', hkey], W=['b2'])
                    for c in range(8):
                        MM(B[3][0:96, :n], lhsT=wfm[:, c, 320:416], rhs=hdst[c], start=(c == 0), stop=(c == 7),
                           R=['wfm', hkey], W=['b3'])
                    for c in range(8):
                        MM(B[4][0:96, :n], lhsT=wfm[:, c, 352:448], rhs=hdst[c], start=(c == 0), stop=(c == 7),
                           R=['wfm', hkey], W=['b4'])
                    rmsnorm_fm((sq, rstd, tmpf), [B[2][:, :n]], ['b2'], n, 128, [kvng[:, 0:1]], None,
                               [ckvst[:, :n]], ['ckvst'], scale_keys=['kvng'])
                    DMA('sp', ckvn_s[:, key0:key0 + n], ckvst[:, :n], R=['ckvst'], W=['ckvn_s'])
                    DMA('sp', rc[64:96, :n], ropeC_in[:, key0:key0 + n], W=['rc'])
                    DMA('sp', rs[64:96, :n], ropeS_in[:, key0:key0 + n], W=['rs'])
                    TT('dve', tA[64:96, :n], B[3][64:96, :n], rc[64:96, :n], ALU.mult, R=['b3', 'rc'], W=['tA'])
                    TT('dve', tB[64:96, :n], B[4][64:96, :n], rs[64:96, :n], ALU.mult, R=['b4', 'rs'], W=['tB'])
                    TT('pool', krst[64:96, :n], tA[64:96, :n], tB[64:96, :n], ALU.add, R=['tA', 'tB'], W=['krst'])
                    DMA('sp', kr_s[:, key0:key0 + n], krst[64:96, :n], R=['krst'], W=['kr_s'])
                    if kind != 'rem':
                        for r in range(2):
                            for c in range(8):
                                MM(B[5 + r][:, :n], lhsT=wfm[:, c, r * 128:(r + 1) * 128], rhs=hdst[c],
                                   start=(c == 0), stop=(c == 7), R=['wfm', hkey], W=[BK[5 + r]])
                        rmsnorm_fm((sq, rstd, tmpf), [B[5][:, :n], B[6][:, :n]], ['b5', 'b6'], n, 256,
                                   [qng[:, 0:1], qng[:, 1:2]], None, [cqst[:, 0, :n], cqst[:, 1, :n]],
                                   ['cqst', 'cqst'], scale_keys=['qng'])
                        DMA('sp', cqn_s[:, :, loc0:loc0 + n].rearrange("c p t -> p c t"), cqst[:, :, :n],
                            R=['cqst'], W=['cqn_s'])
                    for ti in range(nt):
                        for c in range(8):
                            MM(B[1][:, ti * 16:(ti + 1) * 16], lhsT=hfull(c, ti * 128, (ti + 1) * 128),
                               rhs=wtm[:, c, 1024:1040], start=(c == 0), stop=(c == 7), R=['wtm', hkey], W=['b1'])
                    gsrc = B[1][:, 0:nt * 16].rearrange("p (t e) -> p t e", e=16)
                    if kind == 'rem':
                        Gd, gk = Gr, 'Gr'
                        TT('dve', Gr[:, :nt, :], gsrc, gateb[:, :nt, :], ALU.add, R=['b1', 'gateb'], W=['Gr'])
                    else:
                        t0 = loc0 // 128
                        Gd, gk = G[:, t0:t0 + nt, :], 'G'
                        TT('dve', Gd, gsrc, gateb[:, :nt, :], ALU.add, R=['b1', 'gateb'], W=['G'])
                    if kind in ('rem', 'ctx'):
                        derive_gates(Gd, gk, nt, LFr, EUr, WWr, DECr, tmpgr, 'r')
                        for ti in reversed(range(nt)):
                            for c in range(8):
                                MM(B[5][:, :512], lhsT=hfull(c, ti * 128, (ti + 1) * 128), rhs=wtm[:, c, 0:512],
                                   start=(c == 0), stop=(c == 7), R=['wtm', hkey], W=['b5'])
                            for c in range(8):
                                MM(B[6][:, :512], lhsT=hfull(c, ti * 128, (ti + 1) * 128), rhs=wtm[:, c, 512:1024],
                                   start=(c == 0), stop=(c == 7), R=['wtm', hkey], W=['b6'])
                            CP('act', mkt[:], B[5][:, :512], R=['b5'], W=['mkt_a'])
                            for hd in range(4):
                                TS('dve', Vw[:, hd, 0:128], B[6][:, hd * 128:(hd + 1) * 128],
                                   WWr[:, ti, 4 + hd:5 + hd], ALU.mult, R=['b6', 'rWW'], W=['Vw_a%d' % hd])
                                CP('dve', Vw[:, hd, 128:129], WWr[:, ti, 4 + hd:5 + hd], R=['rWW'], W=['Vw_a%d' % hd])
                            for hd in range(4):
                                ub = hd % 2
                                MM(B[ub][:, 0:129], lhsT=mkt[:, hd * 128:(hd + 1) * 128], rhs=Vw[:, hd, :],
                                   R=['mkt_a', 'Vw_a%d' % hd], W=[BK[ub]])
                                STT('dve', Crb[:, hd, :], Crb[:, hd, :], DECr[:, ti, 4 + hd:5 + hd], B[ub][:, 0:129],
                                    ALU.mult, ALU.add, R=['Crb', BK[ub], 'rDEC'], W=['Crb'])
                pg.barrier()

        def phase_B(ph, hT, G):
            NT = TLC // 128
            with contextlib.ExitStack() as pb:
                wm = [T(pb, "wm%d" % i, [128, 8, 512], BF16) for i in range(2)]
                mqT = T(pb, "mqT", [128, TLC], BF16)
                mkT = T(pb, "mkT", [128, TLC], BF16)
                mkt = T(pb, "mkt", [128, NT, 128], BF16)
                mvx = T(pb, "mvx", [128, NT, 129], BF16)
                gso = T(pb, "gso", [128, NT, 128], BF16)
                hacc = T(pb, "hacc", [128, NT, 128], F32)
                Cst = T(pb, "Cst", [128, 2, 129], F32)
                Cbf = T(pb, "Cbf", [128, 2, 2, 129], BF16)
                NS = 4
                trilf = [[T(pb, "trilf%d_%d" % (d, i), [128, 128], F32) for i in range(NS)] for d in range(2)]
                E1 = [[T(pb, "E1_%d_%d" % (d, i), [128, 128], F32) for i in range(NS)] for d in range(2)]
                DT = [[T(pb, "DT%d_%d" % (d, i), [128, 128], F32) for i in range(NS)] for d in range(2)]
                ST = [[T(pb, "ST%d_%d" % (d, i), [128, 128], BF16) for i in range(NS)] for d in range(2)]
                QsT = [[T(pb, "QsT%d_%d" % (d, i), [128, 128], BF16) for i in range(NS)] for d in range(2)]
                Vw = [[T(pb, "Vw%d_%d" % (d, i), [128, 129], BF16) for i in range(NS)] for d in range(2)]
                dtmp = [T(pb, "dtmp%d" % i, [128, 2], F32) for i in range(4)]
                sgt = [T(pb, "sgt%d" % i, [128, 128], F32) for i in range(2)]
                LF = T(pb, "LF", [128, NT, 8], F32)
                EU = T(pb, "EU", [128, NT, 8], F32)
                WW = T(pb, "WW", [128, NT, 8], F32)
                DEC = T(pb, "DEC", [128, NT, 8], F32)
                tmpg = T(pb, "tmpg", [128, NT, 8], F32)
                ssq = T(pb, "ssq", [128, NT], F32)
                rsq = T(pb, "rsq", [128, NT], F32)
                junk = T(pb, "junk", [128, 128], F32)
                mlat = [T(pb, "mlat%d" % i, [128, 128], BF16) for i in range(2)]
                ystage = T(pb, "ystage", [128, TLC], BF16)
                mngt = T(pb, "mngt", [128, 512], F32)

                DMA('sp', mngt[:], mng_in, W=['mngt'])
                MEMSET('pool', mvx[:, :, 128:129], 1.0, W=['mvx'])
                derive_gates(G, 'G', NT, LF, EU, WW, DEC, tmpg, 'l')

                def bufsel(d, k):
                    return "%d_%d" % (d, k % NS), k % NS

                def stage1(hd, d, tl, k):
                    gi = d * 4 + hd
                    tri = trif if d == 0 else trib
                    trik = 'trif' if d == 0 else 'trib'
                    S, s_ = bufsel(d, k)
                    pb_ = 0 if d == 0 else 7
                    c0 = (k % 2) * 128
                    ACT(trilf[d][s_][:], tri[:], AF.Copy, scale=LF[:, tl, gi:gi + 1], R=['lLF', trik], W=['trilf' + S])
                    MM(B[pb_][:, c0:c0 + 128], lhsT=onesf[:], rhs=trilf[d][s_][:], R=['onesf', 'trilf' + S], W=[BK[pb_]])

                def stage2(hd, d, tl, k):
                    gi = d * 4 + hd
                    sl = slice(tl * 128, (tl + 1) * 128)
                    S, s_ = bufsel(d, k)
                    pb_ = 0 if d == 0 else 7
                    c0 = (k % 2) * 128
                    ACT(E1[d][s_][:], B[pb_][:, c0:c0 + 128], AF.Exp, R=[BK[pb_]], W=['E1_' + S])
                    ACT(Vw[d][s_][:], mvx[:, tl, :], AF.Copy, scale=WW[:, tl, gi:gi + 1], R=['mvx', 'lWW'], W=['Vw' + S])
                    MM(B[pb_][:, 256 + c0:256 + c0 + 128], lhsT=mkT[:, sl], rhs=mqT[:, sl], R=['mkT', 'mqT'], W=[BK[pb_]])

                def stage3(hd, d, tl, k):
                    gi = d * 4 + hd
                    sl = slice(tl * 128, (tl + 1) * 128)
                    tri = trif if d == 0 else trib
                    trik = 'trif' if d == 0 else 'trib'
                    S, s_ = bufsel(d, k)
                    pb_ = 0 if d == 0 else 7
                    c0 = (k % 2) * 128
                    STT('dve', DT[d][s_][:], E1[d][s_][:], EU[:, tl, gi:gi + 1], tri[:], ALU.mult, ALU.mult,
                        R=['E1_' + S, 'lEU', trik], W=['DT' + S])
                    TT('dve', ST[d][s_][:], B[pb_][:, 256 + c0:256 + c0 + 128], DT[d][s_][:], ALU.mult,
                       R=[BK[pb_], 'DT' + S], W=['ST' + S])
                    TT('pool', QsT[d][s_][:], mqT[:, sl], E1[d][s_][:], ALU.mult, R=['mqT', 'E1_' + S], W=['QsT' + S])
                    ub = 5 + d
                    u0 = (k % 3) * 129
                    MM(B[ub][:, u0:u0 + 129], lhsT=mkt[:, tl, :], rhs=Vw[d][s_][:], R=['mkt', 'Vw' + S], W=[BK[ub]])

                def mchain(hd, d, tl, k):
                    gi = d * 4 + hd
                    S, s_ = bufsel(d, k)
                    D_ = str(d)
                    nb = 1 + 2 * d + (k % 2)
                    cin = 'Cbf%d_%d' % (d, k % 2)
                    cout = 'Cbf%d_%d' % (d, (k + 1) % 2)
                    MM(B[nb][:, 0:129], lhsT=ST[d][s_][:], rhs=mvx[:, tl, :], start=True, stop=False,
                       R=['ST' + S, 'mvx'], W=[BK[nb]])
                    MM(B[nb][:, 0:129], lhsT=QsT[d][s_][:], rhs=Cbf[:, d, k % 2, :], start=False, stop=True,
                       R=['QsT' + S, cin], W=[BK[nb]])
                    ub = 5 + d
                    u0 = (k % 3) * 129
                    STT('dve', Cst[:, d, :], Cst[:, d, :], DEC[:, tl, gi:gi + 1], B[ub][:, u0:u0 + 129],
                        ALU.mult, ALU.add, R=['Cst' + D_, BK[ub], 'lDEC'], W=['Cst' + D_])
                    CP('act', Cbf[:, d, (k + 1) % 2, :], Cst[:, d, :], R=['Cst' + D_], W=[cout])

                def mevac(hd, d, tl, k, first_write):
                    nb = 1 + 2 * d + (k % 2)
                    dk_ = 'dtmp%d_%d' % (d, k % 2)
                    dt_ = dtmp[d * 2 + (k % 2)]
                    ACT(dt_[:, 0:1], B[nb][:, 128:129], AF.Abs, R=[BK[nb]], W=[dk_])
                    TS('dve', dt_[:, 0:1], dt_[:, 0:1], 1.0, ALU.max, R=[dk_], W=[dk_])
                    RECIP(dt_[:, 1:2], dt_[:, 0:1], R=[dk_], W=[dk_])
                    hk = 'hacc%d' % tl
                    if first_write:
                        TS('dve', hacc[:, tl, :], B[nb][:, 0:128], dt_[:, 1:2], ALU.mult, R=[BK[nb], dk_], W=[hk])
                    else:
                        STT('dve', hacc[:, tl, :], B[nb][:, 0:128], dt_[:, 1:2], hacc[:, tl, :], ALU.mult, ALU.add,
                            R=[BK[nb], dk_, hk], W=[hk])

                for hd in range(4):
                    w = wm[hd % 2]
                    wk = 'wm%d' % (hd % 2)
                    DMA('pool', w[:], wview(wm_in[hd]), W=[wk])
                    for g in range(TLC // 512):
                        gs = slice(g * 512, (g + 1) * 512)
                        for which, dst, dk_, scl in ((0, mqT, 'mqT', 128.0 ** -0.5), (1, mkT, 'mkT', 1.0)):
                            fb = (6, 0)[which]
                            for c in range(8):
                                MM(B[fb][:, :512], lhsT=w[:, c, which * 128:(which + 1) * 128], rhs=hT[:, c, gs],
                                   start=(c == 0), stop=(c == 7), R=[wk, 'hT'], W=[BK[fb]])
                            if which == 0:
                                ACT(dst[:, gs], B[fb][:, :512], AF.Copy, scale=scl, R=[BK[fb]], W=[dk_])
                            else:
                                CP('dve', dst[:, gs], B[fb][:, :512], R=[BK[fb]], W=[dk_])
                        for ti in range(4):
                            tl = g * 4 + ti
                            tb_ = 1 + (tl % 4)
                            for c in range(8):
                                MM(B[tb_][:, 0:384], lhsT=hT[:, c, tl * 128:(tl + 1) * 128], rhs=w[:, c, 128:512],
                                   start=(c == 0), stop=(c == 7), R=[wk, 'hT'], W=[BK[tb_]])
                            CP('dve', mkt[:, tl, :], B[tb_][:, 0:128], R=[BK[tb_]], W=['mkt'])
                            CP('act', mvx[:, tl, 0:128], B[tb_][:, 128:256], R=[BK[tb_]], W=['mvx'])
                            ACT(sgt[tl % 2][:], B[tb_][:, 256:384], AF.Sigmoid, R=[BK[tb_]], W=['sgt%d' % (tl % 2)])
                            TT('pool', gso[:, tl, :], sgt[tl % 2][:], mngt[:, hd * 128:(hd + 1) * 128], ALU.mult,
                               R=['sgt%d' % (tl % 2), 'mngt'], W=['gso'])
                    MEMSET('pool', Cst[:], 0.0, W=['Cst0', 'Cst1'])
                    MEMSET('pool', Cbf[:], 0.0, W=['Cbf0_0', 'Cbf0_1', 'Cbf1_0', 'Cbf1_1'])
                    fseq = [34, 35] + list(range(34))
                    bseq = [35, 34] + list(reversed(range(34)))
                    seqs = (fseq, bseq)
                    written = set()
                    for t in range(-3, NT + 1):
                        if 0 <= t < NT:
                            j = t
                            for d in range(2):
                                mchain(hd, d, seqs[d][j], j)
                        if 0 <= t - 1 < NT:
                            for d in range(2):
                                tl = seqs[d][t - 1]
                                mevac(hd, d, tl, t - 1, tl not in written)
                                written.add(tl)
                        if 0 <= t < NT:
                            j = t
                            if j == 1:
                                CP('dve', Cst[:, 1, :], Crb[:, hd, :], R=['Crb'], W=['Cst1'])
                                CP('act', Cbf[:, 1, 0, :], Crb[:, hd, :], R=['Crb'], W=['Cbf1_0'])
                        for stg, off in ((stage3, 1), (stage2, 2), (stage1, 3)):
                            k = t + off
                            if 0 <= k < NT:
                                for d in range(2):
                                    stg(hd, d, seqs[d][k], k)
                    for tl in range(NT):
                        ACT(junk[:], hacc[:, tl, :], AF.Square, accum=ssq[:, tl:tl + 1], R=['hacc%d' % tl],
                            W=['junk', 'ssq'])
                    ACT(rsq[:], ssq[:], AF.Sqrt, bias=epsc[:, 0:1], scale=1.0 / 128.0, R=['ssq', 'epsc'], W=['rsq'])
                    RECIP(rsq[:], rsq[:], R=['rsq'], W=['rsq'])
                    for tl in range(NT):
                        s2 = tl % 2
                        STT('dve', mlat[s2][:], hacc[:, tl, :], rsq[:, tl:tl + 1], gso[:, tl, :], ALU.mult, ALU.mult,
                            R=['hacc%d' % tl, 'rsq', 'gso'], W=['mlat%d' % s2])
                        q4 = tl % 4
                        MM(B[7][:, q4 * 128:(q4 + 1) * 128], lhsT=mlat[s2][:], rhs=identb[:], R=['mlat%d' % s2, 'identb'],
                           W=['b7'])
                        if q4 == 3:
                            CP('act', ystage[:, (tl - 3) * 128:(tl + 1) * 128], B[7][:, 0:512], R=['b7'], W=['ystage'])
                    DMA('sp', ys[hd], ystage[:], R=['ystage'], W=['ys'])
                pg.barrier()

        def phase_C():
            NKT = NK // 128
            with contextlib.ExitStack() as pc:
                cqnT = T(pc, "cqnT", [128, 2, TLC], BF16)
                ckvnT = T(pc, "ckvnT", [128, NK], BF16)
                KT = T(pc, "KT", [128, NK], BF16)
                Vext = T(pc, "Vext", [128, NKT, 128], BF16)
                QT = [T(pc, "QT%d" % i, [128, 512], BF16) for i in range(2)]
                PT = [T(pc, "PT%d" % i, [128, 512], BF16) for i in range(3)]
                ystage = T(pc, "ystageC", [128, TLC], BF16)
                rc = T(pc, "rcC", [128, TLC], F32)
                rs = T(pc, "rsC", [128, TLC], F32)
                tA = T(pc, "tAC", [128, 512], F32)
                tB = T(pc, "tBC", [128, 512], F32)
                rl = T(pc, "rl", [128, 512], F32)
                wuq = T(pc, "wuq", [128, 2, 768], BF16)
                wuqB = T(pc, "wuqB", [128, 2, 768], BF16)
                wukv = T(pc, "wukv", [128, 1024], BF16)

                DMA('sp', cqnT[:], cqn_s.rearrange("c p t -> p c t"), W=['cqnT'])
                DMA('sp', ckvnT[:], ckvn_s, W=['ckvnT'])
                MEMSET('pool', KT[64:128, :], 0.0, W=['KTr', 'KTz'])
                DMA('sp', KT[64:96, :], kr_s, W=['KTr'])
                for i_ in range(2):
                    MEMSET('pool', QT[i_][64:128, :], 0.0, W=['QT%d' % i_])
                DMA('pool', wuq[:], wview(wuq_in), W=['wuq'])
                DMA('pool', wuqB[:], wview(wuqB_in), W=['wuqB'])
                DMA('pool', wukv[:], wukv_in, W=['wukv'])
                DMA('sp', rc[64:96, 0:TL], ropeC_in[:, 0:TL], W=['rcC'])
                DMA('sp', rc[64:96, TL:TLC], ropeC_in[:, SEQ:NK], W=['rcC'])
                DMA('sp', rs[64:96, 0:TL], ropeS_in[:, 0:TL], W=['rsC'])
                DMA('sp', rs[64:96, TL:TLC], ropeS_in[:, SEQ:NK], W=['rsC'])

                qgroups = [(g * 512, 512, list(range(NKT))) for g in range(8)]
                qgroups.append((4096, 256, list(range(NKT))))
                qgroups.append((TL, 256, [NKT - 2, NKT - 1]))
                a_scale = 96.0 ** -0.5
                cnt = 0
                for h in range(8):
                    voff = 0 if h % 2 == 0 else 64
                    ooff = 64 - voff
                    MEMSET('pool', Vext[:, :, ooff:ooff + 64], 1.0, W=['Vext'])
                    for kg in range((NK + 511) // 512):
                        k0 = kg * 512
                        n = min(512, NK - k0)
                        MM(B[0][0:64, :n], lhsT=wukv[:, h * 128:h * 128 + 64], rhs=ckvnT[:, k0:k0 + n],
                           R=['wukv', 'ckvnT'], W=['b0'])
                        CP('act' if kg % 2 else 'dve', KT[0:64, k0:k0 + n], B[0][0:64, :n], R=['b0'], W=['KTn'])
                        ntile = n // 128
                        for i in range(ntile):
                            kt = kg * 4 + i
                            MM(B[1][:, i * 64:(i + 1) * 64], lhsT=ckvnT[:, kt * 128:(kt + 1) * 128],
                               rhs=wukv[:, h * 128 + 64:h * 128 + 128], R=['wukv', 'ckvnT'], W=['b1'])
                        CP('dve' if kg % 2 else 'act', Vext[:, kg * 4:kg * 4 + ntile, voff:voff + 64],
                           B[1][:, 0:ntile * 64].rearrange("p (t e) -> p t e", e=64), R=['b1'], W=['Vext'])
                    items = []
                    for gidx, (q0, n, ktiles) in enumerate(qgroups):
                        for i, kt in enumerate(ktiles):
                            items.append((gidx, q0, n, i, kt, len(ktiles)))

                    def qprep(gidx, h=h):
                        q0, n, _ = qgroups[gidx]
                        qt = QT[gidx % 2]
                        qk = 'QT%d' % (gidx % 2)
                        for r in range(2):
                            MM(B[1][0:96, :n], lhsT=wuq[:, r, h * 96:(h + 1) * 96], rhs=cqnT[:, r, q0:q0 + n],
                               start=(r == 0), stop=(r == 1), R=['wuq', 'cqnT'], W=['b1'])
                        for r in range(2):
                            MM(B[2][0:96, :n], lhsT=wuqB[:, r, h * 96:(h + 1) * 96], rhs=cqnT[:, r, q0:q0 + n],
                               start=(r == 0), stop=(r == 1), R=['wuqB', 'cqnT'], W=['b2'])
                        CP('dve', qt[0:64, :n], B[1][0:64, :n], R=['b1'], W=[qk])
                        TT('dve', tA[64:96, :n], B[1][64:96, :n], rc[64:96, q0:q0 + n], ALU.mult, R=['b1', 'rcC'], W=['tAC'])
                        TT('dve', tB[64:96, :n], B[2][64:96, :n], rs[64:96, q0:q0 + n], ALU.mult, R=['b2', 'rsC'], W=['tBC'])
                        TT('pool', qt[64:96, :n], tA[64:96, :n], tB[64:96, :n], ALU.add, R=['tAC', 'tBC'], W=[qk])

                    def s_item(t):
                        gidx, q0, n, i, kt, nk_ = items[t]
                        qt = QT[gidx % 2]
                        qk = 'QT%d' % (gidx % 2)
                        sb = 3 + (t % 3)
                        pt = PT[t % 3]
                        pk = 'PT%d' % (t % 3)
                        MM(B[sb][:, :n], lhsT=KT[:, kt * 128:(kt + 1) * 128], rhs=qt[:, :n],
                           R=['KTn', 'KTr', 'KTz', qk], W=[BK[sb]])
                        ACT(pt[:, :n], B[sb][:, :n], AF.Exp, scale=a_scale, R=[BK[sb]], W=[pk])

                    def pv_item(t, voff=voff, ooff=ooff):
                        gidx, q0, n, i, kt, nk_ = items[t]
                        pt = PT[t % 3]
                        pk = 'PT%d' % (t % 3)
                        ob = 6 if gidx % 2 == 0 else 0
                        MM(B[ob][:, :n], lhsT=Vext[:, kt, :], rhs=pt[:, :n], start=(i == 0), stop=(i == nk_ - 1),
                           R=['Vext', pk], W=[BK[ob]])
                        if i == nk_ - 1:
                            CP('dve', rl[voff:voff + 64, :n], B[ob][ooff:ooff + 64, :n], R=[BK[ob]], W=['rl'])
                            RECIP(rl[voff:voff + 64, :n], rl[voff:voff + 64, :n], R=['rl'], W=['rl'])
                            TT('dve', ystage[voff:voff + 64, q0:q0 + n], B[ob][voff:voff + 64, :n],
                               rl[voff:voff + 64, :n], ALU.mult, R=[BK[ob], 'rl'], W=['ystageC'])

                    LA = 2
                    qprep(0)
                    for t in range(len(items) + LA):
                        if t < len(items):
                            s_item(t)
                            if items[t][3] == 0 and items[t][0] + 1 < len(qgroups):
                                qprep(items[t][0] + 1)
                        if t - LA >= 0:
                            pv_item(t - LA)
                    if h % 2 == 1:
                        DMA('sp', ys[4 + h // 2], ystage[:], R=['ystageC'], W=['ys'])
                pg.barrier()

        def phase_D(l):
            ngroups = (TLC // 256) if l == 0 else (NOWN // 256)
            with contextlib.ExitStack() as pd:
                wout = T(pd, "wout", [128, 8, 1024], BF16)
                W1 = T(pd, "W1", [128, 8, 4096], BF16)
                W2 = T(pd, "W2", [128, 32, 1024], BF16)
                xgs = [T(pd, "xgD%d" % i, [128, 8, 256], F32) for i in range(2)]
                ygs = [T(pd, "ygD%d" % i, [128, 8, 256], BF16) for i in range(2)]
                h2s = [T(pd, "h2_%d" % i, [128, 8, 256], BF16) for i in range(2)]
                hid = T(pd, "hid", [128, 32, 256], BF16)
                sq = T(pd, "sqD", [128, 8, 256], BF16)
                rstd = T(pd, "rstdD", [128, 256], F32)
                tmpf = [T(pd, "tmpfD%d" % i, [128, 256], F32) for i in range(2)]
                rt = [T(pd, "rt%d" % i, [128, 256], F32) for i in range(2)]
                DMA('pool', wout[:], wview(wout0_in if l == 0 else wout1_in), W=['wout'])
                for q in range(4):
                    DMA('pool', W1[:, :, q * 1024:(q + 1) * 1024], wview(w1_in[l][:, q * 1024:(q + 1) * 1024]),
                        W=['W1_%d' % q])
                for q in range(4):
                    DMA('pool', W2[:, q * 8:(q + 1) * 8, :], wview(w2_in[l][q * 1024:(q + 1) * 1024, :]), W=['W2_%d' % q])
                n = 256
                mk = 'mod%d' % l

                def load_group(gi):
                    t0 = gi * 256
                    jm = 1 if t0 >= TL else 0
                    if l == 0:
                        src = ctxT_in if jm else xT_in[:, :, t0:t0 + n]
                    else:
                        src = xs[:, :, t0:t0 + n]
                    DMA('sp', xgs[gi % 2][:], src.rearrange("c p t -> p c t"), R=['xs'] if l == 1 else [],
                        W=['xgD%d' % (gi % 2)])
                    DMA('sp', ygs[gi % 2][:], ys[:, :, t0:t0 + n].rearrange("c p t -> p c t"), R=['ys'],
                        W=['ygD%d' % (gi % 2)])

                def front(gi):
                    t0 = gi * 256
                    jm = 1 if t0 >= TL else 0
                    xg, yg = xgs[gi % 2], ygs[gi % 2]
                    XK, YK = 'xgD%d' % (gi % 2), 'ygD%d' % (gi % 2)
                    h2_ = h2s[gi % 2]
                    HK = 'h2_%d' % (gi % 2)
                    for dc in range(8):
                        ob = 3 + dc % 4
                        for c in range(8):
                            MM(B[ob][:, :n], lhsT=wout[:, c, dc * 128:(dc + 1) * 128], rhs=yg[:, c, :],
                               start=(c == 0), stop=(c == 7), R=['wout', YK], W=[BK[ob]])
                        STT('dve', xg[:, dc, :], B[ob][:, :n], mod_ap(l, jm, 2, dc), xg[:, dc, :], ALU.mult, ALU.add,
                            R=[BK[ob], mk, XK], W=[XK])
                    rmsnorm_fm((sq, rstd, tmpf), [xg[:, c, :] for c in range(8)], [XK] * 8, n, D,
                               [A2[l][:, jm * 8 + c:jm * 8 + c + 1] for c in range(8)],
                               [mod_ap(l, jm, 3, c) for c in range(8)], [h2_[:, c, :] for c in range(8)], [HK] * 8,
                               scale_keys=['A2_%d' % l, mk])

                def w1_stage(gi):
                    h2_ = h2s[gi % 2]
                    HK = 'h2_%d' % (gi % 2)
                    for fc in range(32):
                        hb = 3 + fc % 4
                        for c in range(8):
                            MM(B[hb][:, :n], lhsT=W1[:, c, fc * 128:(fc + 1) * 128], rhs=h2_[:, c, :],
                               start=(c == 0), stop=(c == 7), R=['W1_%d' % (fc // 8), HK], W=[BK[hb]])
                        r_ = rt[fc % 2]
                        rk = 'rt%d' % (fc % 2)
                        ACT(r_[:], B[hb][:, :n], AF.Relu, R=[BK[hb]], W=[rk])
                        TT('pool', hid[:, fc, :], r_[:], r_[:], ALU.mult, R=[rk], W=['hid'])

                def w2_stage(gi):
                    t0 = gi * 256
                    jm = 1 if t0 >= TL else 0
                    xg = xgs[gi % 2]
                    XK = 'xgD%d' % (gi % 2)
                    for dc in range(8):
                        ob = 1 + dc % 2
                        for fc in range(32):
                            MM(B[ob][:, :n], lhsT=W2[:, fc, dc * 128:(dc + 1) * 128], rhs=hid[:, fc, :],
                               start=(fc == 0), stop=(fc == 31), R=['W2_%d' % (fc // 8), 'hid'], W=[BK[ob]])
                        STT('dve', xg[:, dc, :], B[ob][:, :n], mod_ap(l, jm, 5, dc), xg[:, dc, :], ALU.mult, ALU.add,
                            R=[BK[ob], mk, XK], W=[XK])
                    if l == 0:
                        DMA('sp', xs[:, :, t0:t0 + n].rearrange("c p t -> p c t"), xg[:], R=[XK], W=['xs'])
                    else:
                        rmsnorm_fm((sq, rstd, tmpf), [xg[:, c, :] for c in range(8)], [XK] * 8, n, D,
                                   [fng[:, c:c + 1] for c in range(8)], None, [xg[:, c, :] for c in range(8)],
                                   [XK] * 8, scale_keys=['fng'])
                        DMA('sp', out_ap[:, :, t0:t0 + n].rearrange("c p t -> p c t"), xg[:], R=[XK], W=['out'])

                load_group(0)
                if ngroups > 1:
                    load_group(1)
                front(0)
                for gi in range(ngroups):
                    w1_stage(gi)
                    if gi + 1 < ngroups:
                        front(gi + 1)
                    w2_stage(gi)
                    if gi + 2 < ngroups:
                        load_group(gi + 2)
                pg.barrier()

        def phase_N():
            NT = TLC // 128
            with contextlib.ExitStack() as pn:
                hT = T(pn, "hT1", [128, 8, TLC], BF16)
                with contextlib.ExitStack() as pa:
                    xg = [T(pa, "xgN%d" % i, [128, 8, 512], F32) for i in range(2)]
                    sq = T(pa, "sqN", [128, 8, 512], BF16)
                    rstd = T(pa, "rstdN", [128, 512], F32)
                    tmpf = [T(pa, "tmpfN%d" % i, [128, 512], F32) for i in range(2)]
                    for gi in range(TLC // 512):
                        x = xg[gi % 2]
                        xk = 'xgN%d' % (gi % 2)
                        DMA('sp', x[:], xs[:, :, gi * 512:(gi + 1) * 512].rearrange("c p t -> p c t"), R=['xs'], W=[xk])
                        parts = [(0, 512, 0)] if gi < 8 else [(0, 256, 0), (256, 512, 1)]
                        for (a, b_, jm) in parts:
                            n = b_ - a
                            rmsnorm_fm((sq, rstd, tmpf), [x[:, c, a:b_] for c in range(8)], [xk] * 8, n, D,
                                       [A1[1][:, jm * 8 + c:jm * 8 + c + 1] for c in range(8)],
                                       [mod_ap(1, jm, 0, c) for c in range(8)],
                                       [hT[:, c, gi * 512 + a:gi * 512 + b_] for c in range(8)], ['hT1'] * 8,
                                       scale_keys=['A1_1', 'mod1'])
                    pg.barrier()
                with contextlib.ExitStack() as pb:
                    wn = [T(pb, "wn%d" % i, [128, 8, 384], BF16) for i in range(2)]
                    qz = [T(pb, "qz%d" % i, [128, TLC], BF16) for i in range(2)]
                    MEMSET('pool', qz[0][64:128, :], 0.0, W=['qz0'])
                    MEMSET('pool', qz[1][0:64, :], 0.0, W=['qz1'])
                    kT = T(pb, "kT", [128, TLC], BF16)
                    Vx = T(pb, "Vx", [128, NT, 2, 128], BF16)
                    tab = [T(pb, "tab%d" % i, [128, 2, 6 * 256], F32) for i in range(2)]
                    NSB = 6
                    SBANKS = [0, 1, 2, 3, 4, 7]
                    PT = [T(pb, "PTn%d" % i, [128, 256], BF16) for i in range(NSB)]
                    sb_ = [T(pb, "sbias%d" % i, [128, 256], F32) for i in range(NSB)]
                    rls = [T(pb, "rlN%d" % i, [128, 256], F32) for i in range(2)]
                    ystage = T(pb, "ystageN", [128, NOWN], BF16)
                    scale = 64.0 ** -0.5
                    MEMSET('pool', Vx[:, :, 0, 64:128], 1.0, W=['Vx'])
                    MEMSET('pool', Vx[:, :, 1, 0:64], 1.0, W=['Vx'])
                    tcnt = 0
                    for cp in range(8):
                        w = wn[cp % 2]
                        wk = 'wn%d' % (cp % 2)
                        DMA('pool', w[:], wview(wn_in[cp]), W=[wk])
                        for g in range(TLC // 512):
                            gs = slice(g * 512, (g + 1) * 512)
                            for which in (0, 1):
                                fb = (0, 1)[which]
                                for c in range(8):
                                    MM(B[fb][:, :512], lhsT=w[:, c, which * 128:(which + 1) * 128], rhs=hT[:, c, gs],
                                       start=(c == 0), stop=(c == 7), R=[wk, 'hT1'], W=[BK[fb]])
                                if which == 0:
                                    CP('dve', qz[0][0:64, gs], B[fb][0:64, :512], R=[BK[fb]], W=['qz0'])
                                    CP('dve', qz[1][64:128, gs], B[fb][64:128, :512], R=[BK[fb]], W=['qz1'])
                                else:
                                    CP('act', kT[:, gs], B[fb][:, :512], R=[BK[fb]], W=['kT'])
                            vb = (7, 2)[g % 2]
                            for ti in range(4):
                                tl = g * 4 + ti
                                for c in range(8):
                                    MM(B[vb][:, ti * 128:(ti + 1) * 128], lhsT=hT[:, c, tl * 128:(tl + 1) * 128],
                                       rhs=w[:, c, 256:384], start=(c == 0), stop=(c == 7), R=[wk, 'hT1'], W=[BK[vb]])
                            v4 = B[vb][:, :].rearrange("p (t e) -> p t e", e=128)
                            CP('dve', Vx[:, g * 4:(g + 1) * 4, 0, 0:64], v4[:, :, 0:64], R=[BK[vb]], W=['Vx'])
                            CP('act', Vx[:, g * 4:(g + 1) * 4, 1, 64:128], v4[:, :, 64:128], R=[BK[vb]], W=['Vx'])
                        for hh in range(2):
                            h = cp * 2 + hh
                            tb = tab[tcnt % 2]
                            tk = 'tab%d' % (tcnt % 2)
                            tcnt += 1
                            DMA('sp', tb[:], natab_in[h].rearrange("v p n -> p v n"), W=[tk])
                            voff = 0 if hh == 0 else 64
                            ooff = 64 - voff
                            hr = slice(hh * 64, (hh + 1) * 64)
                            items = []
                            for g in range(NOWN // 256):
                                kt0 = max(2 * g - 2, 0)
                                ktl = [(kt0 + i, i) for i in range(6)] + [(NT - 2, None), (NT - 1, None)]
                                for i, (kt, wi) in enumerate(ktl):
                                    items.append((g, i, kt, wi, len(ktl)))

                            def s_item(t, hh_=hh, tb=tb, tk=tk):
                                g, i, kt, wi, nk_ = items[t]
                                q0 = g * 256
                                var = 0 if g == 0 else 1
                                sbk = SBANKS[t % NSB]
                                pt = PT[t % NSB]
                                pk = 'PTn%d' % (t % NSB)
                                MM(B[sbk][:, :256], lhsT=kT[:, kt * 128:(kt + 1) * 128], rhs=qz[hh_][:, q0:q0 + 256],
                                   start=True, stop=True, R=['kT', 'qz%d' % hh_], W=[BK[sbk]])
                                if wi is not None:
                                    sbt = sb_[t % NSB]
                                    sk = 'sbias%d' % (t % NSB)
                                    STT('dve', sbt[:], B[sbk][:, :256], scale, tb[:, var, wi * 256:(wi + 1) * 256],
                                        ALU.mult, ALU.add, R=[BK[sbk], tk], W=[sk])
                                    ACT(pt[:], sbt[:], AF.Exp, R=[sk], W=[pk])
                                else:
                                    ACT(pt[:], B[sbk][:, :256], AF.Exp, scale=scale, R=[BK[sbk]], W=[pk])

                            def pv_item(t, hh=hh, voff=voff, ooff=ooff):
                                g, i, kt, wi, nk_ = items[t]
                                q0 = g * 256
                                pt = PT[t % NSB]
                                pk = 'PTn%d' % (t % NSB)
                                ob = 5 + g % 2
                                MM(B[ob][:, :256], lhsT=Vx[:, kt, hh, :], rhs=pt[:], start=(i == 0),
                                   stop=(i == nk_ - 1), R=['Vx', pk], W=[BK[ob]])
                                if i == nk_ - 1:
                                    rl_ = rls[g % 2]
                                    rk = 'rlN%d' % (g % 2)

                                    def f1(rl_=rl_, rk=rk, ob=ob):
                                        CP('dve', rl_[voff:voff + 64, :], B[ob][ooff:ooff + 64, :256], R=[BK[ob]], W=[rk])

                                    def f2(rl_=rl_, rk=rk):
                                        ACT(rl_[voff:voff + 64, :], rl_[voff:voff + 64, :], AF.Ln, R=[rk], W=[rk])
                                        ACT(rl_[voff:voff + 64, :], rl_[voff:voff + 64, :], AF.Exp, scale=-1.0, R=[rk], W=[rk])

                                    def f3(rl_=rl_, rk=rk, ob=ob, q0=q0):
                                        TT('dve', ystage[voff:voff + 64, q0:q0 + 256], B[ob][voff:voff + 64, :256],
                                           rl_[voff:voff + 64, :], ALU.mult, R=[BK[ob], rk], W=['ystageN'])

                                    deferred.append((t + 2, f1))
                                    deferred.append((t + 4, f2))
                                    deferred.append((t + 7, f3))

                            deferred = []
                            LA = NSB - 1
                            for t in range(len(items) + LA + 8):
                                if t < len(items):
                                    s_item(t)
                                if 0 <= t - LA < len(items):
                                    pv_item(t - LA)
                                while deferred and deferred[0][0] <= t - LA:
                                    deferred.pop(0)[1]()
                            assert not deferred
                        DMA('sp', ys[cp][:, 0:NOWN], ystage[:], R=['ystageN'], W=['ys'])
                    pg.barrier()

        def run_all():
            phase_mod(0)
            if stop_after == 'mod0':
                return
            with contextlib.ExitStack() as l0:
                hT = T(l0, "hT", [128, 8, TLC], BF16)
                G = T(l0, "G", [128, TLC // 128, 16], F32)
                phase_A(l0, hT, G)
                if stop_after == 'A':
                    return
                phase_B(l0, hT, G)
            if stop_after == 'B':
                return
            phase_C()
            if stop_after == 'C':
                return
            phase_D(0)
            if stop_after == 'D0':
                return
            phase_mod(1)
            phase_N()
            if stop_after == 'N':
                return
            phase_D(1)

        run_all()
        if dbg:
            with contextlib.ExitStack() as pdg:
                buf = T(pdg, "dbgbuf", [128, 8, 512], F32)
                bufb = T(pdg, "dbgbufb", [128, 8, 512], BF16)
                src, isbf = dbg
                if src == 'mod':
                    DMA('sp', dbg_ap[0, :, 0:96], mod[0][:], W=['dbg'])
                    DMA('sp', dbg_ap[1, :, 0:16], A1[0][:], W=['dbg'])
                    DMA('sp', dbg_ap[2, :, 0:4 * 129], Crb[:].rearrange("p h e -> p (h e)"), W=['dbg'])
                srcap = {'xs': xs, 'ys': ys, 'mod': xs}[src]
                for gi in range(TLC // 512 if src != 'mod' else 0):
                    sl = slice(gi * 512, (gi + 1) * 512)
                    if isbf:
                        DMA('sp', bufb[:], srcap[:, :, sl].rearrange("c p t -> p c t"), W=['dbgbufb'])
                        CP('dve', buf[:], bufb[:], R=['dbgbufb'], W=['dbgbuf'])
                    else:
                        DMA('sp', buf[:], srcap[:, :, sl].rearrange("c p t -> p c t"), W=['dbgbuf'])
                    DMA('sp', dbg_ap[:, :, sl].rearrange("c p t -> p c t"), buf[:], R=['dbgbuf'], W=['dbg'])
        pg.barrier()
        pg.emit(block)
        nops = pg.nops
    return nc, nops


def _rope_tables(flip):
    j = np.arange(SEQ)
    p = (SEQ - 1 - j) if flip else j
    row = (p // GRID_W).astype(np.float32)
    col = (p % GRID_W).astype(np.float32)
    n_f = 8
    freqs = (np.float32(10000.0) ** (-np.arange(n_f, dtype=np.float32) / np.float32(n_f))).astype(np.float32)
    ang = np.concatenate([row[:, None] * freqs, col[:, None] * freqs], axis=-1).astype(np.float32)
    cos = np.cos(ang).astype(np.float32).T
    sin = np.sin(ang).astype(np.float32).T
    C = np.ones((32, NK), np.float32)
    S = np.zeros((32, NK), np.float32)
    C[0:16, :SEQ] = cos
    C[16:32, :SEQ] = cos
    S[0:16, :SEQ] = -sin
    S[16:32, :SEQ] = sin
    return C, S


def _na_tables(rel_bias, flip):
    H = rel_bias.shape[0]
    rows = SEQ // GRID_W
    out = np.full((H, 2, 128, 6, 256), NEG, np.float32)
    kk = np.arange(128)
    qq = np.arange(256)
    for var in range(2):
        g = 0 if var == 0 else 2
        kt0 = max(2 * g - 2, 0)
        for i in range(6):
            kj = (kt0 + i) * 128 + kk
            qj = g * 256 + qq
            kp = (SEQ - 1 - kj) if flip else kj
            qp = (SEQ - 1 - qj) if flip else qj
            kr, kc = kp // GRID_W, kp % GRID_W
            qr, qc = qp // GRID_W, qp % GRID_W
            rs = np.clip(qr - 4, 0, rows - 8)
            cs = np.clip(qc - 8, 0, GRID_W - 16)
            KR, QR = kr[:, None], qr[None, :]
            KC, QC = kc[:, None], qc[None, :]
            valid = (KR >= rs[None, :]) & (KR < rs[None, :] + 8) & (KC >= cs[None, :]) & (KC < cs[None, :] + 16)
            ri = np.clip(KR - QR + 7, 0, 14)
            ci = np.clip(KC - QC + 15, 0, 30)
            vals = rel_bias[:, ri, ci]
            out[:, var, :, i, :] = np.where(valid[None], vals, np.float32(NEG))
    return out.reshape(H, 2, 128, 6 * 256)


def _prep_inputs(inp):
    f = lambda a: np.ascontiguousarray(np.asarray(a, dtype=np.float32))
    x, c, ctx, c_ctx = f(inp["x"]), f(inp["c"]), f(inp["ctx"]), f(inp["c_ctx"])
    fm = lambda v: np.ascontiguousarray(v.reshape(-1, 128).T)
    shared = {}
    shared["ada_w"] = f(inp["ada_w"])
    shared["ada_b"] = np.stack([fm(f(inp["ada_b"])[l]) for l in range(2)])
    shared["n1g"] = np.stack([fm(f(inp["norm1_g"])[l]) for l in range(2)])
    shared["n2g"] = np.stack([fm(f(inp["norm2_g"])[l]) for l in range(2)])
    shared["fng"] = fm(f(inp["final_norm_g"]))
    shared["mlp_w1"] = f(inp["mlp_w1"])
    shared["mlp_w2"] = f(inp["mlp_w2"])
    w_in = f(inp["ab_w_in"])[0]
    mq, mk, mv, mo = w_in[:, 0:512], w_in[:, 512:1024], w_in[:, 1024:1536], w_in[:, 1536:2048]
    mg = w_in[:, 2048:2064]
    cq, ckv, kr = w_in[:, 2064:2320], w_in[:, 2320:2448], w_in[:, 2448:2480]
    shared["wm"] = np.ascontiguousarray(np.stack(
        [np.concatenate([a[:, h * 128:(h + 1) * 128] for a in (mq, mk, mv, mo)], axis=1) for h in range(4)]))
    perm = np.concatenate([np.arange(16, 32), np.arange(0, 16)])
    shared["wAfm"] = np.ascontiguousarray(np.concatenate([cq, ckv, kr, kr[:, perm]], axis=1))
    gate_b = f(inp["ab_gate_b"])[0]
    ordn = np.concatenate([np.arange(0, 4), np.arange(8, 12), np.arange(4, 8), np.arange(12, 16)])
    ordf = np.concatenate([np.arange(8, 12), np.arange(0, 4), np.arange(12, 16), np.arange(4, 8)])
    shared["mng"] = np.ascontiguousarray(np.tile(f(inp["ab_m_norm_g"])[0].reshape(1, 512), (128, 1)))
    shared["qng"] = fm(f(inp["ab_q_norm_g"])[0])
    shared["kvng"] = fm(f(inp["ab_kv_norm_g"])[0])
    wuq = f(inp["ab_w_uq"])[0]
    shared["wuq"] = wuq
    wuqB = wuq.copy().reshape(256, 8, 96)
    wuqB[:, :, 64:96] = wuqB[:, :, 64:96][:, :, perm]
    shared["wuqB"] = np.ascontiguousarray(wuqB.reshape(256, 768))
    shared["wukv"] = f(inp["ab_w_ukv"])[0]
    shared["wout0"] = f(inp["ab_w_out"])[0]
    nw = f(inp["na_w_in"])[0]
    shared["wn"] = np.ascontiguousarray(np.stack(
        [np.concatenate([nw[:, o + cp * 128:o + (cp + 1) * 128] for o in (0, 1024, 2048)], axis=1) for cp in range(8)]))
    shared["wout1"] = f(inp["na_w_out"])[0]
    rel_bias = f(inp["na_rel_bias"])[0]
    per_flip = {}
    for flip in (False, True):
        o = ordf if flip else ordn
        C, S = _rope_tables(flip)
        per_flip[flip] = {
            "wAtm": np.ascontiguousarray(np.concatenate([mk, mv, mg[:, o]], axis=1)),
            "gateb": np.ascontiguousarray(np.tile(gate_b[o].reshape(1, 16), (128, 4))),
            "ropeC": C, "ropeS": S,
            "natab": _na_tables(rel_bias, flip),
        }
    in_maps = []
    for core in range(8):
        b, s = core // 2, core % 2
        flip = (s == 1)
        xb = x[b][::-1] if flip else x[b]
        cb = ctx[b][::-1] if flip else ctx[b]
        m = dict(shared)
        m.update(per_flip[flip])
        m["xT"] = np.ascontiguousarray(xb.T).reshape(8, 128, SEQ)
        m["ctxT"] = np.ascontiguousarray(cb.T).reshape(8, 128, CTX)
        cc = np.stack([c[b], c_ctx], axis=0)
        m["scc"] = np.ascontiguousarray(cc.reshape(2, 8, 128).transpose(2, 1, 0).reshape(128, 16))
        in_maps.append(m)
    return in_maps


_CACHE = {}


def kernel(**inputs):
    in_maps = _prep_inputs(inputs)
    if "nc" not in _CACHE:
        _CACHE["nc"] = build_program()[0]
    nc = _CACHE["nc"]
    res = run_bass_kernel_spmd(nc, in_maps, core_ids=list(range(8)))
    out = np.empty((4, SEQ, D), np.float32)
    for core in range(8):
        b, s = core // 2, core % 2
        o = np.asarray(res.results[core]["out"]).reshape(D, NOWN).T
        if s == 0:
            out[b, 0:NOWN] = o
        else:
            out[b, SEQ - NOWN:SEQ] = o[::-1]
    return out
```

```python
import contextlib
import numpy as np
import concourse.bass as bass
import concourse.mybir as mybir
from concourse.bass_utils import run_bass_kernel_spmd

F32 = mybir.dt.float32
BF16 = mybir.dt.bfloat16
AF = mybir.ActivationFunctionType
ALU = mybir.AluOpType

D = 1024
SEQ = 8192
CTX = 256
TL = 4352
TLC = TL + CTX
NK = SEQ + CTX
NOWN = 4096
EPS = 1e-6
GRID_W = 64
NEG = -30000.0


class Prog:
    ENG = ('pe', 'act', 'dve', 'pool', 'sp')
    NDSEM = 6

    def __init__(self, nc, stack):
        self.nc = nc
        self.stream = {e: [] for e in self.ENG}
        self.csem = {e: stack.enter_context(nc.semaphore("c_" + e)) for e in ('pe', 'act', 'dve', 'pool')}
        self.ccnt = {e: 0 for e in self.csem}
        self.dsem = {q: [stack.enter_context(nc.semaphore("d_%s%d" % (q, i))) for i in range(self.NDSEM)]
                     for q in ('sp', 'pool', 'act')}
        self.dcnt = {q: [0] * self.NDSEM for q in self.dsem}
        self.dnext = {q: 0 for q in self.dsem}
        self.waited = {e: {} for e in self.ENG}
        self.lastw = {}
        self.readers = {}
        self.nops = 0
        self.pending = {e: False for e in self.ENG}
        self.excl = set(["b%d" % i for i in range(8)])

    def _need(self, eng, dep, waits):
        key, sem, val = dep
        if self.waited[eng].get(key, 0) >= val:
            return
        self.waited[eng][key] = val
        waits.append((sem, val))

    def _deps(self, eng, reads, writes):
        waits = []
        for r in reads:
            d = self.lastw.get(r)
            if d is not None:
                self._need(eng, d, waits)
        for w in writes:
            d = self.lastw.get(w)
            if d is not None:
                self._need(eng, d, waits)
            for d in self.readers.get(w, ()):
                self._need(eng, d, waits)
        return waits

    def _commit(self, dep, reads, writes):
        for r in reads:
            lst = self.readers.setdefault(r, [])
            lst.append(dep)
            if len(lst) > 64:
                best = {}
                for d in lst:
                    if d[0] not in best or best[d[0]][2] < d[2]:
                        best[d[0]] = d
                self.readers[r] = list(best.values())
        for w in writes:
            self.lastw[w] = dep
            self.readers[w] = []

    def op(self, eng, fn, reads=(), writes=(), inc=True):
        ex = [r for r in reads if r in self.excl]
        if ex:
            writes = list(writes) + ex
        waits = self._deps(eng, reads, writes)
        sem = self.csem[eng]
        if inc:
            self.ccnt[eng] += 1
            n = self.ccnt[eng]
            incv = (sem, 1)
            self.pending[eng] = False
        else:
            n = self.ccnt[eng] + 1
            incv = None
            self.pending[eng] = True
        dep = (('c', eng), sem, n)
        if eng == 'pe':
            self.waited[eng][('c', eng)] = n
        self.stream[eng].append((waits, fn, incv))
        self._commit(dep, reads, writes)
        self.nops += 1

    def dma(self, q, fn, reads=(), writes=()):
        waits = self._deps(q, reads, writes)
        s = self.dnext[q]
        self.dnext[q] = (s + 1) % self.NDSEM
        sem = self.dsem[q][s]
        prev = self.dcnt[q][s]
        if prev > 0:
            self._need(q, (('d', q, s), sem, 16 * prev), waits)
        self.dcnt[q][s] = prev + 1
        dep = (('d', q, s), sem, 16 * (prev + 1))
        self.stream[q].append((waits, fn, (sem, 16)))
        self._commit(dep, reads, writes)
        self.nops += 1

    def barrier(self):
        assert not any(self.pending.values()), self.pending
        deps = []
        for e in self.csem:
            if self.ccnt[e] > 0:
                deps.append((('c', e), self.csem[e], self.ccnt[e]))
        for q in self.dsem:
            for s in range(self.NDSEM):
                if self.dcnt[q][s] > 0:
                    deps.append((('d', q, s), self.dsem[q][s], 16 * self.dcnt[q][s]))
        for eng in self.ENG:
            waits = []
            for d in deps:
                if d[0] == ('c', eng) and eng == 'pe':
                    continue
                self._need(eng, d, waits)
            if waits:
                self.stream[eng].append((waits, None, None))
        self.lastw = {}
        self.readers = {}

    def emit(self, block):
        def run(engobj, lst):
            for waits, fn, inc in lst:
                for sem, val in waits:
                    engobj.wait_ge(sem, val)
                if fn is not None:
                    ins = fn(engobj)
                    if inc is not None:
                        ins.then_inc(inc[0], inc[1])

        @block.tensor
        def _(e):
            run(e, self.stream['pe'])

        @block.scalar
        def _(e):
            run(e, self.stream['act'])

        @block.vector
        def _(e):
            run(e, self.stream['dve'])

        @block.gpsimd
        def _(e):
            run(e, self.stream['pool'])

        @block.sync
        def _(e):
            run(e, self.stream['sp'])


def build_program(stop_after=None, dbg=False):
    nc = bass.Bass("TRN2", target_bir_lowering=False)

    def din(name, shape, dt=F32):
        return nc.dram_tensor(name, list(shape), dt, kind="ExternalInput").ap()

    def dscr(name, shape, dt):
        return nc.dram_tensor(name, list(shape), dt, kind="Internal").ap()

    xT_in = din("xT", [8, 128, SEQ])
    ctxT_in = din("ctxT", [8, 128, CTX])
    scc_in = din("scc", [128, 16])
    ropeC_in = din("ropeC", [32, NK])
    ropeS_in = din("ropeS", [32, NK])
    ada_w_in = din("ada_w", [2, D, 6 * D])
    ada_b_in = din("ada_b", [2, 128, 48])
    n1g_in = din("n1g", [2, 128, 8])
    n2g_in = din("n2g", [2, 128, 8])
    fng_in = din("fng", [128, 8])
    w1_in = din("mlp_w1", [2, D, 4 * D])
    w2_in = din("mlp_w2", [2, 4 * D, D])
    wm_in = din("wm", [4, D, 512])
    wAtm_in = din("wAtm", [D, 1040])
    wAfm_in = din("wAfm", [D, 448])
    gateb_in = din("gateb", [128, 64])
    mng_in = din("mng", [128, 512])
    qng_in = din("qng", [128, 2])
    kvng_in = din("kvng", [128, 1])
    wuq_in = din("wuq", [256, 768])
    wuqB_in = din("wuqB", [256, 768])
    wukv_in = din("wukv", [128, 1024])
    wout0_in = din("wout0", [D, D])
    wn_in = din("wn", [8, D, 384])
    natab_in = din("natab", [16, 2, 128, 6 * 256])
    wout1_in = din("wout1", [D, D])

    out_ap = nc.dram_tensor("out", [8, 128, NOWN], F32, kind="ExternalOutput").ap()
    dbg_ap = nc.dram_tensor("dbg", [8, 128, TLC], F32, kind="ExternalOutput").ap() if dbg else None

    xs = dscr("xs", [8, 128, TLC], F32)
    ys = dscr("ys", [8, 128, TLC], BF16)
    cqn_s = dscr("cqn_s", [2, 128, TLC], BF16)
    ckvn_s = dscr("ckvn_s", [128, NK], BF16)
    kr_s = dscr("kr_s", [32, NK], BF16)

    with contextlib.ExitStack() as st:
        _tn = [0]

        def T(stack, name, shape, dt):
            _tn[0] += 1
            return stack.enter_context(nc.sbuf_tensor("s%d_%s" % (_tn[0], name), list(shape), dt))

        B = [st.enter_context(nc.psum_tensor("b%d" % i, [128, 512], F32)) for i in range(8)]
        BK = ["b%d" % i for i in range(8)]

        identb = T(st, "identb", [128, 128], BF16)
        onesb = T(st, "onesb", [128, 128], BF16)
        onesf = T(st, "onesf", [128, 128], F32)
        trif = T(st, "trif", [128, 128], F32)
        trib = T(st, "trib", [128, 128], F32)
        epsc = T(st, "epsc", [128, 1], F32)
        onec = T(st, "onec", [128, 1], F32)
        mod = [T(st, "mod%d" % l, [128, 96], F32) for l in range(2)]
        A1 = [T(st, "A1_%d" % l, [128, 16], F32) for l in range(2)]
        A2 = [T(st, "A2_%d" % l, [128, 16], F32) for l in range(2)]
        fng = T(st, "fng", [128, 8], F32)
        Crb = T(st, "Crb", [128, 4, 129], F32)

        pg = Prog(nc, st)
        block = st.enter_context(nc.Block())

        def MM(out, lhsT, rhs, start=True, stop=True, R=(), W=()):
            pg.op('pe', lambda e: e.matmul(out, lhsT=lhsT, rhs=rhs, start=start, stop=stop), R, W, inc=bool(stop))

        def TR(out, in_, R=(), W=()):
            pg.op('pe', lambda e: e.transpose(out, in_, identb[:]), list(R) + ['identb'], W)

        def ACT(out, in_, func, bias=None, scale=None, accum=None, R=(), W=()):
            kw = {}
            if bias is not None:
                kw['bias'] = bias
            if scale is not None:
                kw['scale'] = scale
            if accum is not None:
                kw['accum_out'] = accum
            pg.op('act', lambda e: e.activation(out=out, in_=in_, func=func, **kw), R, W)

        def TT(eng, out, in0, in1, op, R=(), W=()):
            pg.op(eng, lambda e: e.tensor_tensor(out=out, in0=in0, in1=in1, op=op), R, W)

        def TS(eng, out, in0, s1, op0, s2=None, op1=None, R=(), W=()):
            if op1 is None:
                pg.op(eng, lambda e: e.tensor_scalar(out=out, in0=in0, scalar1=s1, scalar2=None, op0=op0), R, W)
            else:
                pg.op(eng, lambda e: e.tensor_scalar(out=out, in0=in0, scalar1=s1, scalar2=s2, op0=op0, op1=op1), R, W)

        def STT(eng, out, in0, scalar, in1, op0, op1, R=(), W=()):
            pg.op(eng, lambda e: e.scalar_tensor_tensor(out=out, in0=in0, scalar=scalar, in1=in1, op0=op0, op1=op1), R, W)

        def CP(eng, out, in_, R=(), W=()):
            if eng == 'act':
                pg.op('act', lambda e: e.copy(out=out, in_=in_), R, W)
            else:
                pg.op(eng, lambda e: e.tensor_copy(out=out, in_=in_), R, W)

        def RECIP(out, in_, R=(), W=()):
            pg.op('dve', lambda e: e.reciprocal(out=out, in_=in_), R, W)

        def MEMSET(eng, ap, val, W=()):
            pg.op(eng, lambda e: e.memset(ap, val), (), W)

        def DMA(q, out, in_, R=(), W=()):
            pg.dma(q, lambda e: e.dma_start(out=out, in_=in_), R, W)

        def wview(ap2d):
            return ap2d.rearrange("(kc p) n -> p kc n", p=128)

        MEMSET('pool', onesf[:], 1.0, W=['onesf'])
        MEMSET('pool', trif[:], 1.0, W=['trif'])
        MEMSET('pool', trib[:], 1.0, W=['trib'])
        MEMSET('pool', epsc[:], EPS, W=['epsc'])
        MEMSET('pool', onec[:], 1.0, W=['onec'])
        MEMSET('pool', Crb[:], 0.0, W=['Crb'])
        pg.op('pool', lambda e: e.affine_select(out=trif[:], in_=trif[:], pattern=[[1, 128]], compare_op=ALU.is_ge,
                                                fill=0.0, base=0, channel_multiplier=-1), ['trif'], ['trif'])
        pg.op('pool', lambda e: e.affine_select(out=trib[:], in_=trib[:], pattern=[[-1, 128]], compare_op=ALU.is_ge,
                                                fill=0.0, base=0, channel_multiplier=1), ['trib'], ['trib'])
        CP('dve', onesb[:], onesf[:], R=['onesf'], W=['onesb'])
        TT('dve', identb[:], trif[:], trib[:], ALU.mult, R=['trif', 'trib'], W=['identb'])
        DMA('sp', fng[:], fng_in, W=['fng'])

        def rmsnorm_fm(ph_bufs, srcs, src_keys, n, Dn, scales, shifts, outs, out_keys, scale_keys=()):
            sq, rstd, tmpf = ph_bufs
            nch = len(srcs)
            for c in range(nch):
                ACT(sq[:, c, :n], srcs[c], AF.Square, R=[src_keys[c]], W=['sq%d' % c])
            for c in range(nch):
                MM(B[0][:, :n], lhsT=onesb[:], rhs=sq[:, c, :n], start=(c == 0), stop=(c == nch - 1),
                   R=['sq%d' % c, 'onesb'], W=['b0'])
            ACT(rstd[:, :n], B[0][:, :n], AF.Sqrt, bias=epsc[:, 0:1], scale=1.0 / Dn, R=['b0', 'epsc'], W=['rstd'])
            RECIP(rstd[:, :n], rstd[:, :n], R=['rstd'], W=['rstd'])
            for c in range(nch):
                if shifts is None:
                    STT('dve', outs[c], srcs[c], scales[c], rstd[:, :n], ALU.mult, ALU.mult,
                        R=[src_keys[c], 'rstd'] + list(scale_keys), W=[out_keys[c]])
                else:
                    tk = 'tmpf%d' % (c % 2)
                    STT('dve', tmpf[c % 2][:, :n], srcs[c], scales[c], rstd[:, :n], ALU.mult, ALU.mult,
                        R=[src_keys[c], 'rstd'] + list(scale_keys), W=[tk])
                    ACT(outs[c], tmpf[c % 2][:, :n], AF.Identity, bias=shifts[c], R=[tk] + list(scale_keys),
                        W=[out_keys[c]])

        def phase_mod(l):
            with contextlib.ExitStack() as ph:
                wq = [T(ph, "adaq%d" % i, [128, 8, 1536], BF16) for i in range(2)]
                sccf = T(ph, "sccf", [128, 16], F32)
                sccb = T(ph, "sccb", [128, 16], BF16)
                adab = T(ph, "adab", [128, 48], F32)
                n1 = T(ph, "n1", [128, 8], F32)
                n2 = T(ph, "n2", [128, 8], F32)
                DMA('sp', sccf[:], scc_in, W=['sccf'])
                DMA('sp', adab[:], ada_b_in[l], W=['adab'])
                DMA('sp', n1[:], n1g_in[l], W=['n1'])
                DMA('sp', n2[:], n2g_in[l], W=['n2'])
                ACT(sccb[:], sccf[:], AF.Silu, R=['sccf'], W=['sccb'])
                for q in range(4):
                    buf = wq[q % 2]
                    key = "adaq%d" % (q % 2)
                    DMA('pool', buf[:], wview(ada_w_in[l][:, q * 1536:(q + 1) * 1536]), W=[key])
                    for f in range(12):
                        fc = q * 12 + f
                        for kc in range(8):
                            MM(B[1][:, fc:fc + 49:48], lhsT=buf[:, kc, f * 128:(f + 1) * 128],
                               rhs=sccb[:, 2 * kc:2 * kc + 2], start=(kc == 0), stop=(kc == 7),
                               R=[key, 'sccb'], W=['b1'])
                m = mod[l]
                mk = 'mod%d' % l
                for j in range(2):
                    TT('dve', m[:, j * 48:(j + 1) * 48], B[1][:, j * 48:(j + 1) * 48], adab[:], ALU.add,
                       R=['b1', 'adab'], W=[mk])
                for j in range(2):
                    STT('dve', A1[l][:, j * 8:(j + 1) * 8], m[:, j * 48 + 8:j * 48 + 16], 1.0, n1[:], ALU.add, ALU.mult,
                        R=[mk, 'n1'], W=['A1_%d' % l])
                    STT('dve', A2[l][:, j * 8:(j + 1) * 8], m[:, j * 48 + 32:j * 48 + 40], 1.0, n2[:], ALU.add, ALU.mult,
                        R=[mk, 'n2'], W=['A2_%d' % l])
                pg.barrier()

        def mod_ap(l, j, which, c):
            col = j * 48 + which * 8 + c
            return mod[l][:, col:col + 1]

        def derive_gates(G, gkey, nt, LF, EU, WW, DEC, tmpg, pfx):
            k = lambda s: pfx + s
            ACT(tmpg[:, :nt, :], G[:, :nt, 8:16], AF.Exp, scale=-1.0, R=[gkey], W=[k('tmpg')])
            ACT(tmpg[:, :nt, :], tmpg[:, :nt, :], AF.Ln, bias=onec[:, 0:1], R=[k('tmpg'), 'onec'], W=[k('tmpg')])
            TS('dve', LF[:, :nt, :], tmpg[:, :nt, :], -1.0, ALU.mult, R=[k('tmpg')], W=[k('LF')])
            for ti in range(nt):
                MM(B[1][:, ti * 8:ti * 8 + 4], lhsT=trif[:], rhs=LF[:, ti, 0:4], R=[k('LF'), 'trif'], W=['b1'])
                MM(B[1][:, ti * 8 + 4:ti * 8 + 8], lhsT=trib[:], rhs=LF[:, ti, 4:8], R=[k('LF'), 'trib'], W=['b1'])
                MM(B[0][:, ti * 8:ti * 8 + 8], lhsT=onesf[:], rhs=LF[:, ti, :], R=[k('LF'), 'onesf'], W=['b0'])
            bview = B[1][:, 0:nt * 8].rearrange("p (t e) -> p t e", e=8)
            tview = B[0][:, 0:nt * 8].rearrange("p (t e) -> p t e", e=8)
            TT('dve', tmpg[:, :nt, :], G[:, :nt, 0:8], bview, ALU.subtract, R=[gkey, 'b1'], W=[k('tmpg')])
            ACT(EU[:, :nt, :], tmpg[:, :nt, :], AF.Exp, R=[k('tmpg')], W=[k('EU')])
            TT('dve', tmpg[:, :nt, :], tmpg[:, :nt, :], tview, ALU.add, R=[k('tmpg'), 'b0'], W=[k('tmpg')])
            ACT(WW[:, :nt, :], tmpg[:, :nt, :], AF.Exp, R=[k('tmpg')], W=[k('WW')])
            ACT(DEC[:, :nt, :], tview, AF.Exp, R=['b0'], W=[k('DEC')])

        def phase_A(ph, hT, G):
            with contextlib.ExitStack() as pa:
                xg = [T(pa, "xg%d" % i, [128, 8, 512], F32) for i in range(2)]
                hgs = [T(pa, "hg%d" % i, [128, 8, 512], BF16) for i in range(2)]
                sq = T(pa, "sq", [128, 8, 512], BF16)
                rstd = T(pa, "rstd", [128, 512], F32)
                tmpf = [T(pa, "tmpf%d" % i, [128, 512], F32) for i in range(2)]
                wtm = T(pa, "wtm", [128, 8, 1040], BF16)
                wfm = T(pa, "wfm", [128, 8, 448], BF16)
                gateb = T(pa, "gateb", [128, 4, 16], F32)
                qng = T(pa, "qng", [128, 2], F32)
                kvng = T(pa, "kvng", [128, 1], F32)
                rc = T(pa, "rc", [128, 512], F32)
                rs = T(pa, "rs", [128, 512], F32)
                tA = T(pa, "tA", [128, 512], F32)
                tB = T(pa, "tB", [128, 512], F32)
                krst = T(pa, "krst", [128, 512], BF16)
                ckvst = T(pa, "ckvst", [128, 512], BF16)
                cqst = T(pa, "cqst", [128, 2, 512], BF16)
                Gr = T(pa, "Gr", [128, 4, 16], F32)
                LFr = T(pa, "LFr", [128, 4, 8], F32)
                EUr = T(pa, "EUr", [128, 4, 8], F32)
                WWr = T(pa, "WWr", [128, 4, 8], F32)
                DECr = T(pa, "DECr", [128, 4, 8], F32)
                tmpgr = T(pa, "tmpgr", [128, 4, 8], F32)
                mkt = T(pa, "mkt_a", [128, 512], BF16)
                Vw = T(pa, "Vw_a", [128, 4, 129], BF16)

                DMA('pool', wtm[:], wview(wAtm_in), W=['wtm'])
                DMA('pool', wfm[:], wview(wAfm_in), W=['wfm'])
                DMA('sp', gateb[:], gateb_in.rearrange("p (t e) -> p t e", e=16), W=['gateb'])
                DMA('sp', qng[:], qng_in, W=['qng'])
                DMA('sp', kvng[:], kvng_in, W=['kvng'])

                groups = []
                groups.append(('ctx', 0, CTX, TL, SEQ))
                j = SEQ
                first = True
                while j > TL:
                    n = 256 if first else 512
                    first = False
                    j -= n
                    groups.append(('rem', j, n, None, j))
                j = 0
                while j < TL:
                    n = min(512, TL - j)
                    groups.append(('loc', j, n, j, j))
                    j += n

                ginfo = {}

                def p1(gi):
                    kind, j0, n, loc0, key0 = groups[gi]
                    jm = 1 if kind == 'ctx' else 0
                    x = xg[gi % 2]
                    xk = 'xg%d' % (gi % 2)
                    src = ctxT_in if kind == 'ctx' else xT_in[:, :, j0:j0 + n]
                    DMA('sp', x[:, :, :n], src.rearrange("c p t -> p c t"), W=[xk])
                    if kind == 'rem':
                        hg_ = hgs[gi % 2]
                        hdst = [hg_[:, c, :n] for c in range(8)]
                        hk = ['hg%d' % (gi % 2)] * 8
                        hfull = lambda c, a, b_, hg_=hg_: hg_[:, c, a:b_]
                    else:
                        hdst = [hT[:, c, loc0:loc0 + n] for c in range(8)]
                        hk = ['hTg%d' % gi] * 8
                        hfull = lambda c, a, b_, _l=loc0: hT[:, c, _l + a:_l + b_]
                    rmsnorm_fm((sq, rstd, tmpf), [x[:, c, :n] for c in range(8)], [xk] * 8, n, D,
                               [A1[0][:, jm * 8 + c:jm * 8 + c + 1] for c in range(8)],
                               [mod_ap(0, jm, 0, c) for c in range(8)], hdst, hk, scale_keys=['A1_0', 'mod0'])
                    ginfo[gi] = (hdst, hk, hfull)

                p1(0)
                for gi, (kind, j0, n, loc0, key0) in enumerate(groups):
                    nt = n // 128
                    if gi + 1 < len(groups):
                        p1(gi + 1)
                    hdst, hk, hfull = ginfo.pop(gi)
                    hkey = hk[0]
                    for c in range(8):
                        MM(B[2][:, :n], lhsT=wfm[:, c, 256:384], rhs=hdst[c], start=(c == 0), stop=(c == 7),
                           R=['wfm', hkey], W=['b2'])
                    for c in range(8):
                        MM(B[3][0:96, :n], lhsT=wfm[:, c, 320:416], rhs=hdst[c], start=(c == 0), stop=(c == 7),
                           R=['wfm', hkey], W=['b3'])
                    for c in range(8):
                        MM(B[4][0:96, :n], lhsT=wfm[:, c, 352:448], rhs=hdst[c], start=(c == 0), stop=(c == 7),
                           R=['wfm', hkey], W=['b4'])
                    rmsnorm_fm((sq, rstd, tmpf), [B[2][:, :n]], ['b2'], n, 128, [kvng[:, 0:1]], None,
                               [ckvst[:, :n]], ['ckvst'], scale_keys=['kvng'])
                    DMA('sp', ckvn_s[:, key0:key0 + n], ckvst[:, :n], R=['ckvst'], W=['ckvn_s'])
                    DMA('sp', rc[64:96, :n], ropeC_in[:, key0:key0 + n], W=['rc'])
                    DMA('sp', rs[64:96, :n], ropeS_in[:, key0:key0 + n], W=['rs'])
                    TT('dve', tA[64:96, :n], B[3][64:96, :n], rc[64:96, :n], ALU.mult, R=['b3', 'rc'], W=['tA'])
                    TT('dve', tB[64:96, :n], B[4][64:96, :n], rs[64:96, :n], ALU.mult, R=['b4', 'rs'], W=['tB'])
                    TT('pool', krst[64:96, :n], tA[64:96, :n], tB[64:96, :n], ALU.add, R=['tA', 'tB'], W=['krst'])
                    DMA('sp', kr_s[:, key0:key0 + n], krst[64:96, :n], R=['krst'], W=['kr_s'])
                    if kind != 'rem':
                        for r in range(2):
                            for c in range(8):
                                MM(B[5 + r][:, :n], lhsT=wfm[:, c, r * 128:(r + 1) * 128], rhs=hdst[c],
                                   start=(c == 0), stop=(c == 7), R=['wfm', hkey], W=[BK[5 + r]])
                        rmsnorm_fm((sq, rstd, tmpf), [B[5][:, :n], B[6][:, :n]], ['b5', 'b6'], n, 256,
                                   [qng[:, 0:1], qng[:, 1:2]], None, [cqst[:, 0, :n], cqst[:, 1, :n]],
                                   ['cqst', 'cqst'], scale_keys=['qng'])
                        DMA('sp', cqn_s[:, :, loc0:loc0 + n].rearrange("c p t -> p c t"), cqst[:, :, :n],
                            R=['cqst'], W=['cqn_s'])
                    for ti in range(nt):
                        for c in range(8):
                            MM(B[1][:, ti * 16:(ti + 1) * 16], lhsT=hfull(c, ti * 128, (ti + 1) * 128),
                               rhs=wtm[:, c, 1024:1040], start=(c == 0), stop=(c == 7), R=['wtm', hkey], W=['b1'])
                    gsrc = B[1][:, 0:nt * 16].rearrange("p (t e) -> p t e", e=16)
                    if kind == 'rem':
                        Gd, gk = Gr, 'Gr'
                        TT('dve', Gr[:, :nt, :], gsrc, gateb[:, :nt, :], ALU.add, R=['b1', 'gateb'], W=['Gr'])
                    else:
                        t0 = loc0 // 128
                        Gd, gk = G[:, t0:t0 + nt, :], 'G'
                        TT('dve', Gd, gsrc, gateb[:, :nt, :], ALU.add, R=['b1', 'gateb'], W=['G'])
                    if kind in ('rem', 'ctx'):
                        derive_gates(Gd, gk, nt, LFr, EUr, WWr, DECr, tmpgr, 'r')
                        for ti in reversed(range(nt)):
                            for c in range(8):
                                MM(B[5][:, :512], lhsT=hfull(c, ti * 128, (ti + 1) * 128), rhs=wtm[:, c, 0:512],
                                   start=(c == 0), stop=(c == 7), R=['wtm', hkey], W=['b5'])
                            for c in range(8):
                                MM(B[6][:, :512], lhsT=hfull(c, ti * 128, (ti + 1) * 128), rhs=wtm[:, c, 512:1024],
                                   start=(c == 0), stop=(c == 7), R=['wtm', hkey], W=['b6'])
                            CP('act', mkt[:], B[5][:, :512], R=['b5'], W=['mkt_a'])
                            for hd in range(4):
                                TS('dve', Vw[:, hd, 0:128], B[6][:, hd * 128:(hd + 1) * 128],
                                   WWr[:, ti, 4 + hd:5 + hd], ALU.mult, R=['b6', 'rWW'], W=['Vw_a%d' % hd])
                                CP('dve', Vw[:, hd, 128:129], WWr[:, ti, 4 + hd:5 + hd], R=['rWW'], W=['Vw_a%d' % hd])
                            for hd in range(4):
                                ub = hd % 2
                                MM(B[ub][:, 0:129], lhsT=mkt[:, hd * 128:(hd + 1) * 128], rhs=Vw[:, hd, :],
                                   R=['mkt_a', 'Vw_a%d' % hd], W=[BK[ub]])
                                STT('dve', Crb[:, hd, :], Crb[:, hd, :], DECr[:, ti, 4 + hd:5 + hd], B[ub][:, 0:129],
                                    ALU.mult, ALU.add, R=['Crb', BK[ub], 'rDEC'], W=['Crb'])
                pg.barrier()

        def phase_B(ph, hT, G):
            NT = TLC // 128
            with contextlib.ExitStack() as pb:
                wm = [T(pb, "wm%d" % i, [128, 8, 512], BF16) for i in range(2)]
                mqT = T(pb, "mqT", [128, TLC], BF16)
                mkT = T(pb, "mkT", [128, TLC], BF16)
                mkt = T(pb, "mkt", [128, NT, 128], BF16)
                mvx = T(pb, "mvx", [128, NT, 129], BF16)
                gso = T(pb, "gso", [128, NT, 128], BF16)
                hacc = T(pb, "hacc", [128, NT, 128], F32)
                Cst = T(pb, "Cst", [128, 2, 129], F32)
                Cbf = T(pb, "Cbf", [128, 2, 2, 129], BF16)
                NS = 4
                trilf = [[T(pb, "trilf%d_%d" % (d, i), [128, 128], F32) for i in range(NS)] for d in range(2)]
                E1 = [[T(pb, "E1_%d_%d" % (d, i), [128, 128], F32) for i in range(NS)] for d in range(2)]
                DT = [[T(pb, "DT%d_%d" % (d, i), [128, 128], F32) for i in range(NS)] for d in range(2)]
                ST = [[T(pb, "ST%d_%d" % (d, i), [128, 128], BF16) for i in range(NS)] for d in range(2)]
                QsT = [[T(pb, "QsT%d_%d" % (d, i), [128, 128], BF16) for i in range(NS)] for d in range(2)]
                Vw = [[T(pb, "Vw%d_%d" % (d, i), [128, 129], BF16) for i in range(NS)] for d in range(2)]
                dtmp = [T(pb, "dtmp%d" % i, [128, 2], F32) for i in range(4)]
                sgt = [T(pb, "sgt%d" % i, [128, 128], F32) for i in range(2)]
                LF = T(pb, "LF", [128, NT, 8], F32)
                EU = T(pb, "EU", [128, NT, 8], F32)
                WW = T(pb, "WW", [128, NT, 8], F32)
                DEC = T(pb, "DEC", [128, NT, 8], F32)
                tmpg = T(pb, "tmpg", [128, NT, 8], F32)
                ssq = T(pb, "ssq", [128, NT], F32)
                rsq = T(pb, "rsq", [128, NT], F32)
                junk = T(pb, "junk", [128, 128], F32)
                mlat = [T(pb, "mlat%d" % i, [128, 128], BF16) for i in range(2)]
                ystage = T(pb, "ystage", [128, TLC], BF16)
                mngt = T(pb, "mngt", [128, 512], F32)

                DMA('sp', mngt[:], mng_in, W=['mngt'])
                MEMSET('pool', mvx[:, :, 128:129], 1.0, W=['mvx'])
                derive_gates(G, 'G', NT, LF, EU, WW, DEC, tmpg, 'l')

                def bufsel(d, k):
                    return "%d_%d" % (d, k % NS), k % NS

                def stage1(hd, d, tl, k):
                    gi = d * 4 + hd
                    tri = trif if d == 0 else trib
                    trik = 'trif' if d == 0 else 'trib'
                    S, s_ = bufsel(d, k)
                    pb_ = 0 if d == 0 else 7
                    c0 = (k % 2) * 128
                    ACT(trilf[d][s_][:], tri[:], AF.Copy, scale=LF[:, tl, gi:gi + 1], R=['lLF', trik], W=['trilf' + S])
                    MM(B[pb_][:, c0:c0 + 128], lhsT=onesf[:], rhs=trilf[d][s_][:], R=['onesf', 'trilf' + S], W=[BK[pb_]])

                def stage2(hd, d, tl, k):
                    gi = d * 4 + hd
                    sl = slice(tl * 128, (tl + 1) * 128)
                    S, s_ = bufsel(d, k)
                    pb_ = 0 if d == 0 else 7
                    c0 = (k % 2) * 128
                    ACT(E1[d][s_][:], B[pb_][:, c0:c0 + 128], AF.Exp, R=[BK[pb_]], W=['E1_' + S])
                    ACT(Vw[d][s_][:], mvx[:, tl, :], AF.Copy, scale=WW[:, tl, gi:gi + 1], R=['mvx', 'lWW'], W=['Vw' + S])
                    MM(B[pb_][:, 256 + c0:256 + c0 + 128], lhsT=mkT[:, sl], rhs=mqT[:, sl], R=['mkT', 'mqT'], W=[BK[pb_]])

                def stage3(hd, d, tl, k):
                    gi = d * 4 + hd
                    sl = slice(tl * 128, (tl + 1) * 128)
                    tri = trif if d == 0 else trib
                    trik = 'trif' if d == 0 else 'trib'
                    S, s_ = bufsel(d, k)
                    pb_ = 0 if d == 0 else 7
                    c0 = (k % 2) * 128
                    STT('dve', DT[d][s_][:], E1[d][s_][:], EU[:, tl, gi:gi + 1], tri[:], ALU.mult, ALU.mult,
                        R=['E1_' + S, 'lEU', trik], W=['DT' + S])
                    TT('dve', ST[d][s_][:], B[pb_][:, 256 + c0:256 + c0 + 128], DT[d][s_][:], ALU.mult,
                       R=[BK[pb_], 'DT' + S], W=['ST' + S])
                    TT('pool', QsT[d][s_][:], mqT[:, sl], E1[d][s_][:], ALU.mult, R=['mqT', 'E1_' + S], W=['QsT' + S])
                    ub = 5 + d
                    u0 = (k % 3) * 129
                    MM(B[ub][:, u0:u0 + 129], lhsT=mkt[:, tl, :], rhs=Vw[d][s_][:], R=['mkt', 'Vw' + S], W=[BK[ub]])

                def mchain(hd, d, tl, k):
                    gi = d * 4 + hd
                    S, s_ = bufsel(d, k)
                    D_ = str(d)
                    nb = 1 + 2 * d + (k % 2)
                    cin = 'Cbf%d_%d' % (d, k % 2)
                    cout = 'Cbf%d_%d' % (d, (k + 1) % 2)
                    MM(B[nb][:, 0:129], lhsT=ST[d][s_][:], rhs=mvx[:, tl, :], start=True, stop=False,
                       R=['ST' + S, 'mvx'], W=[BK[nb]])
                    MM(B[nb][:, 0:129], lhsT=QsT[d][s_][:], rhs=Cbf[:, d, k % 2, :], start=False, stop=True,
                       R=['QsT' + S, cin], W=[BK[nb]])
                    ub = 5 + d
                    u0 = (k % 3) * 129
                    STT('dve', Cst[:, d, :], Cst[:, d, :], DEC[:, tl, gi:gi + 1], B[ub][:, u0:u0 + 129],
                        ALU.mult, ALU.add, R=['Cst' + D_, BK[ub], 'lDEC'], W=['Cst' + D_])
                    CP('act', Cbf[:, d, (k + 1) % 2, :], Cst[:, d, :], R=['Cst' + D_], W=[cout])

                def mevac(hd, d, tl, k, first_write):
                    nb = 1 + 2 * d + (k % 2)
                    dk_ = 'dtmp%d_%d' % (d, k % 2)
                    dt_ = dtmp[d * 2 + (k % 2)]
                    ACT(dt_[:, 0:1], B[nb][:, 128:129], AF.Abs, R=[BK[nb]], W=[dk_])
                    TS('dve', dt_[:, 0:1], dt_[:, 0:1], 1.0, ALU.max, R=[dk_], W=[dk_])
                    RECIP(dt_[:, 1:2], dt_[:, 0:1], R=[dk_], W=[dk_])
                    hk = 'hacc%d' % tl
                    if first_write:
                        TS('dve', hacc[:, tl, :], B[nb][:, 0:128], dt_[:, 1:2], ALU.mult, R=[BK[nb], dk_], W=[hk])
                    else:
                        STT('dve', hacc[:, tl, :], B[nb][:, 0:128], dt_[:, 1:2], hacc[:, tl, :], ALU.mult, ALU.add,
                            R=[BK[nb], dk_, hk], W=[hk])

                for hd in range(4):
                    w = wm[hd % 2]
                    wk = 'wm%d' % (hd % 2)
                    DMA('pool', w[:], wview(wm_in[hd]), W=[wk])
                    for g in range(TLC // 512):
                        gs = slice(g * 512, (g + 1) * 512)
                        for which, dst, dk_, scl in ((0, mqT, 'mqT', 128.0 ** -0.5), (1, mkT, 'mkT', 1.0)):
                            fb = (6, 0)[which]
                            for c in range(8):
                                MM(B[fb][:, :512], lhsT=w[:, c, which * 128:(which + 1) * 128], rhs=hT[:, c, gs],
                                   start=(c == 0), stop=(c == 7), R=[wk, 'hT'], W=[BK[fb]])
                            if which == 0:
                                ACT(dst[:, gs], B[fb][:, :512], AF.Copy, scale=scl, R=[BK[fb]], W=[dk_])
                            else:
                                CP('dve', dst[:, gs], B[fb][:, :512], R=[BK[fb]], W=[dk_])
                        for ti in range(4):
                            tl = g * 4 + ti
                            tb_ = 1 + (tl % 4)
                            for c in range(8):
                                MM(B[tb_][:, 0:384], lhsT=hT[:, c, tl * 128:(tl + 1) * 128], rhs=w[:, c, 128:512],
                                   start=(c == 0), stop=(c == 7), R=[wk, 'hT'], W=[BK[tb_]])
                            CP('dve', mkt[:, tl, :], B[tb_][:, 0:128], R=[BK[tb_]], W=['mkt'])
                            CP('act', mvx[:, tl, 0:128], B[tb_][:, 128:256], R=[BK[tb_]], W=['mvx'])
                            ACT(sgt[tl % 2][:], B[tb_][:, 256:384], AF.Sigmoid, R=[BK[tb_]], W=['sgt%d' % (tl % 2)])
                            TT('pool', gso[:, tl, :], sgt[tl % 2][:], mngt[:, hd * 128:(hd + 1) * 128], ALU.mult,
                               R=['sgt%d' % (tl % 2), 'mngt'], W=['gso'])
                    MEMSET('pool', Cst[:], 0.0, W=['Cst0', 'Cst1'])
                    MEMSET('pool', Cbf[:], 0.0, W=['Cbf0_0', 'Cbf0_1', 'Cbf1_0', 'Cbf1_1'])
                    fseq = [34, 35] + list(range(34))
                    bseq = [35, 34] + list(reversed(range(34)))
                    seqs = (fseq, bseq)
                    written = set()
                    for t in range(-3, NT + 1):
                        if 0 <= t < NT:
                            j = t
                            for d in range(2):
                                mchain(hd, d, seqs[d][j], j)
                        if 0 <= t - 1 < NT:
                            for d in range(2):
                                tl = seqs[d][t - 1]
                                mevac(hd, d, tl, t - 1, tl not in written)
                                written.add(tl)
                        if 0 <= t < NT:
                            j = t
                            if j == 1:
                                CP('dve', Cst[:, 1, :], Crb[:, hd, :], R=['Crb'], W=['Cst1'])
                                CP('act', Cbf[:, 1, 0, :], Crb[:, hd, :], R=['Crb'], W=['Cbf1_0'])
                        for stg, off in ((stage3, 1), (stage2, 2), (stage1, 3)):
                            k = t + off
                            if 0 <= k < NT:
                                for d in range(2):
                                    stg(hd, d, seqs[d][k], k)
                    for tl in range(NT):
                        ACT(junk[:], hacc[:, tl, :], AF.Square, accum=ssq[:, tl:tl + 1], R=['hacc%d' % tl],
                            W=['junk', 'ssq'])
                    ACT(rsq[:], ssq[:], AF.Sqrt, bias=epsc[:, 0:1], scale=1.0 / 128.0, R=['ssq', 'epsc'], W=['rsq'])
                    RECIP(rsq[:], rsq[:], R=['rsq'], W=['rsq'])
                    for tl in range(NT):
                        s2 = tl % 2
                        STT('dve', mlat[s2][:], hacc[:, tl, :], rsq[:, tl:tl + 1], gso[:, tl, :], ALU.mult, ALU.mult,
                            R=['hacc%d' % tl, 'rsq', 'gso'], W=['mlat%d' % s2])
                        q4 = tl % 4
                        MM(B[7][:, q4 * 128:(q4 + 1) * 128], lhsT=mlat[s2][:], rhs=identb[:], R=['mlat%d' % s2, 'identb'],
                           W=['b7'])
                        if q4 == 3:
                            CP('act', ystage[:, (tl - 3) * 128:(tl + 1) * 128], B[7][:, 0:512], R=['b7'], W=['ystage'])
                    DMA('sp', ys[hd], ystage[:], R=['ystage'], W=['ys'])
                pg.barrier()

        def phase_C():
            NKT = NK // 128
            with contextlib.ExitStack() as pc:
                cqnT = T(pc, "cqnT", [128, 2, TLC], BF16)
                ckvnT = T(pc, "ckvnT", [128, NK], BF16)
                KT = T(pc, "KT", [128, NK], BF16)
                Vext = T(pc, "Vext", [128, NKT, 128], BF16)
                QT = [T(pc, "QT%d" % i, [128, 512], BF16) for i in range(2)]
                PT = [T(pc, "PT%d" % i, [128, 512], BF16) for i in range(3)]
                ystage = T(pc, "ystageC", [128, TLC], BF16)
                rc = T(pc, "rcC", [128, TLC], F32)
                rs = T(pc, "rsC", [128, TLC], F32)
                tA = T(pc, "tAC", [128, 512], F32)
                tB = T(pc, "tBC", [128, 512], F32)
                rl = T(pc, "rl", [128, 512], F32)
                wuq = T(pc, "wuq", [128, 2, 768], BF16)
                wuqB = T(pc, "wuqB", [128, 2, 768], BF16)
                wukv = T(pc, "wukv", [128, 1024], BF16)

                DMA('sp', cqnT[:], cqn_s.rearrange("c p t -> p c t"), W=['cqnT'])
                DMA('sp', ckvnT[:], ckvn_s, W=['ckvnT'])
                MEMSET('pool', KT[64:128, :], 0.0, W=['KTr', 'KTz'])
                DMA('sp', KT[64:96, :], kr_s, W=['KTr'])
                for i_ in range(2):
                    MEMSET('pool', QT[i_][64:128, :], 0.0, W=['QT%d' % i_])
                DMA('pool', wuq[:], wview(wuq_in), W=['wuq'])
                DMA('pool', wuqB[:], wview(wuqB_in), W=['wuqB'])
                DMA('pool', wukv[:], wukv_in, W=['wukv'])
                DMA('sp', rc[64:96, 0:TL], ropeC_in[:, 0:TL], W=['rcC'])
                DMA('sp', rc[64:96, TL:TLC], ropeC_in[:, SEQ:NK], W=['rcC'])
                DMA('sp', rs[64:96, 0:TL], ropeS_in[:, 0:TL], W=['rsC'])
                DMA('sp', rs[64:96, TL:TLC], ropeS_in[:, SEQ:NK], W=['rsC'])

                qgroups = [(g * 512, 512, list(range(NKT))) for g in range(8)]
                qgroups.append((4096, 256, list(range(NKT))))
                qgroups.append((TL, 256, [NKT - 2, NKT - 1]))
                a_scale = 96.0 ** -0.5
                cnt = 0
                for h in range(8):
                    voff = 0 if h % 2 == 0 else 64
                    ooff = 64 - voff
                    MEMSET('pool', Vext[:, :, ooff:ooff + 64], 1.0, W=['Vext'])
                    for kg in range((NK + 511) // 512):
                        k0 = kg * 512
                        n = min(512, NK - k0)
                        MM(B[0][0:64, :n], lhsT=wukv[:, h * 128:h * 128 + 64], rhs=ckvnT[:, k0:k0 + n],
                           R=['wukv', 'ckvnT'], W=['b0'])
                        CP('dve', KT[0:64, k0:k0 + n], B[0][0:64, :n], R=['b0'], W=['KTn'])
                        ntile = n // 128
                        for i in range(ntile):
                            kt = kg * 4 + i
                            MM(B[1][:, i * 64:(i + 1) * 64], lhsT=ckvnT[:, kt * 128:(kt + 1) * 128],
                               rhs=wukv[:, h * 128 + 64:h * 128 + 128], R=['wukv', 'ckvnT'], W=['b1'])
                        CP('dve', Vext[:, kg * 4:kg * 4 + ntile, voff:voff + 64],
                           B[1][:, 0:ntile * 64].rearrange("p (t e) -> p t e", e=64), R=['b1'], W=['Vext'])
                    items = []
                    for gidx, (q0, n, ktiles) in enumerate(qgroups):
                        for i, kt in enumerate(ktiles):
                            items.append((gidx, q0, n, i, kt, len(ktiles)))

                    def qprep(gidx, h=h):
                        q0, n, _ = qgroups[gidx]
                        qt = QT[gidx % 2]
                        qk = 'QT%d' % (gidx % 2)
                        for r in range(2):
                            MM(B[1][0:96, :n], lhsT=wuq[:, r, h * 96:(h + 1) * 96], rhs=cqnT[:, r, q0:q0 + n],
                               start=(r == 0), stop=(r == 1), R=['wuq', 'cqnT'], W=['b1'])
                        for r in range(2):
                            MM(B[2][0:96, :n], lhsT=wuqB[:, r, h * 96:(h + 1) * 96], rhs=cqnT[:, r, q0:q0 + n],
                               start=(r == 0), stop=(r == 1), R=['wuqB', 'cqnT'], W=['b2'])
                        CP('dve', qt[0:64, :n], B[1][0:64, :n], R=['b1'], W=[qk])
                        TT('dve', tA[64:96, :n], B[1][64:96, :n], rc[64:96, q0:q0 + n], ALU.mult, R=['b1', 'rcC'], W=['tAC'])
                        TT('dve', tB[64:96, :n], B[2][64:96, :n], rs[64:96, q0:q0 + n], ALU.mult, R=['b2', 'rsC'], W=['tBC'])
                        TT('pool', qt[64:96, :n], tA[64:96, :n], tB[64:96, :n], ALU.add, R=['tAC', 'tBC'], W=[qk])

                    def s_item(t):
                        gidx, q0, n, i, kt, nk_ = items[t]
                        qt = QT[gidx % 2]
                        qk = 'QT%d' % (gidx % 2)
                        sb = 3 + (t % 3)
                        pt = PT[t % 3]
                        pk = 'PT%d' % (t % 3)
                        MM(B[sb][:, :n], lhsT=KT[:, kt * 128:(kt + 1) * 128], rhs=qt[:, :n],
                           R=['KTn', 'KTr', 'KTz', qk], W=[BK[sb]])
                        ACT(pt[:, :n], B[sb][:, :n], AF.Exp, scale=a_scale, R=[BK[sb]], W=[pk])

                    def pv_item(t, voff=voff, ooff=ooff):
                        gidx, q0, n, i, kt, nk_ = items[t]
                        pt = PT[t % 3]
                        pk = 'PT%d' % (t % 3)
                        ob = 6 if gidx % 2 == 0 else 0
                        MM(B[ob][:, :n], lhsT=Vext[:, kt, :], rhs=pt[:, :n], start=(i == 0), stop=(i == nk_ - 1),
                           R=['Vext', pk], W=[BK[ob]])
                        if i == nk_ - 1:
                            CP('dve', rl[voff:voff + 64, :n], B[ob][ooff:ooff + 64, :n], R=[BK[ob]], W=['rl'])
                            RECIP(rl[voff:voff + 64, :n], rl[voff:voff + 64, :n], R=['rl'], W=['rl'])
                            TT('dve', ystage[voff:voff + 64, q0:q0 + n], B[ob][voff:voff + 64, :n],
                               rl[voff:voff + 64, :n], ALU.mult, R=[BK[ob], 'rl'], W=['ystageC'])

                    LA = 2
                    qprep(0)
                    for t in range(len(items) + LA):
                        if t < len(items):
                            s_item(t)
                            if items[t][3] == 0 and items[t][0] + 1 < len(qgroups):
                                qprep(items[t][0] + 1)
                        if t - LA >= 0:
                            pv_item(t - LA)
                    if h % 2 == 1:
                        DMA('sp', ys[4 + h // 2], ystage[:], R=['ystageC'], W=['ys'])
                pg.barrier()

        def phase_D(l):
            ngroups = (TLC // 256) if l == 0 else (NOWN // 256)
            with contextlib.ExitStack() as pd:
                wout = T(pd, "wout", [128, 8, 1024], BF16)
                W1 = T(pd, "W1", [128, 8, 4096], BF16)
                W2 = T(pd, "W2", [128, 32, 1024], BF16)
                xgs = [T(pd, "xgD%d" % i, [128, 8, 256], F32) for i in range(2)]
                ygs = [T(pd, "ygD%d" % i, [128, 8, 256], BF16) for i in range(2)]
                h2s = [T(pd, "h2_%d" % i, [128, 8, 256], BF16) for i in range(2)]
                hid = T(pd, "hid", [128, 32, 256], BF16)
                sq = T(pd, "sqD", [128, 8, 256], BF16)
                rstd = T(pd, "rstdD", [128, 256], F32)
                tmpf = [T(pd, "tmpfD%d" % i, [128, 256], F32) for i in range(2)]
                rt = [T(pd, "rt%d" % i, [128, 256], F32) for i in range(2)]
                DMA('pool', wout[:], wview(wout0_in if l == 0 else wout1_in), W=['wout'])
                for q in range(4):
                    DMA('pool', W1[:, :, q * 1024:(q + 1) * 1024], wview(w1_in[l][:, q * 1024:(q + 1) * 1024]),
                        W=['W1_%d' % q])
                for q in range(4):
                    DMA('pool', W2[:, q * 8:(q + 1) * 8, :], wview(w2_in[l][q * 1024:(q + 1) * 1024, :]), W=['W2_%d' % q])
                n = 256
                mk = 'mod%d' % l

                def load_group(gi):
                    t0 = gi * 256
                    jm = 1 if t0 >= TL else 0
                    if l == 0:
                        src = ctxT_in if jm else xT_in[:, :, t0:t0 + n]
                    else:
                        src = xs[:, :, t0:t0 + n]
                    DMA('sp', xgs[gi % 2][:], src.rearrange("c p t -> p c t"), R=['xs'] if l == 1 else [],
                        W=['xgD%d' % (gi % 2)])
                    DMA('sp', ygs[gi % 2][:], ys[:, :, t0:t0 + n].rearrange("c p t -> p c t"), R=['ys'],
                        W=['ygD%d' % (gi % 2)])

                def front(gi):
                    t0 = gi * 256
                    jm = 1 if t0 >= TL else 0
                    xg, yg = xgs[gi % 2], ygs[gi % 2]
                    XK, YK = 'xgD%d' % (gi % 2), 'ygD%d' % (gi % 2)
                    h2_ = h2s[gi % 2]
                    HK = 'h2_%d' % (gi % 2)
                    for dc in range(8):
                        ob = 3 + dc % 4
                        for c in range(8):
                            MM(B[ob][:, :n], lhsT=wout[:, c, dc * 128:(dc + 1) * 128], rhs=yg[:, c, :],
                               start=(c == 0), stop=(c == 7), R=['wout', YK], W=[BK[ob]])
                        STT('dve', xg[:, dc, :], B[ob][:, :n], mod_ap(l, jm, 2, dc), xg[:, dc, :], ALU.mult, ALU.add,
                            R=[BK[ob], mk, XK], W=[XK])
                    rmsnorm_fm((sq, rstd, tmpf), [xg[:, c, :] for c in range(8)], [XK] * 8, n, D,
                               [A2[l][:, jm * 8 + c:jm * 8 + c + 1] for c in range(8)],
                               [mod_ap(l, jm, 3, c) for c in range(8)], [h2_[:, c, :] for c in range(8)], [HK] * 8,
                               scale_keys=['A2_%d' % l, mk])

                def w1_stage(gi):
                    h2_ = h2s[gi % 2]
                    HK = 'h2_%d' % (gi % 2)
                    for fc in range(32):
                        hb = 3 + fc % 4
                        for c in range(8):
                            MM(B[hb][:, :n], lhsT=W1[:, c, fc * 128:(fc + 1) * 128], rhs=h2_[:, c, :],
                               start=(c == 0), stop=(c == 7), R=['W1_%d' % (fc // 8), HK], W=[BK[hb]])
                        r_ = rt[fc % 2]
                        rk = 'rt%d' % (fc % 2)
                        ACT(r_[:], B[hb][:, :n], AF.Relu, R=[BK[hb]], W=[rk])
                        TT('pool', hid[:, fc, :], r_[:], r_[:], ALU.mult, R=[rk], W=['hid'])

                def w2_stage(gi):
                    t0 = gi * 256
                    jm = 1 if t0 >= TL else 0
                    xg = xgs[gi % 2]
                    XK = 'xgD%d' % (gi % 2)
                    for dc in range(8):
                        ob = 1 + dc % 2
                        for fc in range(32):
                            MM(B[ob][:, :n], lhsT=W2[:, fc, dc * 128:(dc + 1) * 128], rhs=hid[:, fc, :],
                               start=(fc == 0), stop=(fc == 31), R=['W2_%d' % (fc // 8), 'hid'], W=[BK[ob]])
                        STT('dve', xg[:, dc, :], B[ob][:, :n], mod_ap(l, jm, 5, dc), xg[:, dc, :], ALU.mult, ALU.add,
                            R=[BK[ob], mk, XK], W=[XK])
                    if l == 0:
                        DMA('sp', xs[:, :, t0:t0 + n].rearrange("c p t -> p c t"), xg[:], R=[XK], W=['xs'])
                    else:
                        rmsnorm_fm((sq, rstd, tmpf), [xg[:, c, :] for c in range(8)], [XK] * 8, n, D,
                                   [fng[:, c:c + 1] for c in range(8)], None, [xg[:, c, :] for c in range(8)],
                                   [XK] * 8, scale_keys=['fng'])
                        DMA('sp', out_ap[:, :, t0:t0 + n].rearrange("c p t -> p c t"), xg[:], R=[XK], W=['out'])

                load_group(0)
                if ngroups > 1:
                    load_group(1)
                front(0)
                for gi in range(ngroups):
                    w1_stage(gi)
                    if gi + 1 < ngroups:
                        front(gi + 1)
                    w2_stage(gi)
                    if gi + 2 < ngroups:
                        load_group(gi + 2)
                pg.barrier()

        def phase_N():
            NT = TLC // 128
            with contextlib.ExitStack() as pn:
                hT = T(pn, "hT1", [128, 8, TLC], BF16)
                with contextlib.ExitStack() as pa:
                    xg = [T(pa, "xgN%d" % i, [128, 8, 512], F32) for i in range(2)]
                    sq = T(pa, "sqN", [128, 8, 512], BF16)
                    rstd = T(pa, "rstdN", [128, 512], F32)
                    tmpf = [T(pa, "tmpfN%d" % i, [128, 512], F32) for i in range(2)]
                    for gi in range(TLC // 512):
                        x = xg[gi % 2]
                        xk = 'xgN%d' % (gi % 2)
                        DMA('sp', x[:], xs[:, :, gi * 512:(gi + 1) * 512].rearrange("c p t -> p c t"), R=['xs'], W=[xk])
                        parts = [(0, 512, 0)] if gi < 8 else [(0, 256, 0), (256, 512, 1)]
                        for (a, b_, jm) in parts:
                            n = b_ - a
                            rmsnorm_fm((sq, rstd, tmpf), [x[:, c, a:b_] for c in range(8)], [xk] * 8, n, D,
                                       [A1[1][:, jm * 8 + c:jm * 8 + c + 1] for c in range(8)],
                                       [mod_ap(1, jm, 0, c) for c in range(8)],
                                       [hT[:, c, gi * 512 + a:gi * 512 + b_] for c in range(8)], ['hT1'] * 8,
                                       scale_keys=['A1_1', 'mod1'])
                    pg.barrier()
                with contextlib.ExitStack() as pb:
                    wn = [T(pb, "wn%d" % i, [128, 8, 384], BF16) for i in range(2)]
                    qz = [T(pb, "qz%d" % i, [128, TLC], BF16) for i in range(2)]
                    MEMSET('pool', qz[0][64:128, :], 0.0, W=['qz0'])
                    MEMSET('pool', qz[1][0:64, :], 0.0, W=['qz1'])
                    kT = T(pb, "kT", [128, TLC], BF16)
                    Vx = T(pb, "Vx", [128, NT, 2, 128], BF16)
                    tab = [T(pb, "tab%d" % i, [128, 2, 6 * 256], F32) for i in range(2)]
                    NSB = 6
                    SBANKS = [0, 1, 2, 3, 4, 7]
                    PT = [T(pb, "PTn%d" % i, [128, 256], BF16) for i in range(NSB)]
                    sb_ = [T(pb, "sbias%d" % i, [128, 256], F32) for i in range(NSB)]
                    rls = [T(pb, "rlN%d" % i, [128, 256], F32) for i in range(2)]
                    ystage = T(pb, "ystageN", [128, NOWN], BF16)
                    scale = 64.0 ** -0.5
                    MEMSET('pool', Vx[:, :, 0, 64:128], 1.0, W=['Vx'])
                    MEMSET('pool', Vx[:, :, 1, 0:64], 1.0, W=['Vx'])
                    tcnt = 0
                    for cp in range(8):
                        w = wn[cp % 2]
                        wk = 'wn%d' % (cp % 2)
                        DMA('pool', w[:], wview(wn_in[cp]), W=[wk])
                        for g in range(TLC // 512):
                            gs = slice(g * 512, (g + 1) * 512)
                            for which in (0, 1):
                                fb = (0, 1)[which]
                                for c in range(8):
                                    MM(B[fb][:, :512], lhsT=w[:, c, which * 128:(which + 1) * 128], rhs=hT[:, c, gs],
                                       start=(c == 0), stop=(c == 7), R=[wk, 'hT1'], W=[BK[fb]])
                                if which == 0:
                                    CP('dve', qz[0][0:64, gs], B[fb][0:64, :512], R=[BK[fb]], W=['qz0'])
                                    CP('dve', qz[1][64:128, gs], B[fb][64:128, :512], R=[BK[fb]], W=['qz1'])
                                else:
                                    CP('act', kT[:, gs], B[fb][:, :512], R=[BK[fb]], W=['kT'])
                            vb = (7, 2)[g % 2]
                            for ti in range(4):
                                tl = g * 4 + ti
                                for c in range(8):
                                    MM(B[vb][:, ti * 128:(ti + 1) * 128], lhsT=hT[:, c, tl * 128:(tl + 1) * 128],
                                       rhs=w[:, c, 256:384], start=(c == 0), stop=(c == 7), R=[wk, 'hT1'], W=[BK[vb]])
                            v4 = B[vb][:, :].rearrange("p (t e) -> p t e", e=128)
                            CP('dve', Vx[:, g * 4:(g + 1) * 4, 0, 0:64], v4[:, :, 0:64], R=[BK[vb]], W=['Vx'])
                            CP('act', Vx[:, g * 4:(g + 1) * 4, 1, 64:128], v4[:, :, 64:128], R=[BK[vb]], W=['Vx'])
                        for hh in range(2):
                            h = cp * 2 + hh
                            tb = tab[tcnt % 2]
                            tk = 'tab%d' % (tcnt % 2)
                            tcnt += 1
                            DMA('sp', tb[:], natab_in[h].rearrange("v p n -> p v n"), W=[tk])
                            voff = 0 if hh == 0 else 64
                            ooff = 64 - voff
                            hr = slice(hh * 64, (hh + 1) * 64)
                            items = []
                            for g in range(NOWN // 256):
                                kt0 = max(2 * g - 2, 0)
                                ktl = [(kt0 + i, i) for i in range(6)] + [(NT - 2, None), (NT - 1, None)]
                                for i, (kt, wi) in enumerate(ktl):
                                    items.append((g, i, kt, wi, len(ktl)))

                            def s_item(t, hh_=hh, tb=tb, tk=tk):
                                g, i, kt, wi, nk_ = items[t]
                                q0 = g * 256
                                var = 0 if g == 0 else 1
                                sbk = SBANKS[t % NSB]
                                pt = PT[t % NSB]
                                pk = 'PTn%d' % (t % NSB)
                                MM(B[sbk][:, :256], lhsT=kT[:, kt * 128:(kt + 1) * 128], rhs=qz[hh_][:, q0:q0 + 256],
                                   start=True, stop=True, R=['kT', 'qz%d' % hh_], W=[BK[sbk]])
                                if wi is not None:
                                    sbt = sb_[t % NSB]
                                    sk = 'sbias%d' % (t % NSB)
                                    STT('dve', sbt[:], B[sbk][:, :256], scale, tb[:, var, wi * 256:(wi + 1) * 256],
                                        ALU.mult, ALU.add, R=[BK[sbk], tk], W=[sk])
                                    ACT(pt[:], sbt[:], AF.Exp, R=[sk], W=[pk])
                                else:
                                    ACT(pt[:], B[sbk][:, :256], AF.Exp, scale=scale, R=[BK[sbk]], W=[pk])

                            def pv_item(t, hh=hh, voff=voff, ooff=ooff):
                                g, i, kt, wi, nk_ = items[t]
                                q0 = g * 256
                                pt = PT[t % NSB]
                                pk = 'PTn%d' % (t % NSB)
                                ob = 5 + g % 2
                                MM(B[ob][:, :256], lhsT=Vx[:, kt, hh, :], rhs=pt[:], start=(i == 0),
                                   stop=(i == nk_ - 1), R=['Vx', pk], W=[BK[ob]])
                                if i == nk_ - 1:
                                    rl_ = rls[g % 2]
                                    rk = 'rlN%d' % (g % 2)

                                    def f1(rl_=rl_, rk=rk, ob=ob):
                                        CP('dve', rl_[voff:voff + 64, :], B[ob][ooff:ooff + 64, :256], R=[BK[ob]], W=[rk])

                                    def f2(rl_=rl_, rk=rk):
                                        ACT(rl_[voff:voff + 64, :], rl_[voff:voff + 64, :], AF.Ln, R=[rk], W=[rk])
                                        ACT(rl_[voff:voff + 64, :], rl_[voff:voff + 64, :], AF.Exp, scale=-1.0, R=[rk], W=[rk])

                                    def f3(rl_=rl_, rk=rk, ob=ob, q0=q0):
                                        TT('dve', ystage[voff:voff + 64, q0:q0 + 256], B[ob][voff:voff + 64, :256],
                                           rl_[voff:voff + 64, :], ALU.mult, R=[BK[ob], rk], W=['ystageN'])

                                    deferred.append((t + 2, f1))
                                    deferred.append((t + 4, f2))
                                    deferred.append((t + 7, f3))

                            deferred = []
                            LA = NSB - 1
                            for t in range(len(items) + LA + 8):
                                if t < len(items):
                                    s_item(t)
                                if 0 <= t - LA < len(items):
                                    pv_item(t - LA)
                                while deferred and deferred[0][0] <= t - LA:
                                    deferred.pop(0)[1]()
                            assert not deferred
                        DMA('sp', ys[cp][:, 0:NOWN], ystage[:], R=['ystageN'], W=['ys'])
                    pg.barrier()

        def run_all():
            phase_mod(0)
            if stop_after == 'mod0':
                return
            with contextlib.ExitStack() as l0:
                hT = T(l0, "hT", [128, 8, TLC], BF16)
                G = T(l0, "G", [128, TLC // 128, 16], F32)
                phase_A(l0, hT, G)
                if stop_after == 'A':
                    return
                phase_B(l0, hT, G)
            if stop_after == 'B':
                return
            phase_C()
            if stop_after == 'C':
                return
            phase_D(0)
            if stop_after == 'D0':
                return
            phase_mod(1)
            phase_N()
            if stop_after == 'N':
                return
            phase_D(1)

        run_all()
        if dbg:
            with contextlib.ExitStack() as pdg:
                buf = T(pdg, "dbgbuf", [128, 8, 512], F32)
                bufb = T(pdg, "dbgbufb", [128, 8, 512], BF16)
                src, isbf = dbg
                if src == 'mod':
                    DMA('sp', dbg_ap[0, :, 0:96], mod[0][:], W=['dbg'])
                    DMA('sp', dbg_ap[1, :, 0:16], A1[0][:], W=['dbg'])
                    DMA('sp', dbg_ap[2, :, 0:4 * 129], Crb[:].rearrange("p h e -> p (h e)"), W=['dbg'])
                srcap = {'xs': xs, 'ys': ys, 'mod': xs}[src]
                for gi in range(TLC // 512 if src != 'mod' else 0):
                    sl = slice(gi * 512, (gi + 1) * 512)
                    if isbf:
                        DMA('sp', bufb[:], srcap[:, :, sl].rearrange("c p t -> p c t"), W=['dbgbufb'])
                        CP('dve', buf[:], bufb[:], R=['dbgbufb'], W=['dbgbuf'])
                    else:
                        DMA('sp', buf[:], srcap[:, :, sl].rearrange("c p t -> p c t"), W=['dbgbuf'])
                    DMA('sp', dbg_ap[:, :, sl].rearrange("c p t -> p c t"), buf[:], R=['dbgbuf'], W=['dbg'])
        pg.barrier()
        pg.emit(block)
        nops = pg.nops
    return nc, nops


def _rope_tables(flip):
    j = np.arange(SEQ)
    p = (SEQ - 1 - j) if flip else j
    row = (p // GRID_W).astype(np.float32)
    col = (p % GRID_W).astype(np.float32)
    n_f = 8
    freqs = (np.float32(10000.0) ** (-np.arange(n_f, dtype=np.float32) / np.float32(n_f))).astype(np.float32)
    ang = np.concatenate([row[:, None] * freqs, col[:, None] * freqs], axis=-1).astype(np.float32)
    cos = np.cos(ang).astype(np.float32).T
    sin = np.sin(ang).astype(np.float32).T
    C = np.ones((32, NK), np.float32)
    S = np.zeros((32, NK), np.float32)
    C[0:16, :SEQ] = cos
    C[16:32, :SEQ] = cos
    S[0:16, :SEQ] = -sin
    S[16:32, :SEQ] = sin
    return C, S


def _na_tables(rel_bias, flip):
    H = rel_bias.shape[0]
    rows = SEQ // GRID_W
    out = np.full((H, 2, 128, 6, 256), NEG, np.float32)
    kk = np.arange(128)
    qq = np.arange(256)
    for var in range(2):
        g = 0 if var == 0 else 2
        kt0 = max(2 * g - 2, 0)
        for i in range(6):
            kj = (kt0 + i) * 128 + kk
            qj = g * 256 + qq
            kp = (SEQ - 1 - kj) if flip else kj
            qp = (SEQ - 1 - qj) if flip else qj
            kr, kc = kp // GRID_W, kp % GRID_W
            qr, qc = qp // GRID_W, qp % GRID_W
            rs = np.clip(qr - 4, 0, rows - 8)
            cs = np.clip(qc - 8, 0, GRID_W - 16)
            KR, QR = kr[:, None], qr[None, :]
            KC, QC = kc[:, None], qc[None, :]
            valid = (KR >= rs[None, :]) & (KR < rs[None, :] + 8) & (KC >= cs[None, :]) & (KC < cs[None, :] + 16)
            ri = np.clip(KR - QR + 7, 0, 14)
            ci = np.clip(KC - QC + 15, 0, 30)
            vals = rel_bias[:, ri, ci]
            out[:, var, :, i, :] = np.where(valid[None], vals, np.float32(NEG))
    return out.reshape(H, 2, 128, 6 * 256)


def _prep_inputs(inp):
    f = lambda a: np.ascontiguousarray(np.asarray(a, dtype=np.float32))
    x, c, ctx, c_ctx = f(inp["x"]), f(inp["c"]), f(inp["ctx"]), f(inp["c_ctx"])
    fm = lambda v: np.ascontiguousarray(v.reshape(-1, 128).T)
    shared = {}
    shared["ada_w"] = f(inp["ada_w"])
    shared["ada_b"] = np.stack([fm(f(inp["ada_b"])[l]) for l in range(2)])
    shared["n1g"] = np.stack([fm(f(inp["norm1_g"])[l]) for l in range(2)])
    shared["n2g"] = np.stack([fm(f(inp["norm2_g"])[l]) for l in range(2)])
    shared["fng"] = fm(f(inp["final_norm_g"]))
    shared["mlp_w1"] = f(inp["mlp_w1"])
    shared["mlp_w2"] = f(inp["mlp_w2"])
    w_in = f(inp["ab_w_in"])[0]
    mq, mk, mv, mo = w_in[:, 0:512], w_in[:, 512:1024], w_in[:, 1024:1536], w_in[:, 1536:2048]
    mg = w_in[:, 2048:2064]
    cq, ckv, kr = w_in[:, 2064:2320], w_in[:, 2320:2448], w_in[:, 2448:2480]
    shared["wm"] = np.ascontiguousarray(np.stack(
        [np.concatenate([a[:, h * 128:(h + 1) * 128] for a in (mq, mk, mv, mo)], axis=1) for h in range(4)]))
    perm = np.concatenate([np.arange(16, 32), np.arange(0, 16)])
    shared["wAfm"] = np.ascontiguousarray(np.concatenate([cq, ckv, kr, kr[:, perm]], axis=1))
    gate_b = f(inp["ab_gate_b"])[0]
    ordn = np.concatenate([np.arange(0, 4), np.arange(8, 12), np.arange(4, 8), np.arange(12, 16)])
    ordf = np.concatenate([np.arange(8, 12), np.arange(0, 4), np.arange(12, 16), np.arange(4, 8)])
    shared["mng"] = np.ascontiguousarray(np.tile(f(inp["ab_m_norm_g"])[0].reshape(1, 512), (128, 1)))
    shared["qng"] = fm(f(inp["ab_q_norm_g"])[0])
    shared["kvng"] = fm(f(inp["ab_kv_norm_g"])[0])
    wuq = f(inp["ab_w_uq"])[0]
    shared["wuq"] = wuq
    wuqB = wuq.copy().reshape(256, 8, 96)
    wuqB[:, :, 64:96] = wuqB[:, :, 64:96][:, :, perm]
    shared["wuqB"] = np.ascontiguousarray(wuqB.reshape(256, 768))
    shared["wukv"] = f(inp["ab_w_ukv"])[0]
    shared["wout0"] = f(inp["ab_w_out"])[0]
    nw = f(inp["na_w_in"])[0]
    shared["wn"] = np.ascontiguousarray(np.stack(
        [np.concatenate([nw[:, o + cp * 128:o + (cp + 1) * 128] for o in (0, 1024, 2048)], axis=1) for cp in range(8)]))
    shared["wout1"] = f(inp["na_w_out"])[0]
    rel_bias = f(inp["na_rel_bias"])[0]
    per_flip = {}
    for flip in (False, True):
        o = ordf if flip else ordn
        C, S = _rope_tables(flip)
        per_flip[flip] = {
            "wAtm": np.ascontiguousarray(np.concatenate([mk, mv, mg[:, o]], axis=1)),
            "gateb": np.ascontiguousarray(np.tile(gate_b[o].reshape(1, 16), (128, 4))),
            "ropeC": C, "ropeS": S,
            "natab": _na_tables(rel_bias, flip),
        }
    in_maps = []
    for core in range(8):
        b, s = core // 2, core % 2
        flip = (s == 1)
        xb = x[b][::-1] if flip else x[b]
        cb = ctx[b][::-1] if flip else ctx[b]
        m = dict(shared)
        m.update(per_flip[flip])
        m["xT"] = np.ascontiguousarray(xb.T).reshape(8, 128, SEQ)
        m["ctxT"] = np.ascontiguousarray(cb.T).reshape(8, 128, CTX)
        cc = np.stack([c[b], c_ctx], axis=0)
        m["scc"] = np.ascontiguousarray(cc.reshape(2, 8, 128).transpose(2, 1, 0).reshape(128, 16))
        in_maps.append(m)
    return in_maps


_CACHE = {}


def kernel(**inputs):
    in_maps = _prep_inputs(inputs)
    if "nc" not in _CACHE:
        _CACHE["nc"] = build_program()[0]
    nc = _CACHE["nc"]
    res = run_bass_kernel_spmd(nc, in_maps, core_ids=list(range(8)))
    out = np.empty((4, SEQ, D), np.float32)
    for core in range(8):
        b, s = core // 2, core % 2
        o = np.asarray(res.results[core]["out"]).reshape(D, NOWN).T
        if s == 0:
            out[b, 0:NOWN] = o
        else:
            out[b, SEQ - NOWN:SEQ] = o[::-1]
    return out
```
